# Optimizing a Trainium2 kernel written in Bass

```python
import jax, jax.numpy as jnp
from jax import lax
import numpy as np

D_MODEL = 1024
BATCH = 16
SEQ = 256
DEPTH = 2
DEC_BATCH = 4
DEC_SEQ = 1024
PAST_LEN = 256

GRID_W = 64
HEAD_DIM = 64
BRANCH_DIM = D_MODEL // 2
A_Q_HEADS = BRANCH_DIM // HEAD_DIM
A_KV_HEADS = A_Q_HEADS // 4
A_GROUP = A_Q_HEADS // A_KV_HEADS
A_WINDOW = 128
A_BLOCK = 128
B_HEADS = BRANCH_DIM // HEAD_DIM
B_WIN_ROWS = 8
B_WIN_COLS = 16
C_GROUPS = 4
C_GROUP_DIM = BRANCH_DIM // C_GROUPS
N_BRANCH = 3
D_FF = -(-8 * D_MODEL // (3 * 256)) * 256
ROPE_BASE = 10000.0
NORM_EPS = 1e-6
NEG_INF = -1e30
IN_SIZES = (A_Q_HEADS * HEAD_DIM, A_KV_HEADS * HEAD_DIM, A_KV_HEADS * HEAD_DIM,
            B_HEADS * HEAD_DIM, B_HEADS * HEAD_DIM, B_HEADS * HEAD_DIM,
            BRANCH_DIM, N_BRANCH * D_MODEL)
IN_SPLITS = tuple(int(s) for s in np.cumsum(IN_SIZES)[:-1])
D_IN = int(sum(IN_SIZES))

kernel_name = "hybrid_flow_prefix_step"


def rms_norm(x, g):
    xf = x.astype(jnp.float32)
    y = xf * lax.rsqrt(jnp.mean(xf * xf, axis=-1, keepdims=True) + NORM_EPS)
    return (y * g.astype(jnp.float32)).astype(x.dtype)


def axial_rope(n_tokens):
    t = jnp.arange(n_tokens, dtype=jnp.int32)
    row = (t // GRID_W).astype(jnp.float32)
    col = (t % GRID_W).astype(jnp.float32)
    n_pairs_axis = HEAD_DIM // 4
    inv = ROPE_BASE ** (-jnp.arange(n_pairs_axis, dtype=jnp.float32) / n_pairs_axis)
    ang = jnp.concatenate([row[:, None] * inv, col[:, None] * inv], axis=-1)
    return jnp.cos(ang), jnp.sin(ang)


def apply_rope(x, cos, sin):
    half = HEAD_DIM // 2
    xf = x.astype(jnp.float32)
    x1, x2 = xf[..., :half], xf[..., half:]
    c = cos[None, :, None, :]
    s = sin[None, :, None, :]
    return jnp.concatenate([x1 * c - x2 * s, x1 * s + x2 * c], axis=-1).astype(x.dtype)


def sink_softmax(s, sink):
    m = jnp.maximum(jnp.max(s, axis=-1, keepdims=True), sink)
    e = jnp.exp(s - m)
    return e / (jnp.sum(e, axis=-1, keepdims=True) + jnp.exp(sink - m))


def combined_projection(h, w_in):
    B, T, _ = h.shape
    qa, ka, va, qb, kb, vb, uc, gates = jnp.split(h @ w_in, IN_SPLITS, axis=-1)
    heads = lambda z, n: z.reshape(B, T, n, HEAD_DIM)
    return (heads(qa, A_Q_HEADS), heads(ka, A_KV_HEADS), heads(va, A_KV_HEADS),
            heads(qb, B_HEADS), heads(kb, B_HEADS), heads(vb, B_HEADS), uc, gates)


def gqa_sink_context(q, k, v, sink):
    B, S = q.shape[:2]
    qg = q.reshape(B, S, A_KV_HEADS, A_GROUP, HEAD_DIM)
    s = jnp.einsum('bqkgd,bskd->bkgqs', qg, k, preferred_element_type=jnp.float32) * HEAD_DIM ** -0.5
    p = sink_softmax(s, sink.astype(jnp.float32).reshape(1, A_KV_HEADS, A_GROUP, 1, 1))
    o = jnp.einsum('bkgqs,bskd->bqkgd', p.astype(v.dtype), v)
    return o.reshape(B, S, A_Q_HEADS * HEAD_DIM)


def window_gqa_latent(q, k, v, ck, cv, sink):
    B, T = q.shape[:2]
    nb = T // A_BLOCK
    qb = q.reshape(B, nb, A_BLOCK, A_KV_HEADS, A_GROUP, HEAD_DIM)
    pad = ((0, 0), (A_BLOCK, A_BLOCK), (0, 0), (0, 0))
    kp = jnp.pad(k, pad)
    vp = jnp.pad(v, pad)
    idx = np.arange(nb)[:, None] * A_BLOCK + np.arange(3 * A_BLOCK)[None, :]
    kb = kp[:, idx]
    vb = vp[:, idx]
    qpos = np.arange(T).reshape(nb, A_BLOCK)
    kpos = idx - A_BLOCK
    valid = ((np.abs(qpos[:, :, None] - kpos[:, None, :]) <= A_WINDOW)
             & (kpos >= 0)[:, None, :] & (kpos < T)[:, None, :])
    scale = HEAD_DIM ** -0.5
    s_band = jnp.einsum('bnqkgd,bnskd->bnkgqs', qb, kb, preferred_element_type=jnp.float32) * scale
    s_band = jnp.where(valid[None, :, None, None], s_band, NEG_INF)
    s_ctx = jnp.einsum('bnqkgd,bpkd->bnkgqp', qb, ck, preferred_element_type=jnp.float32) * scale
    p = sink_softmax(jnp.concatenate([s_band, s_ctx], axis=-1),
                     sink.astype(jnp.float32).reshape(1, 1, A_KV_HEADS, A_GROUP, 1, 1))
    n_band = 3 * A_BLOCK
    o = (jnp.einsum('bnkgqs,bnskd->bnqkgd', p[..., :n_band].astype(v.dtype), vb)
         + jnp.einsum('bnkgqp,bpkd->bnqkgd', p[..., n_band:].astype(cv.dtype), cv))
    return o.reshape(B, T, A_Q_HEADS * HEAD_DIM)


def mha_context(q, k, v):
    B, S = q.shape[:2]
    s = jnp.einsum('bqhd,bshd->bhqs', q, k, preferred_element_type=jnp.float32) * HEAD_DIM ** -0.5
    p = jax.nn.softmax(s, axis=-1)
    o = jnp.einsum('bhqs,bshd->bqhd', p.astype(v.dtype), v)
    return o.reshape(B, S, B_HEADS * HEAD_DIM)


def neighbourhood_latent(q, k, v, ck, cv, rpb):
    B, T, H, D = q.shape
    rows = T // GRID_W
    kr = min(B_WIN_ROWS, rows)
    kc = B_WIN_COLS
    reg = 2 * kc
    ncb = GRID_W // kc
    cols = np.arange(GRID_W)
    win_start = np.clip(cols - kc // 2, 0, GRID_W - kc)
    reg_cols = np.clip(np.arange(ncb) * kc - kc // 2, 0, GRID_W - reg)[:, None] + np.arange(reg)
    q_cols = cols.reshape(ncb, kc)
    ws = win_start[q_cols][:, :, None]
    col_ok = (reg_cols[:, None, :] >= ws) & (reg_cols[:, None, :] < ws + kc)
    dc_idx = np.clip(reg_cols[:, None, :] - q_cols[:, :, None] + B_WIN_COLS - 1, 0, 2 * B_WIN_COLS - 2)
    scale = HEAD_DIM ** -0.5
    qg = q.reshape(B, rows, GRID_W, H, D)
    kg = k.reshape(B, rows, GRID_W, H, D)
    vg = v.reshape(B, rows, GRID_W, H, D)
    n_loc = kr * reg

    def row_block(r):
        r0 = jnp.clip(r - kr // 2, 0, rows - kr)
        k_reg = lax.dynamic_slice_in_dim(kg, r0, kr, axis=1)[:, :, reg_cols]
        v_reg = lax.dynamic_slice_in_dim(vg, r0, kr, axis=1)[:, :, reg_cols]
        q_blk = lax.dynamic_index_in_dim(qg, r, axis=1, keepdims=False).reshape(B, ncb, kc, H, D)
        s_loc = jnp.einsum('bnqhd,bmnshd->bhnqms', q_blk, k_reg, preferred_element_type=jnp.float32) * scale
        row_idx = (r0 + jnp.arange(kr) - r + B_WIN_ROWS - 1)[None, None, :, None]
        bias = rpb[:, row_idx, dc_idx[:, :, None, :]].astype(jnp.float32)
        s_loc = jnp.where(col_ok[None, None, :, :, None, :], s_loc + bias[None], NEG_INF)
        s_loc = s_loc.reshape(B, H, ncb, kc, n_loc)
        s_ctx = jnp.einsum('bnqhd,bphd->bhnqp', q_blk, ck, preferred_element_type=jnp.float32) * scale
        p = jax.nn.softmax(jnp.concatenate([s_loc, s_ctx], axis=-1), axis=-1)
        p_loc = p[..., :n_loc].reshape(B, H, ncb, kc, kr, reg).astype(v.dtype)
        o = (jnp.einsum('bhnqms,bmnshd->bnqhd', p_loc, v_reg)
             + jnp.einsum('bhnqp,bphd->bnqhd', p[..., n_loc:].astype(cv.dtype), cv))
        return o.reshape(B, GRID_W, H * D)

    o = lax.map(row_block, jnp.arange(rows))
    return jnp.moveaxis(o, 0, 1).reshape(B, T, H * D)


def fourier_mix(u):
    B, T, _ = u.shape
    ug = u.reshape(B, T, C_GROUPS, C_GROUP_DIM).astype(jnp.float32)
    f = jnp.fft.fft2(ug, axes=(1, 3), norm='ortho').real
    return f.reshape(B, T, BRANCH_DIM).astype(u.dtype)


def merge_branches(oa, ob, oc, gates, w_branch, w_out):
    ga, gb, gc = jnp.split(gates, N_BRANCH, axis=-1)
    m = (jax.nn.sigmoid(ga) * (oa @ w_branch[0]) + jax.nn.sigmoid(gb) * (ob @ w_branch[1])
         + jax.nn.sigmoid(gc) * (oc @ w_branch[2]))
    return m @ w_out


def swiglu(h, w_ffn_in, w_ffn_out):
    g, u = jnp.split(h @ w_ffn_in, 2, axis=-1)
    return (jax.nn.silu(g) * u) @ w_ffn_out


def trunk_layer(x, mod, g_pre, g_post, w_in_l, w_branch_l, w_out_l, w_ffn_in_l, w_ffn_out_l, attend):
    sh1, sc1, g1, sh2, sc2, g2 = jnp.split(mod, 6, axis=-1)
    h = rms_norm(x, g_pre[0]) * (1 + sc1) + sh1
    qa, ka, va, qb, kb, vb, uc, gates = combined_projection(h, w_in_l)
    oa, ob = attend(qa, ka, va, qb, kb, vb)
    oc = fourier_mix(uc)
    y = merge_branches(oa, ob, oc, gates, w_branch_l, w_out_l)
    x = x + g1 * rms_norm(y, g_post[0])
    h = rms_norm(x, g_pre[1]) * (1 + sc2) + sh2
    x = x + g2 * rms_norm(swiglu(h, w_ffn_in_l, w_ffn_out_l), g_post[1])
    return x, (ka, va, kb, vb)


def setup_inputs(seed: int = 0) -> dict:
    key = jax.random.key(seed)
    ks = jax.random.split(key, 19)
    f32 = jnp.float32
    nrm = lambda k, shape, s: jax.random.normal(k, shape, f32) * s
    return {
        "x_prompt": nrm(ks[0], (BATCH, SEQ, D_MODEL), 1.0),
        "x_sample": nrm(ks[1], (DEC_BATCH, DEC_SEQ, D_MODEL), 1.0),
        "cache_a_k": nrm(ks[2], (DEC_BATCH, DEPTH, PAST_LEN, A_KV_HEADS, HEAD_DIM), 1.0),
        "cache_a_v": nrm(ks[3], (DEC_BATCH, DEPTH, PAST_LEN, A_KV_HEADS, HEAD_DIM), 1.0),
        "cache_b_k": nrm(ks[4], (DEC_BATCH, DEPTH, PAST_LEN, B_HEADS, HEAD_DIM), 1.0),
        "cache_b_v": nrm(ks[5], (DEC_BATCH, DEPTH, PAST_LEN, B_HEADS, HEAD_DIM), 1.0),
        "c": nrm(ks[6], (DEC_BATCH, D_MODEL), 1.0),
        "c_ctx": nrm(ks[7], (D_MODEL,), 1.0),
        "w_ada": nrm(ks[8], (DEPTH, D_MODEL, 6 * D_MODEL), 0.5 * D_MODEL ** -0.5),
        "b_ada": nrm(ks[9], (DEPTH, 6 * D_MODEL), 0.02),
        "norm_pre": 1.0 + nrm(ks[10], (DEPTH, 2, D_MODEL), 0.02),
        "norm_post": 1.0 + nrm(ks[11], (DEPTH, 2, D_MODEL), 0.02),
        "w_in": nrm(ks[12], (DEPTH, D_MODEL, D_IN), D_MODEL ** -0.5),
        "a_sink": nrm(ks[13], (DEPTH, A_Q_HEADS), 1.0),
        "b_rpb": nrm(ks[14], (DEPTH, B_HEADS, 2 * B_WIN_ROWS - 1, 2 * B_WIN_COLS - 1), 0.1),
        "w_branch": nrm(ks[15], (DEPTH, N_BRANCH, BRANCH_DIM, D_MODEL), BRANCH_DIM ** -0.5),
        "w_out": nrm(ks[16], (DEPTH, D_MODEL, D_MODEL), D_MODEL ** -0.5),
        "w_ffn_in": nrm(ks[17], (DEPTH, D_MODEL, 2 * D_FF), D_MODEL ** -0.5),
        "w_ffn_out": nrm(ks[18], (DEPTH, D_FF, D_MODEL), D_FF ** -0.5),
    }


def reference(x_prompt, x_sample, cache_a_k, cache_a_v, cache_b_k, cache_b_v, c, c_ctx,
              w_ada, b_ada, norm_pre, norm_post, w_in, a_sink, b_rpb, w_branch, w_out,
              w_ffn_in, w_ffn_out):
    x = x_prompt
    ak, av, bk, bv = [], [], [], []
    for l in range(DEPTH):
        mod = jax.nn.silu(c_ctx) @ w_ada[l] + b_ada[l]
        x, (ka, va, kb, vb) = trunk_layer(
            x, mod, norm_pre[l], norm_post[l], w_in[l], w_branch[l], w_out[l], w_ffn_in[l], w_ffn_out[l],
            lambda qa, ka, va, qb, kb, vb: (gqa_sink_context(qa, ka, va, a_sink[l]), mha_context(qb, kb, vb)))
        ak.append(ka)
        av.append(va)
        bk.append(kb)
        bv.append(vb)
    y_prompt = x
    new_a_k = jnp.stack(ak, axis=1)
    new_a_v = jnp.stack(av, axis=1)
    new_b_k = jnp.stack(bk, axis=1)
    new_b_v = jnp.stack(bv, axis=1)

    cos, sin = axial_rope(x_sample.shape[1])
    x = x_sample
    for l in range(DEPTH):
        mod = (jax.nn.silu(c) @ w_ada[l] + b_ada[l])[:, None, :]
        x, _ = trunk_layer(
            x, mod, norm_pre[l], norm_post[l], w_in[l], w_branch[l], w_out[l], w_ffn_in[l], w_ffn_out[l],
            lambda qa, ka, va, qb, kb, vb: (
                window_gqa_latent(apply_rope(qa, cos, sin), apply_rope(ka, cos, sin), va,
                                  cache_a_k[:, l], cache_a_v[:, l], a_sink[l]),
                neighbourhood_latent(qb, kb, vb, cache_b_k[:, l], cache_b_v[:, l], b_rpb[l])))
    y_sample = x
    return (y_prompt, y_sample, new_a_k, new_a_v, new_b_k, new_b_v)
```

```python
import numpy as np
from contextlib import ExitStack
import concourse.bass as bass
import concourse.mybir as mybir
from concourse.bass_utils import run_bass_kernel_spmd

F32 = mybir.dt.float32
BF16 = mybir.dt.bfloat16
AF = mybir.ActivationFunctionType
ALU = mybir.AluOpType

NEG = -1e30
T = 1024
NS = 5
SLOT = 4096
WIN_COLS = 7168


class FW:
    def __init__(self, nc):
        self.nc = nc
        self.prog = {e: [] for e in ("pe", "act", "dve", "pool", "sp")}
        self.state = {}
        self.pending = {}
        self.dma_count = {}
        self.nseq = {e: 0 for e in self.prog}

    @staticmethod
    def _name(k):
        return k[0] if isinstance(k, tuple) else k

    def _get(self, k):
        st = self.state.get(k)
        if st is None:
            nm = self._name(k)
            if nm in self.pending:
                st = {"w": None, "r": dict(self.pending[nm])}
                self.state[k] = st
        return st

    def _collect(self, reads, writes):
        deps = {}

        def add(kk, v):
            if kk not in deps or deps[kk] < v:
                deps[kk] = v
        for k in reads:
            st = self._get(k)
            if st and st["w"] is not None:
                add((st["w"][0], st["w"][1]), st["w"][2])
        for k in writes:
            st = self._get(k)
            if st:
                if st["w"] is not None:
                    add((st["w"][0], st["w"][1]), st["w"][2])
                for kk, v in st["r"].items():
                    add(kk, v)
        return deps

    def _update(self, ticket, reads, writes):
        kk = (ticket[0], ticket[1])
        for k in reads:
            st = self.state.setdefault(k, {"w": None, "r": {}})
            if st["r"].get(kk, -1) < ticket[2]:
                st["r"][kk] = ticket[2]
        for k in writes:
            self.state[k] = {"w": ticket, "r": {}}

    def fence(self, old_names, new_names):
        old_names = set(old_names)
        new_names = set(new_names)
        F = {}
        for k, st in self.state.items():
            if self._name(k) in old_names:
                if st["w"] is not None:
                    kk = (st["w"][0], st["w"][1])
                    F[kk] = max(F.get(kk, -1), st["w"][2])
                for kk, v in st["r"].items():
                    F[kk] = max(F.get(kk, -1), v)
        for k, st in self.state.items():
            if self._name(k) in new_names:
                for kk, v in F.items():
                    st["r"][kk] = max(st["r"].get(kk, -1), v)
        for nm in new_names:
            p = self.pending.setdefault(nm, {})
            for kk, v in F.items():
                p[kk] = max(p.get(kk, -1), v)

    def op(self, eng, fn, reads=(), writes=()):
        ps_r = [k for k in reads if self._name(k) == "ps"]
        if ps_r:
            reads = [k for k in reads if self._name(k) != "ps"]
            writes = list(writes) + ps_r
        deps = self._collect(reads, writes)
        seq = self.nseq[eng]
        self.nseq[eng] += 1
        ticket = ("c", eng, seq)
        self.prog[eng].append({"fn": fn, "deps": deps, "ticket": ticket, "dma": None})
        self._update(ticket, reads, writes)

    def dma(self, queue, semkey, fn, reads=(), writes=()):
        deps = self._collect(reads, writes)
        cnt = self.dma_count.get(semkey, 0) + 1
        self.dma_count[semkey] = cnt
        ticket = ("d", semkey, cnt * 16)
        seq = self.nseq[queue]
        self.nseq[queue] += 1
        self.prog[queue].append({"fn": fn, "deps": deps, "ticket": ("c", queue, seq), "dma": semkey})
        self._update(ticket, reads, writes)

    def final_wait(self, queue, keys):
        deps = self._collect(keys, ())
        self.prog[queue].append({"fn": None, "deps": deps, "ticket": None, "dma": None})

    def emit(self, stack):
        nc = self.nc
        signal = {e: set() for e in self.prog}
        for e, lst in self.prog.items():
            for ins in lst:
                for (kind, key), val in ins["deps"].items():
                    if kind == "c":
                        if key == e and e in ("pe", "sp"):
                            continue
                        signal[key].add(val)
        rank = {}
        for e in self.prog:
            rank[e] = {seq: i + 1 for i, seq in enumerate(sorted(signal[e]))}
        sems = {}
        for e in self.prog:
            sems[("c", e)] = stack.enter_context(nc.semaphore("s_" + e))
        for n, k in enumerate(sorted(self.dma_count, key=str)):
            sems[("d", k)] = stack.enter_context(nc.semaphore("d%d" % n))
        import os as _os2
        if _os2.environ.get("KDEBUG"):
            print("SIGNALS", {e: len(rank[e]) for e in rank}, "NINST", {e: len(self.prog[e]) for e in self.prog},
                  "DMA", {str(k): v for k, v in self.dma_count.items()})
        block = stack.enter_context(nc.Block())
        engobj = {"pe": "tensor", "act": "scalar", "dve": "vector", "pool": "gpsimd", "sp": "sync"}

        def run(e, eng):
            waited = {}
            for ins in self.prog[e]:
                for (kind, key), val in sorted(ins["deps"].items(), key=str):
                    if kind == "c":
                        if key == e and e in ("pe", "sp"):
                            continue
                        v = rank[key][val]
                    else:
                        v = val
                    if waited.get((kind, key), 0) >= v:
                        continue
                    waited[(kind, key)] = v
                    eng.wait_ge(sems[(kind, key)], v)
                if ins["fn"] is None:
                    continue
                bi = ins["fn"](eng)
                if ins["dma"] is not None:
                    bi.then_inc(sems[("d", ins["dma"])], 16)
                elif ins["ticket"][2] in rank[e]:
                    bi.then_inc(sems[("c", e)], 1)

        for e in self.prog:
            if self.prog[e]:
                getattr(block, engobj[e])(lambda eng, e=e: run(e, eng))


def build_program():
    nc = bass.Bass("TRN2", target_bir_lowering=False)

    def D(name, shape, kind="ExternalInput"):
        return nc.dram_tensor(name, list(shape), F32, kind=kind).ap()

    x_d = D("x", [T, 1024])
    cvec_d = D("cvec", [128, 8])
    cak_d = D("cak", [2, 256, 128])
    cav_d = D("cav", [2, 256, 128])
    cbk_d = D("cbk", [2, 256, 512])
    cbv_d = D("cbv", [2, 256, 512])
    wada_d = D("w_ada", [2, 1024, 6144])
    bada_d = D("b_adaT", [2, 128, 48])
    npre_d = D("npreT", [128, 32])
    npost_d = D("npostT", [128, 32])
    winx_d = D("w_inx", [2, 1024, WIN_COLS])
    sink_d = D("sinkb", [128, 16])
    ttb_d = D("ttb", [2, 2, 128, 4 * 1152])
    tri_d = D("tri", [128, 1024])
    valid_d = D("valid", [128, 96])
    rope_d = D("rope", [4, 128, T])
    dftc_d = nc.dram_tensor("dft_ct", [T, T], BF16, kind="ExternalInput").ap()
    dfts_d = nc.dram_tensor("dft_nst", [T, T], BF16, kind="ExternalInput").ap()
    cs_d = D("dft_cs", [128, 256])
    wbr_d = D("w_branch", [2, 3, 512, 1024])
    wout_d = D("w_out", [2, 1024, 1024])
    wfi_d = D("w_ffn_in", [2, 1024, 5632])
    wfo_d = D("w_ffn_out", [2, 2816, 1024])
    ident_d = D("ident", [128, 128])
    y_d = D("y", [T, 1024], kind="ExternalOutput")
    kv_d = D("kvout", [2, T, 1280], kind="ExternalOutput")

    st = ExitStack()
    with st:
        NB = 206 * 1024
        arena = st.enter_context(nc.sbuf_tensor("arena", [128, NB // 2], BF16))
        psum_all = st.enter_context(nc.psum_tensor("psall", [128, 8, 512], F32))
        psum = [psum_all[:, i, :] for i in range(8)]
        fw = FW(nc)

        def view(off, shape, dt):
            n = 1
            for s in shape[1:]:
                n *= s
            assert off % 4 == 0
            if dt == BF16:
                ap = arena[:, off // 2: off // 2 + n]
                nbytes = n * 2
            else:
                ap = arena[:, off // 2: off // 2 + 2 * n].bitcast(F32)
                nbytes = n * 4
            assert off + nbytes <= NB, (off, nbytes)
            if len(shape) == 3:
                ap = ap.rearrange("p (a b) -> p a b", a=shape[1])
            elif len(shape) == 4:
                ap = ap.rearrange("p (a b c) -> p a b c", a=shape[1], b=shape[2])
            return ap

        class Alloc:
            def __init__(self, base, limit):
                self.off = base
                self.limit = limit

            def __call__(self, shape, dt):
                n = 1
                for s in shape[1:]:
                    n *= s
                nb = n * (2 if dt == BF16 else 4)
                nb = (nb + 31) // 32 * 32
                v = view(self.off, shape, dt)
                self.off += nb
                assert self.off <= self.limit, (self.off, self.limit)
                return v

        fx = Alloc(0, 106 * 1024)
        xT = fx([128, 8, T], F32)
        wring = [fx([128, SLOT], BF16) for _ in range(NS)]
        tmp = [fx([128, T], F32) for _ in range(2)]
        rstd = fx([128, T], F32)
        identF = fx([128, 128], F32)
        identB = fx([128, 128], BF16)
        onesB = fx([128, 128], BF16)
        validT = fx([128, 96], F32)
        sinkE = fx([128, 16], F32)
        epsT = fx([128, 1], F32)
        cvecT = fx([128, 8], F32)
        sB = fx([128, 8], BF16)
        badaT = fx([128, 2, 48], F32)
        modc = fx([128, 2, 48], F32)
        npreT = fx([128, 32], F32)
        npostT = fx([128, 32], F32)
        dv = fx([128, 2, 6, 8], F32)
        csB = fx([128, 256], BF16)
        stage = [fx([128, 1280], F32) for _ in range(2)]
        xtok = [fx([128, 1024], F32) for _ in range(2)]
        nrm = fx([128, 2, 16], F32)
        ABASE = fx.off
        assert ABASE <= 106 * 1024

        hT = view(ABASE, [128, 8, T], BF16)
        OT0 = ABASE + 16384
        OAT = view(OT0, [128, 4, T], BF16)
        OBT = view(OT0 + 8192, [128, 4, T], BF16)
        OCT = view(OT0 + 16384, [128, 4, T], BF16)
        S0 = OT0 + 24576
        SLIM = NB
        assert SLIM - S0 >= 58 * 1024, (SLIM - S0)
        sq = view(S0, [128, 8, T], BF16)
        mT = view(S0, [128, 8, T], BF16)
        yT = view(S0 + 16384, [128, 8, T], F32)
        c_al = Alloc(S0, SLIM)
        UCT = c_al([128, 4, T], BF16)
        ABtok = c_al([128, 8, 4, 256], BF16)
        a_al = Alloc(S0, SLIM)
        QAT = a_al([128, 4, T], BF16)
        KATp = [a_al([128, T], BF16) for _ in range(2)]
        VA = a_al([128, 8, 2, 80], BF16)
        ropeT = a_al([128, 4, T], F32)
        triB = a_al([128, 2, 512], BF16)
        KcATp = [a_al([128, 256], BF16) for _ in range(2)]
        VcA = a_al([128, 2, 2, 80], BF16)
        PTa = [a_al([128, 5, 512], BF16) for _ in range(2)]
        OtokA = [a_al([128, 512], BF16) for _ in range(2)]
        ctmpA = a_al([128, 2, 128], F32)
        b_al = Alloc(S0, SLIM)
        QBT = b_al([128, 4, T], BF16)
        KBT = b_al([128, 4, T], BF16)
        VB = b_al([128, 8, 8, 80], BF16)
        TTB = b_al([128, 4, 1152], BF16)
        KcBT = b_al([128, 4, 256], BF16)
        VcB = b_al([128, 2, 8, 80], BF16)
        PTb = [b_al([128, 7, 512], BF16) for _ in range(2)]
        OtokB = [b_al([128, 256], BF16) for _ in range(2)]
        ctmpB = b_al([128, 2, 512], F32)
        aT = view(OT0, [128, 22, T], BF16)
        y2T = view(OT0 + 45056, [128, 8, T], F32)
        assert OT0 + 45056 + 32768 <= NB
        sq2 = view(ABASE, [128, 8, T], BF16)

        MIX_C = ["UCT", "ABtok"]
        MIX_A = ["QAT", "KAT", "VA", "rope", "tri", "KcAT", "VcA", "PTa", "OtokA", "ctmpA", "QATall", "VA1", "VcA1", "KATz", "KcATz"]
        MIX_B = ["QBT", "KBT", "VB", "TTB", "KcBT", "VcB", "PTb", "OtokB", "ctmpB", "QKBall", "VB1", "VcB1"]
        SQ = ["mT"]
        WOUT = ["mT", "yT"]
        FFN = ["aT", "y2T"]
        OTS = ["OAT", "OBT", "OCT"]

        cnt = {"bank": 0, "w": 0, "obank": 0}

        def bank():
            b = cnt["bank"] % 6
            cnt["bank"] += 1
            return psum[b], ("ps", b)

        def bank_pair():
            if cnt["bank"] % 2:
                cnt["bank"] += 1
            b = cnt["bank"] % 6
            cnt["bank"] += 2
            return b

        def obank():
            b = 6 + cnt["obank"] % 2
            cnt["obank"] += 1
            return psum[b], ("ps", b)

        def wload(src, shape):
            s = cnt["w"] % NS
            cnt["w"] += 1
            n = 1
            for k in shape[1:]:
                n *= k
            assert n <= SLOT
            dst = wring[s][:, 0:n]
            if len(shape) == 3:
                dst = dst.rearrange("p (a b) -> p a b", a=shape[1])
            else:
                dst = dst.rearrange("p (a b c) -> p a b c", a=shape[1], b=shape[2])
            if len(shape) == 3:
                fw.dma("pool", ("w", s), lambda e, dst=dst, src=src: e.dma_start(out=dst, in_=src),
                       writes=[("wslot", s)])
            else:
                for q in range(shape[2]):
                    fw.dma("pool", ("w", s), lambda e, dst=dst, src=src, q=q: e.dma_start(out=dst[:, :, q, :], in_=src[:, :, q, :]),
                           writes=[("wslot", s)])
            return dst, ("wslot", s)

        def mm(out, lhsT, rhs, start, stop, reads, writes):
            fw.op("pe", lambda e: e.matmul(out, lhsT=lhsT, rhs=rhs, start=start, stop=stop), reads, writes)

        def tr(out, in_, reads, writes):
            fw.op("pe", lambda e: e.transpose(out, in_, identF[:]), list(reads) + ["identF"], writes)

        def act(out, in_, func, reads, writes, bias=None, scale=None):
            kw = {}
            if bias is not None:
                kw["bias"] = bias
            if scale is not None:
                kw["scale"] = scale
            fw.op("act", lambda e: e.activation(out=out, in_=in_, func=func, **kw), reads, writes)

        def dve(fn, reads, writes):
            fw.op("dve", fn, reads, writes)

        def sp_load(key, out, in_, writes):
            fw.dma("sp", key, lambda e: e.dma_start(out=out, in_=in_), writes=writes)

        sp_load("c0", identF, ident_d, ["identF"])
        sp_load("c1", validT, valid_d, ["valid"])
        sp_load("c2", sinkE, sink_d, ["sinkE"])
        sp_load("c3", cvecT, cvec_d, ["cvec"])
        sp_load("c4", badaT, bada_d.rearrange("l p j -> p l j"), ["bada"])
        sp_load("c5", npreT, npre_d, ["npre"])
        sp_load("c6", npostT, npost_d, ["npost"])
        fw.dma("pool", "c7", lambda e: e.dma_start(out=csB, in_=cs_d), writes=["csB"])
        dve(lambda e: e.tensor_copy(out=identB, in_=identF), ["identF"], ["identB"])
        dve(lambda e: e.memset(onesB, 1.0), [], ["onesB"])
        dve(lambda e: e.memset(epsT, 1e-6), [], ["eps"])
        act(sinkE, sinkE, AF.Exp, ["sinkE"], ["sinkE"])
        act(sB, cvecT, AF.Silu, ["cvec"], ["sB"])

        for b in range(8):
            s = b % 2
            sp_load(("xin", s), xtok[s], x_d[b * 128:(b + 1) * 128, :], [("xtok", s)])
            for g in range(2):
                pb, pk = bank()
                for j in range(4):
                    c = g * 4 + j
                    tr(pb[:, j * 128:(j + 1) * 128], xtok[s][:, c * 128:(c + 1) * 128], [("xtok", s)], [pk])
                dve(lambda e, pb=pb, g=g, b=b: e.tensor_copy(out=xT[:, g * 4:(g + 1) * 4, b * 128:(b + 1) * 128],
                                                             in_=pb[:].rearrange("p (j t) -> p j t", j=4)),
                    [pk], [("xT", c) for c in range(g * 4, g * 4 + 4)])

        DVSLOT = {0: 1, 1: 0, 2: 2, 3: 4, 4: 3, 5: 5}

        def ada_seg(l, i):
            wva = wada_d[l].rearrange("(kc p) n -> p kc n", p=128)
            for cb in (2 * i, 2 * i + 1):
                ws, wk = wload(wva[:, :, cb * 512:(cb + 1) * 512], [128, 8, 512])
                pb, pk = bank()
                for jj in range(4):
                    for kc in range(8):
                        mm(pb[:, jj:jj + 1], ws[:, kc, jj * 128:(jj + 1) * 128], sB[:, kc:kc + 1], kc == 0, kc == 7,
                           [wk, "sB"], [pk])
                dve(lambda e, pb=pb, l=l, cb=cb: e.tensor_tensor(out=modc[:, l, cb * 4:(cb + 1) * 4], in0=pb[:, 0:4],
                                                                in1=badaT[:, l, cb * 4:(cb + 1) * 4], op=ALU.add),
                    [pk, "bada"], [("modc", l, cb)])
            mv = modc[:, l, i * 8:(i + 1) * 8]
            slot = DVSLOT[i]
            rk = [("modc", l, 2 * i), ("modc", l, 2 * i + 1), "npre", "npost"]
            wk2 = [("dv", l, slot)]
            if i in (1, 4):
                npv = npreT[:, l * 16 + (0 if i == 1 else 8): l * 16 + (8 if i == 1 else 16)]
                dve(lambda e: e.tensor_scalar_add(out=dv[:, l, slot, :], in0=mv, scalar1=1.0), rk, wk2)
                dve(lambda e: e.tensor_tensor(out=dv[:, l, slot, :], in0=dv[:, l, slot, :], in1=npv, op=ALU.mult), rk + wk2, wk2)
            elif i in (0, 3):
                dve(lambda e: e.tensor_copy(out=dv[:, l, slot, :], in_=mv), rk, wk2)
            else:
                nqv = npostT[:, l * 16 + (0 if i == 2 else 8): l * 16 + (8 if i == 2 else 16)]
                dve(lambda e: e.tensor_tensor(out=dv[:, l, slot, :], in0=mv, in1=nqv, op=ALU.mult), rk, wk2)

        ada_seg(0, 0)
        ada_seg(0, 1)

        def rms_stats(src, srcname, sqv, sqname, presquared=False):
            for c in range(8):
                if not presquared:
                    act(sqv[:, c, :], src[:, c, :], AF.Square, [(srcname, c)], [(sqname, c)])
            for hf in range(2):
                pb, pk = bank()
                for c in range(8):
                    mm(pb[:], onesB[:], sqv[:, c, hf * 512:(hf + 1) * 512], c == 0, c == 7, [(sqname, c), "onesB"], [pk])
                act(rstd[:, hf * 512:(hf + 1) * 512], pb[:], AF.Sqrt, [pk, "eps"], [("rstd", hf)],
                    bias=epsT[:, 0:1], scale=1.0 / 1024.0)
                dve(lambda e, hf=hf: e.reciprocal(out=rstd[:, hf * 512:(hf + 1) * 512], in_=rstd[:, hf * 512:(hf + 1) * 512]),
                    [("rstd", hf)], [("rstd", hf)])

        def mod_norm(l, si, bi_, dst, dstname):
            for c in range(8):
                t = tmp[c % 2]
                dve(lambda e, t=t, c=c: e.scalar_tensor_tensor(out=t, in0=xT[:, c, :], scalar=dv[:, l, si, c:c + 1], in1=rstd,
                                                              op0=ALU.mult, op1=ALU.mult),
                    [("xT", c), ("rstd", 0), ("rstd", 1), ("dv", l, si)], [("tmp", c % 2)])
                act(dst[:, c, :], t, AF.Identity, [("tmp", c % 2), ("dv", l, bi_)], [(dstname, c)], bias=dv[:, l, bi_, c:c + 1], scale=1.0)

        def post_resid(l, gi, ysrc, yname):
            for c in range(8):
                t = tmp[c % 2]
                dve(lambda e, t=t, c=c: e.scalar_tensor_tensor(out=t, in0=ysrc[:, c, :], scalar=dv[:, l, gi, c:c + 1], in1=rstd,
                                                              op0=ALU.mult, op1=ALU.mult),
                    [(yname, c), ("rstd", 0), ("rstd", 1), ("dv", l, gi)], [("tmp", c % 2)])
                dve(lambda e, t=t, c=c: e.tensor_tensor(out=xT[:, c, :], in0=xT[:, c, :], in1=t, op=ALU.add),
                    [("tmp", c % 2), ("xT", c)], [("xT", c)])

        def fm_group(ws, wk, col0, kcs, rhs_of, hf):
            pb, pk = bank()
            for kc in range(kcs):
                rap, rkey = rhs_of(kc, hf)
                mm(pb[:], ws[:, kc, col0:col0 + 128], rap, kc == 0, kc == kcs - 1, [wk, rkey], [pk])
            return pb, pk

        def h_rhs(kc, hf):
            return hT[:, kc, hf * 512:(hf + 1) * 512], ("hT", kc)

        evq = {"n": 0}

        def copy_evac(out, in_, reads, writes, scale=None):
            evq["n"] += 1
            if scale is not None or evq["n"] % 2 == 0:
                act(out, in_, AF.Copy, reads, writes, scale=scale)
            else:
                dve(lambda e: e.tensor_copy(out=out, in_=in_), reads, writes)

        def attn_scores(kblocks, q_rhs, n_q_mm, PT, PTname):
            pi = cnt.setdefault(PTname, 0) % 2
            cnt[PTname] = pi + 1
            pt = PT[pi]
            for kbi, kb in enumerate(kblocks):
                hasb = kb["bias"] is not None
                vb = validT[:, kb["valid"]:kb["valid"] + 1]
                if n_q_mm == 1:
                    pb, pk = bank()
                    kap, kkey = kb["kT"](0)
                    qap, qkey = q_rhs(0)
                    out = pb[:].rearrange("p (c q) -> p c q", c=4)
                    mm(out, kap, qap, True, not hasb, [kkey, qkey], [pk])
                    if hasb:
                        bap, bkey = kb["bias"](0)
                        mm(out, identB[:], bap, False, True, [bkey, "identB"], [pk])
                    act(pt[:, kbi, :], pb[:], AF.Exp, [pk, "valid"], [(PTname, pi, kbi)], bias=vb, scale=1.0)
                else:
                    b0 = bank_pair()
                    pbs = [(psum[b0], ("ps", b0)), (psum[b0 + 1], ("ps", b0 + 1))]
                    for c in range(4):
                        pb, pk = pbs[c % 2]
                        kap, kkey = kb["kT"](c)
                        qap, qkey = q_rhs(c)
                        out = pb[:, (c // 2) * 128:(c // 2 + 1) * 128]
                        mm(out, kap, qap, True, True, [kkey, qkey], [pk])
                    act(pt[:, kbi, :].rearrange("p (b x) -> p b x", b=2), psum_all[:, b0:b0 + 2, 0:256], AF.Exp,
                        [pbs[0][1], pbs[1][1], "valid"], [(PTname, pi, kbi, 0), (PTname, pi, kbi, 1)], bias=vb, scale=1.0)
                    if hasb:
                        eap, ekey = kb["bias"](0)
                        dve(lambda e, eap=eap, kbi=kbi: e.tensor_tensor(out=pt[:, kbi, :].rearrange("p (c q) -> p c q", c=4),
                                                                       in0=pt[:, kbi, :].rearrange("p (c q) -> p c q", c=4), in1=eap, op=ALU.mult),
                            [ekey], [(PTname, pi, kbi, 0), (PTname, pi, kbi, 1)])
                for (ks, qs) in kb.get("zero", ()):
                    dve(lambda e, ks=ks, qs=qs, kbi=kbi: e.memset(
                        pt[ks * 64:(ks + 1) * 64, kbi, :].rearrange("p (c q) -> p c q", c=4)[:, :, qs * 64:(qs + 1) * 64], 0.0),
                        [], [(PTname, pi, kbi, 0), (PTname, pi, kbi, 1)])
            return (pt, pi)

        def attn_pv(kblocks, n_q_mm, h, PTname, otk, otkkey, ocols0, sink_col0):
            pt, pi = h
            nkb = len(kblocks)
            ob, ok = obank()
            for c in range(4):
                pos = c if n_q_mm == 1 else (c % 2) * 2 + c // 2
                for kbi, kb in enumerate(kblocks):
                    vap, vkey = kb["V"](c)
                    ptk = [(PTname, pi, kbi)] if n_q_mm == 1 else [(PTname, pi, kbi, c % 2)]
                    mm(ob[:, c * 80:c * 80 + 65], pt[:, kbi, pos * 128:(pos + 1) * 128], vap, kbi == 0, kbi == nkb - 1,
                       ptk + [vkey], [ok])
            ov = ob[:, 0:320].rearrange("p (c e) -> p c e", c=4)
            ns = cnt.setdefault("nrm", 0) % 2
            cnt["nrm"] = ns + 1
            den = nrm[:, ns, 0:4]
            rec = nrm[:, ns, 8:12]
            nk = ("nrm", ns)
            if sink_col0 is not None:
                dve(lambda e: e.tensor_tensor(out=den, in0=ov[:, :, 64], in1=sinkE[:, sink_col0:sink_col0 + 4], op=ALU.add),
                    [ok, "sinkE"], [nk])
                dve(lambda e: e.reciprocal(out=rec, in_=den), [nk], [nk])
            else:
                dve(lambda e: e.reciprocal(out=rec, in_=ov[:, :, 64]), [ok], [nk])
            for c in range(4):
                act(otk[:, ocols0 + c * 64: ocols0 + (c + 1) * 64], ov[:, c, 0:64], AF.Copy, [ok, nk], [otkkey], scale=rec[:, c:c + 1])

        def otok_to_fm(otk, otkkey, ncols, OT, OTname, ch0, i):
            nch = ncols // 128
            pb, pk = bank()
            pbb = pb[:].bitcast(BF16)
            for ch in range(nch):
                fw.op("pe", lambda e, ch=ch: e.transpose(pbb[:, ch * 128:(ch + 1) * 128], otk[:, ch * 128:(ch + 1) * 128], identB[:]),
                      [otkkey, "identB"], [pk])
            copy_evac(OT[:, ch0:ch0 + nch, i * 128:(i + 1) * 128],
                      pbb[:, 0:nch * 128].rearrange("p (c q) -> p c q", c=nch), [pk],
                      [(OTname, ch0 + ch) for ch in range(nch)])

        def run_pipeline(blocks, hooks=None):
            prev = None

            def finish(p):
                b, h = p
                attn_pv(b["kbl"], b["nq"], h, b["PTname"], b["otk"], b["otkkey"], b["ocols0"], b["sink"])
                if b["post"] is not None:
                    b["post"]()
            for bi_, b in enumerate(blocks):
                h = attn_scores(b["kbl"], b["q_rhs"], b["nq"], b["PT"], b["PTname"])
                if prev is not None:
                    finish(prev)
                prev = (b, h)
                if hooks and bi_ in hooks:
                    hooks[bi_]()
            finish(prev)

        import os as _os
        KSTOP = int(_os.environ.get("KSTOP", "99"))
        for l in range(2):
          for _stage in range(1):
            wv = winx_d[l].rearrange("(kc p) n -> p kc n", p=128)
            if l * 6 + 1 > KSTOP:
                break
            PH = fw.__dict__.setdefault("phases", [])
            PH.append(("L%d norm1" % l, len(fw.prog["pe"])))

            fw.fence(FFN + ["hT"], SQ + ["hT"])
            rms_stats(xT, "xT", sq, "mT")
            mod_norm(l, 0, 1, hT, "hT")

            if l * 6 + 2 > KSTOP:
                break
            PH.append(("L%d C" % l, len(fw.prog["pe"])))
            fw.fence(SQ + WOUT + FFN, MIX_C)
            ws, wk = wload(wv[:, :, 0:512], [128, 8, 512])
            for j in range(4):
                for hf in range(2):
                    pb, pk = fm_group(ws, wk, j * 128, 8, h_rhs, hf)
                    copy_evac(UCT[:, j, hf * 512:(hf + 1) * 512], pb[:], [pk], [("UCT", j)])
            for tb in range(8):
                for gp in range(2):
                    pb, pk = bank()
                    for gg in range(2):
                        g = gp * 2 + gg
                        mm(pb[:, gg * 256:(gg + 1) * 256], UCT[:, g, tb * 128:(tb + 1) * 128], csB[:], True, True,
                           [("UCT", g), "csB"], [pk])
                    copy_evac(ABtok[:, tb, gp * 2:gp * 2 + 2, :], pb[:].rearrange("p (g e) -> p g e", g=2), [pk], [("ABtok", tb)])
            fw.fence(FFN, OTS)
            for hf in range(2):
                cv = dftc_d.rearrange("(tb p) n -> p tb n", p=128)[:, :, hf * 512:(hf + 1) * 512]
                sv = dfts_d.rearrange("(tb p) n -> p tb n", p=128)[:, :, hf * 512:(hf + 1) * 512]
                wc, wck = wload(cv, [128, 8, 512])
                wsn, wsk = wload(sv, [128, 8, 512])
                for g in range(4):
                    pb, pk = bank()
                    for tb in range(8):
                        mm(pb[:], ABtok[:, tb, g, 0:128], wc[:, tb, :], tb == 0, False, [("ABtok", tb), wck], [pk])
                        mm(pb[:], ABtok[:, tb, g, 128:256], wsn[:, tb, :], False, tb == 7, [("ABtok", tb), wsk], [pk])
                    copy_evac(OCT[:, g, hf * 512:(hf + 1) * 512], pb[:], [pk], [("OCT", g)])

            if l * 6 + 3 > KSTOP:
                break
            PH.append(("L%d A" % l, len(fw.prog["pe"])))
            fw.fence(MIX_C, MIX_A)
            sp_load("ropeld", ropeT, rope_d.rearrange("k p t -> p k t"), ["rope"])
            fw.dma("pool", "trild", lambda e: e.dma_start(out=triB, in_=tri_d.rearrange("p (a b) -> p a b", a=2)), writes=["tri"])
            for tb in range(2):
                s = tb % 2
                sp_load(("xin", s), xtok[s][:, 0:128], cak_d[l, tb * 128:(tb + 1) * 128, :], [("xtok", s)])
                pb, pk = bank()
                tr(pb[:, 0:128], xtok[s][:, 0:128], [("xtok", s)], [pk])
                for g in range(2):
                    copy_evac(KcATp[g][g * 64:(g + 1) * 64, tb * 128:(tb + 1) * 128], pb[g * 64:(g + 1) * 64, 0:128], [pk], [("KcAT", g, tb)])
            for tb in range(2):
                fw.dma("pool", "vcald", lambda e, l=l, tb=tb: e.dma_start(out=VcA[:, tb, :, 0:64],
                                                                        in_=cav_d[l, tb * 128:(tb + 1) * 128, :].rearrange("p (h d) -> p h d", h=2)),
                       writes=["VcA"])
            dve(lambda e: e.memset(VcA[:, :, :, 64:65], 1.0), [], ["VcA1"])
            for g in range(2):
                og = 1 - g
                dve(lambda e, g=g, og=og: e.memset(KATp[g][og * 64:(og + 1) * 64, :], 0.0), [], [("KATz", g)])
                dve(lambda e, g=g, og=og: e.memset(KcATp[g][og * 64:(og + 1) * 64, :], 0.0), [], [("KcATz", g)])
            dve(lambda e: e.memset(VA[:, :, :, 64:65], 1.0), [], ["VA1"])
            KSUB = int(_os.environ.get("KSUB", "99"))
            if KSUB < 1:
                break
            for u in range(2):
                ws, wk = wload(wv[:, :, 512 + u * 512: 1024 + u * 512], [128, 8, 512])
                for cc in range(2):
                    c = u * 2 + cc
                    for hf in range(2):
                        p1, k1 = fm_group(ws, wk, cc * 256, 8, h_rhs, hf)
                        p2, k2 = fm_group(ws, wk, cc * 256 + 128, 8, h_rhs, hf)
                        sl = slice(hf * 512, (hf + 1) * 512)
                        dve(lambda e, p1=p1, sl=sl: e.tensor_tensor(out=tmp[0][:, 0:512], in0=p1[:], in1=ropeT[:, 0, sl], op=ALU.mult),
                            [k1, "rope"], [("tmp", 0)])
                        dve(lambda e, p2=p2, sl=sl: e.tensor_tensor(out=tmp[1][:, 0:512], in0=p2[:], in1=ropeT[:, 1, sl], op=ALU.mult),
                            [k2, "rope"], [("tmp", 1)])
                        dve(lambda e, c=c, sl=sl: e.tensor_tensor(out=QAT[:, c, sl], in0=tmp[0][:, 0:512], in1=tmp[1][:, 0:512], op=ALU.add),
                            [("tmp", 0), ("tmp", 1)], [("QAT", c)])
            if KSUB < 2:
                break
            ws, wk = wload(wv[:, :, 1536:2048], [128, 8, 512])
            for hf in range(2):
                p1, k1 = fm_group(ws, wk, 0, 8, h_rhs, hf)
                p2, k2 = fm_group(ws, wk, 128, 8, h_rhs, hf)
                sl = slice(hf * 512, (hf + 1) * 512)
                dve(lambda e, p1=p1, sl=sl: e.tensor_tensor(out=tmp[0][:, 0:512], in0=p1[:], in1=ropeT[:, 2, sl], op=ALU.mult),
                    [k1, "rope"], [("tmp", 0)])
                dve(lambda e, p2=p2, sl=sl: e.tensor_tensor(out=tmp[1][:, 0:512], in0=p2[:], in1=ropeT[:, 3, sl], op=ALU.mult),
                    [k2, "rope"], [("tmp", 1)])
                for g in range(2):
                    dve(lambda e, sl=sl, g=g: e.tensor_tensor(out=KATp[g][g * 64:(g + 1) * 64, sl], in0=tmp[0][g * 64:(g + 1) * 64, 0:512],
                                                             in1=tmp[1][g * 64:(g + 1) * 64, 0:512], op=ALU.add),
                        [("tmp", 0), ("tmp", 1)], [("KAT", g, hf)])
            KV = int(_os.environ.get("KVAR", "7"))
            for tb in range(8 if KV & 8 == 0 else 0):
                pb, pk = bank()
                for kc in range(8):
                    mm(pb[:, 0:256], hT[:, kc, tb * 128:(tb + 1) * 128], ws[:, kc, 256:512], kc == 0, kc == 7, [wk, ("hT", kc)], [pk])
                s = tb % 2
                if KV & 1:
                    act(stage[s][:, 0:256], pb[:, 0:256], AF.Copy, [pk], [("stageA", s)])
                if KV & 2:
                    dve(lambda e, pb=pb, tb=tb: e.tensor_copy(out=VA[:, tb, :, 0:64], in_=pb[:, 128:256].rearrange("p (h d) -> p h d", h=2)),
                        [pk], [("VA", tb)])
                if KV & 4:
                    fw.dma("sp", ("kvoA", s), lambda e, s=s, tb=tb, l=l: e.dma_start(out=kv_d[l, tb * 128:(tb + 1) * 128, 0:256], in_=stage[s][:, 0:256]),
                           reads=[("stageA", s)], writes=[("kvo", l, tb, 0)])
            if KSUB < 3:
                break
            PH.append(("L%d A-attn" % l, len(fw.prog["pe"])))
            dve(lambda e: e.memset(ctmpA[:, 0, 0:1], 0.0),
                [("QAT", c) for c in range(4)] + ["VA1", "VcA1"] + [("KATz", g) for g in range(2)] + [("KcATz", g) for g in range(2)]
                + [("KAT", g, hf) for g in range(2) for hf in range(2)] + [("KcAT", g, tb) for g in range(2) for tb in range(2)],
                [("QATall",)])
            blocksA = []
            for i in range(8 if KSUB > 3 else 1):
                for g in range(2):
                    def mk_local(kbk, tri_idx, vcol, g=g):
                        return {"kT": (lambda c, kbk=kbk, g=g: (KATp[g][:, kbk * 128:(kbk + 1) * 128], ("QATall",))),
                                "bias": None if tri_idx is None else (lambda c, tri_idx=tri_idx: (triB[:, tri_idx, :].rearrange("p (c q) -> p c q", c=4), "tri")),
                                "valid": vcol,
                                "V": (lambda c, kbk=kbk, g=g: (VA[:, kbk, g, 0:65], ("VA", kbk)))}

                    def mk_ctx(tb, vcol, g=g):
                        return {"kT": (lambda c, tb=tb, g=g: (KcATp[g][:, tb * 128:(tb + 1) * 128], ("QATall",))),
                                "bias": None, "valid": vcol,
                                "V": (lambda c, tb=tb, g=g: (VcA[:, tb, g, 0:65], "VcA"))}
                    kbl = [mk_local(max(i - 1, 0), 0, i * 5 + 0), mk_local(i, None, i * 5 + 1), mk_local(min(i + 1, 7), 1, i * 5 + 2),
                           mk_ctx(0, i * 5 + 3), mk_ctx(1, i * 5 + 4)]
                    oi = i % 2
                    post = None
                    if g == 1:
                        post = (lambda oi=oi, i=i: otok_to_fm(OtokA[oi], ("OtokA", oi), 512, OAT, "OAT", 0, i))
                    blocksA.append({"kbl": kbl, "q_rhs": (lambda c, i=i: (QAT[:, :, i * 128:(i + 1) * 128], ("QATall",))), "nq": 1,
                                    "PT": PTa, "PTname": "PTa", "otk": OtokA[oi], "otkkey": ("OtokA", oi), "ocols0": g * 256,
                                    "sink": l * 8 + g * 4, "post": post})
            run_pipeline(blocksA, hooks={3: (lambda l=l: ada_seg(l, 2)), 9: (lambda l=l: ada_seg(l, 3))} if KSUB > 3 else None)

            if l * 6 + 4 > KSTOP:
                break
            PH.append(("L%d B" % l, len(fw.prog["pe"])))
            fw.fence(MIX_A, MIX_B)
            for tb in range(2):
                for ch in range(4):
                    s = (tb * 4 + ch) % 2
                    sp_load(("xin", s), xtok[s][:, 0:128], cbk_d[l, tb * 128:(tb + 1) * 128, ch * 128:(ch + 1) * 128], [("xtok", s)])
                    pb, pk = bank()
                    tr(pb[:, 0:128], xtok[s][:, 0:128], [("xtok", s)], [pk])
                    copy_evac(KcBT[:, ch, tb * 128:(tb + 1) * 128], pb[:, 0:128], [pk], ["KcBT"])
            for tb in range(2):
                fw.dma("pool", "vcbld", lambda e, l=l, tb=tb: e.dma_start(out=VcB[:, tb, :, 0:64],
                                                                        in_=cbv_d[l, tb * 128:(tb + 1) * 128, :].rearrange("p (h d) -> p h d", h=8)),
                       writes=["VcB"])
            dve(lambda e: e.memset(VcB[:, :, :, 64:65], 1.0), [], ["VcB1"])
            dve(lambda e: e.memset(VB[:, :, :, 64:65], 1.0), [], ["VB1"])
            ws, wk = wload(wv[:, :, 2048:2560], [128, 8, 512])
            for j in range(4):
                for hf in range(2):
                    pb, pk = fm_group(ws, wk, j * 128, 8, h_rhs, hf)
                    copy_evac(QBT[:, j, hf * 512:(hf + 1) * 512], pb[:], [pk], [("QBT", j)], scale=0.125)
            ws, wk = wload(wv[:, :, 2560:3072], [128, 8, 512])
            for j in range(4):
                for hf in range(2):
                    pb, pk = fm_group(ws, wk, j * 128, 8, h_rhs, hf)
                    copy_evac(KBT[:, j, hf * 512:(hf + 1) * 512], pb[:], [pk], [("KBT", j)])
            wsk_, wkk = wload(wv[:, :, 3072:3584], [128, 8, 512])
            wsv_, wkv = wload(wv[:, :, 3584:4096], [128, 8, 512])
            for tb in range(8):
                s = tb % 2
                pb, pk = bank()
                for kc in range(8):
                    mm(pb[:], hT[:, kc, tb * 128:(tb + 1) * 128], wsk_[:, kc, :], kc == 0, kc == 7, [wkk, ("hT", kc)], [pk])
                act(stage[s][:, 256:768], pb[:], AF.Copy, [pk], [("stageB", s)])
                pb2, pk2 = bank()
                for kc in range(8):
                    mm(pb2[:], hT[:, kc, tb * 128:(tb + 1) * 128], wsv_[:, kc, :], kc == 0, kc == 7, [wkv, ("hT", kc)], [pk2])
                act(stage[s][:, 768:1280], pb2[:], AF.Copy, [pk2], [("stageB", s)])
                dve(lambda e, pb2=pb2, tb=tb: e.tensor_copy(out=VB[:, tb, :, 0:64], in_=pb2[:].rearrange("p (h d) -> p h d", h=8)),
                    [pk2], [("VB", tb)])
                fw.dma("sp", ("kvoB", s), lambda e, s=s, tb=tb, l=l: e.dma_start(out=kv_d[l, tb * 128:(tb + 1) * 128, 256:1280], in_=stage[s][:, 256:1280]),
                       reads=[("stageB", s)], writes=[("kvo", l, tb, 1)])
            dve(lambda e: e.memset(ctmpB[:, 0, 0:1], 0.0), [("QBT", c) for c in range(4)] + [("KBT", c) for c in range(4)] + ["VB1", "VcB1", "KcBT", "VcB"],
                [("QKBall",)])
            PH.append(("L%d B-attn" % l, len(fw.prog["pe"])))
            for hg in range(2):
                fw.dma("pool", "ttbld", lambda e, l=l, hg=hg: e.dma_start(out=TTB, in_=ttb_d[l, hg].rearrange("p (h c) -> p h c", h=4)),
                       writes=["TTB"])
                for c4 in range(4):
                    act(TTB[:, c4, :], TTB[:, c4, :], AF.Exp, ["TTB"], ["TTB"])
                blocksB = []
                for i in range(8):
                    kb0 = min(max(i - 2, 0), 3)
                    kbl = []
                    for j in range(5):
                        kbk = kb0 + j
                        delta = kbk - i
                        pos0 = 8 - 2 * delta
                        d = {"kT": (lambda c, kbk=kbk, hg=hg: (KBT[(c % 2) * 64:(c % 2) * 64 + 64, hg * 2 + c // 2, kbk * 128:(kbk + 1) * 128], ("QKBall",))),
                             "bias": (lambda c, pos0=pos0: (TTB[:, :, pos0 * 64: pos0 * 64 + 128], "TTB")),
                             "valid": 40 + i * 7 + j,
                             "V": (lambda c, kbk=kbk, hg=hg: (VB[:, kbk, hg * 4 + c, 0:65], ("VB", kbk)))}
                        if 2 <= i <= 5 and delta == -2:
                            d["zero"] = [(0, 1)]
                        if 2 <= i <= 5 and delta == 2:
                            d["zero"] = [(0, 0), (1, 0), (1, 1)]
                        kbl.append(d)
                    for tb in range(2):
                        kbl.append({"kT": (lambda c, tb=tb, hg=hg: (KcBT[(c % 2) * 64:(c % 2) * 64 + 64, hg * 2 + c // 2, tb * 128:(tb + 1) * 128], ("QKBall",))),
                                    "bias": None, "valid": 40 + i * 7 + 5 + tb,
                                    "V": (lambda c, tb=tb, hg=hg: (VcB[:, tb, hg * 4 + c, 0:65], ("QKBall",)))})
                    oi = i % 2
                    blocksB.append({"kbl": kbl,
                                    "q_rhs": (lambda c, i=i, hg=hg: (QBT[(c % 2) * 64:(c % 2) * 64 + 64, hg * 2 + c // 2, i * 128:(i + 1) * 128], ("QKBall",))),
                                    "nq": 4, "PT": PTb, "PTname": "PTb", "otk": OtokB[oi], "otkkey": ("OtokB", oi), "ocols0": 0, "sink": None,
                                    "post": (lambda oi=oi, i=i, hg=hg: otok_to_fm(OtokB[oi], ("OtokB", oi), 256, OBT, "OBT", hg * 2, i))})
                if hg == 0:
                    hk = {1: (lambda l=l: ada_seg(l, 4)), 4: (lambda l=l: ada_seg(l, 5))}
                else:
                    hk = {1: (lambda: ada_seg(1, 0)), 4: (lambda: ada_seg(1, 1))} if l == 0 else None
                run_pipeline(blocksB, hooks=hk)

            if l * 6 + 5 > KSTOP:
                break
            PH.append(("L%d merge" % l, len(fw.prog["pe"])))
            fw.fence(MIX_B + MIX_A + MIX_C + SQ, WOUT)
            wbv = wbr_d[l].rearrange("b (kc p) n -> p (b kc) n", p=128)
            OTv = [OAT, OBT, OCT]
            OTn = ["OAT", "OBT", "OCT"]
            for j in range(8):
                wg, wgk = wload(wv[:, :, 4096 + j * 384: 4096 + (j + 1) * 384], [128, 8, 384])
                wb, wbk = wload(wbv[:, :, j * 128:(j + 1) * 128], [128, 12, 128])
                for hf in range(2):
                    sl = slice(hf * 512, (hf + 1) * 512)
                    for br in range(3):
                        pg, kg = fm_group(wg, wgk, br * 128, 8, h_rhs, hf)
                        sg = tmp[0][:, 0:512] if br % 2 == 0 else tmp[1][:, 0:512]
                        sgk = ("tmp", br % 2)
                        act(sg, pg[:], AF.Sigmoid, [kg], [sgk])
                        py, ky = bank()
                        for kc in range(4):
                            mm(py[:], wb[:, br * 4 + kc, :], OTv[br][:, kc, sl], kc == 0, kc == 3, [wbk, (OTn[br], kc)], [ky])
                        if br == 0:
                            dve(lambda e, py=py, sg=sg: e.tensor_tensor(out=rstd[:, 0:512], in0=py[:], in1=sg, op=ALU.mult),
                                [ky, sgk], [("rstd", 0)])
                        else:
                            dve(lambda e, py=py, sg=sg: e.tensor_tensor(out=sg, in0=py[:], in1=sg, op=ALU.mult), [ky, sgk], [sgk])
                            if br == 1:
                                dve(lambda e, sg=sg: e.tensor_tensor(out=rstd[:, 0:512], in0=rstd[:, 0:512], in1=sg, op=ALU.add),
                                    [sgk, ("rstd", 0)], [("rstd", 0)])
                            else:
                                dve(lambda e, sg=sg, j=j, sl=sl: e.tensor_tensor(out=mT[:, j, sl], in0=rstd[:, 0:512], in1=sg, op=ALU.add),
                                    [sgk, ("rstd", 0)], [("mT", j)])
            PH.append(("L%d wout" % l, len(fw.prog["pe"])))
            wov = wout_d[l].rearrange("(kc p) n -> p kc n", p=128)
            m_rhs = lambda kc, hf: (mT[:, kc, hf * 512:(hf + 1) * 512], ("mT", kc))
            for u in range(2):
                ws, wk = wload(wov[:, :, u * 512:(u + 1) * 512], [128, 8, 512])
                for jj in range(4):
                    j = u * 4 + jj
                    for hf in range(2):
                        pb, pk = fm_group(ws, wk, jj * 128, 8, m_rhs, hf)
                        sl = slice(hf * 512, (hf + 1) * 512)
                        act(sq2[:, j, sl], pb[:], AF.Square, [pk], [("hT", j)])
                        dve(lambda e, pb=pb, j=j, sl=sl: e.tensor_copy(out=yT[:, j, sl], in_=pb[:]), [pk], [("yT", j)])
            rms_stats(yT, "yT", sq2, "hT", presquared=True)
            post_resid(l, 2, yT, "yT")

            if l * 6 + 6 > KSTOP:
                break
            PH.append(("L%d ffn" % l, len(fw.prog["pe"])))
            fw.fence(WOUT + OTS + SQ, FFN)
            rms_stats(xT, "xT", sq, "mT")
            mod_norm(l, 3, 4, hT, "hT")
            wfv = wfi_d[l].rearrange("(kc p) (s n) -> p kc s n", p=128, s=2)
            for jp in range(11):
                ws, wk = wload(wfv[:, :, :, jp * 256:(jp + 1) * 256], [128, 8, 2, 256])
                for jj in range(2):
                    j = jp * 2 + jj
                    for hf in range(2):
                        sl = slice(hf * 512, (hf + 1) * 512)
                        pg, kg = bank()
                        for kc in range(8):
                            mm(pg[:], ws[:, kc, 0, jj * 128:(jj + 1) * 128], hT[:, kc, sl], kc == 0, kc == 7, [wk, ("hT", kc)], [kg])
                        pu, ku = bank()
                        for kc in range(8):
                            mm(pu[:], ws[:, kc, 1, jj * 128:(jj + 1) * 128], hT[:, kc, sl], kc == 0, kc == 7, [wk, ("hT", kc)], [ku])
                        sg = tmp[(j * 2 + hf) % 2][:, 0:512]
                        sgk = ("tmp", (j * 2 + hf) % 2)
                        act(sg, pg[:], AF.Silu, [kg], [sgk])
                        dve(lambda e, pu=pu, sg=sg, j=j, sl=sl: e.tensor_tensor(out=aT[:, j, sl], in0=pu[:], in1=sg, op=ALU.mult),
                            [ku, sgk], [("aT", j)])
            wfov = wfo_d[l].rearrange("(kc p) n -> p kc n", p=128)
            a_rhs = lambda kc, hf: (aT[:, kc, hf * 512:(hf + 1) * 512], ("aT", kc))
            for u in range(4):
                wsA, wkA = wload(wfov[:, 0:11, u * 256:(u + 1) * 256], [128, 11, 256])
                wsB, wkB = wload(wfov[:, 11:22, u * 256:(u + 1) * 256], [128, 11, 256])
                for jj in range(2):
                    j = u * 2 + jj
                    for hf in range(2):
                        pb, pk = bank()
                        for kc in range(22):
                            ws_, wk_ = (wsA, wkA) if kc < 11 else (wsB, wkB)
                            rap, rkey = a_rhs(kc, hf)
                            mm(pb[:], ws_[:, kc % 11, jj * 128:(jj + 1) * 128], rap, kc == 0, kc == 21, [wk_, rkey], [pk])
                        sl = slice(hf * 512, (hf + 1) * 512)
                        act(sq2[:, j, sl], pb[:], AF.Square, [pk], [("hT", j)])
                        dve(lambda e, pb=pb, j=j, sl=sl: e.tensor_copy(out=y2T[:, j, sl], in_=pb[:]), [pk], [("y2T", j)])
            rms_stats(y2T, "y2T", sq2, "hT", presquared=True)
            post_resid(l, 5, y2T, "y2T")

        fw.__dict__.setdefault("phases", []).append(("out", len(fw.prog["pe"])))
        for b in range(8):
            s = b % 2
            for g in range(2):
                pb, pk = bank()
                for j in range(4):
                    c = g * 4 + j
                    tr(pb[:, j * 128:(j + 1) * 128], xT[:, c, b * 128:(b + 1) * 128], [("xT", c)], [pk])
                copy_evac(xtok[s][:, g * 512:(g + 1) * 512], pb[:], [pk], [("xtok", s)])
            fw.dma("sp", ("yout", s), lambda e, b=b, s=s: e.dma_start(out=y_d[b * 128:(b + 1) * 128, :], in_=xtok[s]),
                   reads=[("xtok", s)], writes=[("yo", b)])
        fw.final_wait("sp", [("yo", b) for b in range(8)] + [k for k in [("kvo", l, tb, k) for l in range(2) for tb in range(8) for k in range(2)] if k in fw.state])
        fw.emit(st)
        nc._phases = fw.__dict__.get("phases", [])
    return nc


def _rope_tables(sample):
    out = np.zeros((4, 128, T), np.float32)
    d = np.arange(128) % 64
    if sample:
        t = np.arange(T)
        row = (t // 64).astype(np.float32)
        col = (t % 64).astype(np.float32)
        inv = (np.float32(10000.0) ** (-np.arange(16, dtype=np.float32) / np.float32(16))).astype(np.float32)
        ang = np.concatenate([row[:, None] * inv, col[:, None] * inv], axis=-1).astype(np.float32)
        cos = np.cos(ang).astype(np.float32)
        sin = np.sin(ang).astype(np.float32)
        C = cos[:, d % 32].T
        S = sin[:, d % 32].T * np.where(d < 32, -1.0, 1.0)[:, None].astype(np.float32)
    else:
        C = np.ones((128, T), np.float32)
        S = np.zeros((128, T), np.float32)
    out[0] = C * np.float32(0.125)
    out[1] = S * np.float32(0.125)
    out[2] = C
    out[3] = S
    return out


def _dft_tables(sample):
    t = np.arange(T, dtype=np.int64)
    if sample:
        n = T
        ph = (t[:, None] * t[None, :]) % n
        m = np.ones((T, T))
    else:
        n = 256
        ph = ((t[:, None] % n) * (t[None, :] % n)) % n
        m = ((t[:, None] // n) == (t[None, :] // n)).astype(np.float64)
    ang = 2.0 * np.pi * ph / n
    import ml_dtypes
    ct = (np.cos(ang) * m / np.sqrt(n)).astype(np.float32).astype(ml_dtypes.bfloat16)
    nst = (-np.sin(ang) * m / np.sqrt(n)).astype(np.float32).astype(ml_dtypes.bfloat16)
    return ct, nst


def _cs_table():
    c = np.arange(128, dtype=np.int64)
    ang = 2.0 * np.pi * ((c[:, None] * c[None, :]) % 128) / 128.0
    return np.concatenate([np.cos(ang), np.sin(ang)], axis=1).astype(np.float32) / np.float32(np.sqrt(128.0))


def _ttb_tables(b_rpb, sample):
    out = np.zeros((2, 2, 128, 4, 18, 64), np.float32)
    if not sample:
        return out.reshape(2, 2, 128, 4 * 1152)
    kcol = np.arange(64)[:, None]
    c = np.arange(64)[None, :]
    ws = np.clip(c - 8, 0, 48)
    colok = (kcol >= ws) & (kcol < ws + 16)
    dc = np.clip(kcol - c + 15, 0, 30)
    for l in range(2):
        for h in range(8):
            for ks in range(2):
                for pos in range(18):
                    a = 15 - pos + ks
                    if 0 <= a <= 14:
                        tile = np.where(colok, b_rpb[l, h, a][dc], np.float32(NEG))
                    else:
                        tile = np.full((64, 64), NEG, np.float32)
                    out[l, h // 4, ks * 64:(ks + 1) * 64, ((h % 4) % 2) * 2 + (h % 4) // 2, pos, :] = tile
    return out.reshape(2, 2, 128, 4 * 1152)


def _tri_table(sample):
    out = np.zeros((128, 2, 4, 128), np.float32)
    if sample:
        j = np.arange(128)[:, None]
        q = np.arange(128)[None, :]
        out[:, 0] = np.where(j >= q, 0.0, NEG)[:, None, :]
        out[:, 1] = np.where(j <= q, 0.0, NEG)[:, None, :]
    return out.reshape(128, 1024)


def _valid_table(sample):
    v = np.zeros((96,), np.float32)
    for i in range(8):
        if sample:
            a = [i >= 1, True, i <= 6, True, True]
        else:
            a = [i % 2 == 1, True, i % 2 == 0, False, False]
        for k in range(5):
            v[i * 5 + k] = 0.0 if a[k] else NEG
        kb0 = min(max(i - 2, 0), 3)
        for j in range(5):
            kb = kb0 + j
            if sample:
                ok = (kb <= 3) if i <= 1 else ((kb >= 4) if i >= 6 else True)
            else:
                ok = (kb // 2) == (i // 2)
            v[40 + i * 7 + j] = 0.0 if ok else NEG
        for tb in range(2):
            v[40 + i * 7 + 5 + tb] = 0.0 if sample else NEG
    return np.broadcast_to(v[None, :], (128, 96)).copy()


def _winx(w_in):
    idx = []
    idx += list(range(2304, 2816))
    rot = lambda base: [base + (d + 32) % 64 for d in range(64)]
    nat = lambda base: [base + d for d in range(64)]
    for c in range(4):
        idx += nat(c * 64) + nat((c + 4) * 64)
        idx += rot(c * 64) + rot((c + 4) * 64)
    idx += nat(512) + nat(576)
    idx += rot(512) + rot(576)
    idx += list(range(512, 640)) + list(range(640, 768))
    idx += list(range(768, 1280))
    idx += list(range(1280, 1792))
    idx += list(range(1280, 1792))
    idx += list(range(1792, 2304))
    for j in range(8):
        for br in range(3):
            idx += list(range(2816 + br * 1024 + j * 128, 2816 + br * 1024 + (j + 1) * 128))
    idx = np.asarray(idx)
    assert idx.shape[0] == WIN_COLS
    return np.ascontiguousarray(w_in[:, :, idx])


_CACHE = {}


def make_in_maps(x_prompt, x_sample, cache_a_k, cache_a_v, cache_b_k, cache_b_v, c, c_ctx,
                 w_ada, b_ada, norm_pre, norm_post, w_in, a_sink, b_rpb, w_branch, w_out,
                 w_ffn_in, w_ffn_out):
    f = lambda a: np.ascontiguousarray(np.asarray(a, dtype=np.float32))
    x_prompt, x_sample = f(x_prompt), f(x_sample)
    cache_a_k, cache_a_v, cache_b_k, cache_b_v = f(cache_a_k), f(cache_a_v), f(cache_b_k), f(cache_b_v)
    c, c_ctx = f(c), f(c_ctx)
    w_ada, b_ada, norm_pre, norm_post, w_in = f(w_ada), f(b_ada), f(norm_pre), f(norm_post), f(w_in)
    a_sink, b_rpb, w_branch, w_out, w_ffn_in, w_ffn_out = f(a_sink), f(b_rpb), f(w_branch), f(w_out), f(w_ffn_in), f(w_ffn_out)

    shared = {
        "w_ada": w_ada,
        "b_adaT": np.ascontiguousarray(b_ada.reshape(2, 48, 128).transpose(0, 2, 1)),
        "npreT": np.ascontiguousarray(norm_pre.reshape(2, 2, 8, 128).transpose(3, 0, 1, 2).reshape(128, 32)),
        "npostT": np.ascontiguousarray(norm_post.reshape(2, 2, 8, 128).transpose(3, 0, 1, 2).reshape(128, 32)),
        "w_inx": _winx(w_in),
        "sinkb": np.ascontiguousarray(np.broadcast_to(a_sink.reshape(1, 16), (128, 16))),
        "dft_cs": _cs_table(),
        "w_branch": w_branch, "w_out": w_out, "w_ffn_in": w_ffn_in, "w_ffn_out": w_ffn_out,
        "ident": np.eye(128, dtype=np.float32),
    }
    per_type = {}
    for sample in (False, True):
        ct, nst = _dft_tables(sample)
        per_type[sample] = {
            "ttb": _ttb_tables(b_rpb, sample), "tri": _tri_table(sample), "valid": _valid_table(sample),
            "rope": _rope_tables(sample), "dft_ct": ct, "dft_nst": nst,
        }
    in_maps = []
    for core in range(8):
        m = dict(shared)
        if core < 4:
            m.update(per_type[False])
            m["x"] = np.ascontiguousarray(x_prompt[core * 4:(core + 1) * 4].reshape(T, 1024))
            m["cvec"] = np.ascontiguousarray(c_ctx.reshape(8, 128).T)
            m["cak"] = np.zeros((2, 256, 128), np.float32)
            m["cav"] = np.zeros((2, 256, 128), np.float32)
            m["cbk"] = np.zeros((2, 256, 512), np.float32)
            m["cbv"] = np.zeros((2, 256, 512), np.float32)
        else:
            b = core - 4
            m.update(per_type[True])
            m["x"] = np.ascontiguousarray(x_sample[b])
            m["cvec"] = np.ascontiguousarray(c[b].reshape(8, 128).T)
            m["cak"] = np.ascontiguousarray(cache_a_k[b].reshape(2, 256, 128))
            m["cav"] = np.ascontiguousarray(cache_a_v[b].reshape(2, 256, 128))
            m["cbk"] = np.ascontiguousarray(cache_b_k[b].reshape(2, 256, 512))
            m["cbv"] = np.ascontiguousarray(cache_b_v[b].reshape(2, 256, 512))
        in_maps.append(m)

    return in_maps


def kernel(**inputs):
    in_maps = make_in_maps(**inputs)
    if "nc" not in _CACHE:
        _CACHE["nc"] = build_program()
    nc = _CACHE["nc"]
    res = run_bass_kernel_spmd(nc, in_maps, core_ids=list(range(8)))
    return assemble(res.results)


def assemble(outs):
    y_prompt = np.concatenate([outs[k]["y"].reshape(4, 256, 1024) for k in range(4)], axis=0).astype(np.float32)
    y_sample = np.stack([outs[4 + k]["y"] for k in range(4)], axis=0).astype(np.float32)
    kv = np.concatenate([outs[k]["kvout"].reshape(2, 4, 256, 1280).transpose(1, 0, 2, 3) for k in range(4)], axis=0)
    new_a_k = np.ascontiguousarray(kv[..., 0:128]).reshape(16, 2, 256, 2, 64)
    new_a_v = np.ascontiguousarray(kv[..., 128:256]).reshape(16, 2, 256, 2, 64)
    new_b_k = np.ascontiguousarray(kv[..., 256:768]).reshape(16, 2, 256, 8, 64)
    new_b_v = np.ascontiguousarray(kv[..., 768:1280]).reshape(16, 2, 256, 8, 64)
    return (y_prompt, y_sample, new_a_k, new_a_v, new_b_k, new_b_v)
```

```python
import numpy as np
from contextlib import ExitStack
import concourse.bass as bass
import concourse.mybir as mybir
from concourse.bass_utils import run_bass_kernel_spmd

F32 = mybir.dt.float32
BF16 = mybir.dt.bfloat16
AF = mybir.ActivationFunctionType
ALU = mybir.AluOpType

NEG = -1e30
T = 1024
NS = 5
SLOT = 4096
WIN_COLS = 7168


class FW:
    def __init__(self, nc):
        self.nc = nc
        self.prog = {e: [] for e in ("pe", "act", "dve", "pool", "sp")}
        self.state = {}
        self.pending = {}
        self.dma_count = {}
        self.nseq = {e: 0 for e in self.prog}

    @staticmethod
    def _name(k):
        return k[0] if isinstance(k, tuple) else k

    def _get(self, k):
        st = self.state.get(k)
        if st is None:
            nm = self._name(k)
            if nm in self.pending:
                st = {"w": None, "r": dict(self.pending[nm])}
                self.state[k] = st
        return st

    def _collect(self, reads, writes):
        deps = {}

        def add(kk, v):
            if kk not in deps or deps[kk] < v:
                deps[kk] = v
        for k in reads:
            st = self._get(k)
            if st and st["w"] is not None:
                add((st["w"][0], st["w"][1]), st["w"][2])
        for k in writes:
            st = self._get(k)
            if st:
                if st["w"] is not None:
                    add((st["w"][0], st["w"][1]), st["w"][2])
                for kk, v in st["r"].items():
                    add(kk, v)
        return deps

    def _update(self, ticket, reads, writes):
        kk = (ticket[0], ticket[1])
        for k in reads:
            st = self.state.setdefault(k, {"w": None, "r": {}})
            if st["r"].get(kk, -1) < ticket[2]:
                st["r"][kk] = ticket[2]
        for k in writes:
            self.state[k] = {"w": ticket, "r": {}}

    def fence(self, old_names, new_names):
        old_names = set(old_names)
        new_names = set(new_names)
        F = {}
        for k, st in self.state.items():
            if self._name(k) in old_names:
                if st["w"] is not None:
                    kk = (st["w"][0], st["w"][1])
                    F[kk] = max(F.get(kk, -1), st["w"][2])
                for kk, v in st["r"].items():
                    F[kk] = max(F.get(kk, -1), v)
        for k, st in self.state.items():
            if self._name(k) in new_names:
                for kk, v in F.items():
                    st["r"][kk] = max(st["r"].get(kk, -1), v)
        for nm in new_names:
            p = self.pending.setdefault(nm, {})
            for kk, v in F.items():
                p[kk] = max(p.get(kk, -1), v)

    def op(self, eng, fn, reads=(), writes=()):
        ps_r = [k for k in reads if self._name(k) == "ps"]
        if ps_r:
            reads = [k for k in reads if self._name(k) != "ps"]
            writes = list(writes) + ps_r
        deps = self._collect(reads, writes)
        seq = self.nseq[eng]
        self.nseq[eng] += 1
        ticket = ("c", eng, seq)
        self.prog[eng].append({"fn": fn, "deps": deps, "ticket": ticket, "dma": None})
        self._update(ticket, reads, writes)

    def dma(self, queue, semkey, fn, reads=(), writes=()):
        deps = self._collect(reads, writes)
        cnt = self.dma_count.get(semkey, 0) + 1
        self.dma_count[semkey] = cnt
        ticket = ("d", semkey, cnt * 16)
        seq = self.nseq[queue]
        self.nseq[queue] += 1
        self.prog[queue].append({"fn": fn, "deps": deps, "ticket": ("c", queue, seq), "dma": semkey})
        self._update(ticket, reads, writes)

    def final_wait(self, queue, keys):
        deps = self._collect(keys, ())
        self.prog[queue].append({"fn": None, "deps": deps, "ticket": None, "dma": None})

    def emit(self, stack):
        nc = self.nc
        signal = {e: set() for e in self.prog}
        for e, lst in self.prog.items():
            for ins in lst:
                for (kind, key), val in ins["deps"].items():
                    if kind == "c":
                        if key == e and e in ("pe", "sp"):
                            continue
                        signal[key].add(val)
        rank = {}
        for e in self.prog:
            rank[e] = {seq: i + 1 for i, seq in enumerate(sorted(signal[e]))}
        sems = {}
        for e in self.prog:
            sems[("c", e)] = stack.enter_context(nc.semaphore("s_" + e))
        for n, k in enumerate(sorted(self.dma_count, key=str)):
            sems[("d", k)] = stack.enter_context(nc.semaphore("d%d" % n))
        import os as _os2
        if _os2.environ.get("KDEBUG"):
            print("SIGNALS", {e: len(rank[e]) for e in rank}, "NINST", {e: len(self.prog[e]) for e in self.prog},
                  "DMA", {str(k): v for k, v in self.dma_count.items()})
        block = stack.enter_context(nc.Block())
        engobj = {"pe": "tensor", "act": "scalar", "dve": "vector", "pool": "gpsimd", "sp": "sync"}

        def run(e, eng):
            waited = {}
            for ins in self.prog[e]:
                for (kind, key), val in sorted(ins["deps"].items(), key=str):
                    if kind == "c":
                        if key == e and e in ("pe", "sp"):
                            continue
                        v = rank[key][val]
                    else:
                        v = val
                    if waited.get((kind, key), 0) >= v:
                        continue
                    waited[(kind, key)] = v
                    eng.wait_ge(sems[(kind, key)], v)
                if ins["fn"] is None:
                    continue
                bi = ins["fn"](eng)
                if ins["dma"] is not None:
                    bi.then_inc(sems[("d", ins["dma"])], 16)
                elif ins["ticket"][2] in rank[e]:
                    bi.then_inc(sems[("c", e)], 1)

        for e in self.prog:
            if self.prog[e]:
                getattr(block, engobj[e])(lambda eng, e=e: run(e, eng))


def build_program():
    nc = bass.Bass("TRN2", target_bir_lowering=False)

    def D(name, shape, kind="ExternalInput"):
        return nc.dram_tensor(name, list(shape), F32, kind=kind).ap()

    x_d = D("x", [T, 1024])
    cvec_d = D("cvec", [128, 8])
    cak_d = D("cak", [2, 256, 128])
    cav_d = D("cav", [2, 256, 128])
    cbk_d = D("cbk", [2, 256, 512])
    cbv_d = D("cbv", [2, 256, 512])
    wada_d = D("w_ada", [2, 1024, 6144])
    bada_d = D("b_adaT", [2, 128, 48])
    npre_d = D("npreT", [128, 32])
    npost_d = D("npostT", [128, 32])
    winx_d = D("w_inx", [2, 1024, WIN_COLS])
    sink_d = D("sinkb", [128, 16])
    ttb_d = D("ttb", [2, 2, 128, 4 * 1152])
    tri_d = D("tri", [128, 1024])
    valid_d = D("valid", [128, 96])
    rope_d = D("rope", [4, 128, T])
    dftc_d = nc.dram_tensor("dft_ct", [T, T], BF16, kind="ExternalInput").ap()
    dfts_d = nc.dram_tensor("dft_nst", [T, T], BF16, kind="ExternalInput").ap()
    cs_d = D("dft_cs", [128, 256])
    wbr_d = D("w_branch", [2, 3, 512, 1024])
    wout_d = D("w_out", [2, 1024, 1024])
    wfi_d = D("w_ffn_in", [2, 1024, 5632])
    wfo_d = D("w_ffn_out", [2, 2816, 1024])
    ident_d = D("ident", [128, 128])
    y_d = D("y", [T, 1024], kind="ExternalOutput")
    kv_d = D("kvout", [2, T, 1280], kind="ExternalOutput")

    st = ExitStack()
    with st:
        NB = 206 * 1024
        arena = st.enter_context(nc.sbuf_tensor("arena", [128, NB // 2], BF16))
        psum_all = st.enter_context(nc.psum_tensor("psall", [128, 8, 512], F32))
        psum = [psum_all[:, i, :] for i in range(8)]
        fw = FW(nc)

        def view(off, shape, dt):
            n = 1
            for s in shape[1:]:
                n *= s
            assert off % 4 == 0
            if dt == BF16:
                ap = arena[:, off // 2: off // 2 + n]
                nbytes = n * 2
            else:
                ap = arena[:, off // 2: off // 2 + 2 * n].bitcast(F32)
                nbytes = n * 4
            assert off + nbytes <= NB, (off, nbytes)
            if len(shape) == 3:
                ap = ap.rearrange("p (a b) -> p a b", a=shape[1])
            elif len(shape) == 4:
                ap = ap.rearrange("p (a b c) -> p a b c", a=shape[1], b=shape[2])
            return ap

        class Alloc:
            def __init__(self, base, limit):
                self.off = base
                self.limit = limit

            def __call__(self, shape, dt):
                n = 1
                for s in shape[1:]:
                    n *= s
                nb = n * (2 if dt == BF16 else 4)
                nb = (nb + 31) // 32 * 32
                v = view(self.off, shape, dt)
                self.off += nb
                assert self.off <= self.limit, (self.off, self.limit)
                return v

        fx = Alloc(0, 106 * 1024)
        xT = fx([128, 8, T], F32)
        wring = [fx([128, SLOT], BF16) for _ in range(NS)]
        tmp = [fx([128, T], F32) for _ in range(2)]
        rstd = fx([128, T], F32)
        identF = fx([128, 128], F32)
        identB = fx([128, 128], BF16)
        onesB = fx([128, 128], BF16)
        validT = fx([128, 96], F32)
        sinkE = fx([128, 16], F32)
        epsT = fx([128, 1], F32)
        cvecT = fx([128, 8], F32)
        sB = fx([128, 8], BF16)
        badaT = fx([128, 2, 48], F32)
        modc = fx([128, 2, 48], F32)
        npreT = fx([128, 32], F32)
        npostT = fx([128, 32], F32)
        dv = fx([128, 2, 6, 8], F32)
        csB = fx([128, 256], BF16)
        stage = [fx([128, 1280], F32) for _ in range(2)]
        xtok = [fx([128, 1024], F32) for _ in range(2)]
        nrm = fx([128, 2, 16], F32)
        ABASE = fx.off
        assert ABASE <= 106 * 1024

        hT = view(ABASE, [128, 8, T], BF16)
        OT0 = ABASE + 16384
        OAT = view(OT0, [128, 4, T], BF16)
        OBT = view(OT0 + 8192, [128, 4, T], BF16)
        OCT = view(OT0 + 16384, [128, 4, T], BF16)
        S0 = OT0 + 24576
        SLIM = NB
        assert SLIM - S0 >= 58 * 1024, (SLIM - S0)
        sq = view(S0, [128, 8, T], BF16)
        mT = view(S0, [128, 8, T], BF16)
        yT = view(S0 + 16384, [128, 8, T], F32)
        c_al = Alloc(S0, SLIM)
        UCT = c_al([128, 4, T], BF16)
        ABtok = c_al([128, 8, 4, 256], BF16)
        a_al = Alloc(S0, SLIM)
        QAT = a_al([128, 4, T], BF16)
        KATp = [a_al([128, T], BF16) for _ in range(2)]
        VA = a_al([128, 8, 2, 80], BF16)
        ropeT = a_al([128, 4, T], F32)
        triB = a_al([128, 2, 512], BF16)
        KcATp = [a_al([128, 256], BF16) for _ in range(2)]
        VcA = a_al([128, 2, 2, 80], BF16)
        PTa = [a_al([128, 5, 512], BF16) for _ in range(2)]
        OtokA = [a_al([128, 512], BF16) for _ in range(2)]
        ctmpA = a_al([128, 2, 128], F32)
        b_al = Alloc(S0, SLIM)
        QBT = b_al([128, 4, T], BF16)
        KBT = b_al([128, 4, T], BF16)
        VB = b_al([128, 8, 8, 80], BF16)
        TTB = b_al([128, 4, 1152], BF16)
        KcBT = b_al([128, 4, 256], BF16)
        VcB = b_al([128, 2, 8, 80], BF16)
        PTb = [b_al([128, 7, 512], BF16) for _ in range(2)]
        OtokB = [b_al([128, 256], BF16) for _ in range(2)]
        ctmpB = b_al([128, 2, 512], F32)
        aT = view(OT0, [128, 22, T], BF16)
        y2T = view(OT0 + 45056, [128, 8, T], F32)
        assert OT0 + 45056 + 32768 <= NB
        sq2 = view(ABASE, [128, 8, T], BF16)

        MIX_C = ["UCT", "ABtok"]
        MIX_A = ["QAT", "KAT", "VA", "rope", "tri", "KcAT", "VcA", "PTa", "OtokA", "ctmpA", "QATall", "VA1", "VcA1", "KATz", "KcATz"]
        MIX_B = ["QBT", "KBT", "VB", "TTB", "KcBT", "VcB", "PTb", "OtokB", "ctmpB", "QKBall", "VB1", "VcB1"]
        SQ = ["mT"]
        WOUT = ["mT", "yT"]
        FFN = ["aT", "y2T"]
        OTS = ["OAT", "OBT", "OCT"]

        cnt = {"bank": 0, "w": 0, "obank": 0}

        def bank():
            b = cnt["bank"] % 6
            cnt["bank"] += 1
            return psum[b], ("ps", b)

        def bank_pair():
            if cnt["bank"] % 2:
                cnt["bank"] += 1
            b = cnt["bank"] % 6
            cnt["bank"] += 2
            return b

        def obank():
            b = 6 + cnt["obank"] % 2
            cnt["obank"] += 1
            return psum[b], ("ps", b)

        def wload(src, shape):
            s = cnt["w"] % NS
            cnt["w"] += 1
            n = 1
            for k in shape[1:]:
                n *= k
            assert n <= SLOT
            dst = wring[s][:, 0:n]
            if len(shape) == 3:
                dst = dst.rearrange("p (a b) -> p a b", a=shape[1])
            else:
                dst = dst.rearrange("p (a b c) -> p a b c", a=shape[1], b=shape[2])
            if len(shape) == 3:
                fw.dma("pool", ("w", s), lambda e, dst=dst, src=src: e.dma_start(out=dst, in_=src),
                       writes=[("wslot", s)])
            else:
                for q in range(shape[2]):
                    fw.dma("pool", ("w", s), lambda e, dst=dst, src=src, q=q: e.dma_start(out=dst[:, :, q, :], in_=src[:, :, q, :]),
                           writes=[("wslot", s)])
            return dst, ("wslot", s)

        def mm(out, lhsT, rhs, start, stop, reads, writes):
            fw.op("pe", lambda e: e.matmul(out, lhsT=lhsT, rhs=rhs, start=start, stop=stop), reads, writes)

        def tr(out, in_, reads, writes):
            fw.op("pe", lambda e: e.transpose(out, in_, identF[:]), list(reads) + ["identF"], writes)

        def act(out, in_, func, reads, writes, bias=None, scale=None):
            kw = {}
            if bias is not None:
                kw["bias"] = bias
            if scale is not None:
                kw["scale"] = scale
            fw.op("act", lambda e: e.activation(out=out, in_=in_, func=func, **kw), reads, writes)

        def dve(fn, reads, writes):
            fw.op("dve", fn, reads, writes)

        def sp_load(key, out, in_, writes):
            fw.dma("sp", key, lambda e: e.dma_start(out=out, in_=in_), writes=writes)

        sp_load("c0", identF, ident_d, ["identF"])
        sp_load("c1", validT, valid_d, ["valid"])
        sp_load("c2", sinkE, sink_d, ["sinkE"])
        sp_load("c3", cvecT, cvec_d, ["cvec"])
        sp_load("c4", badaT, bada_d.rearrange("l p j -> p l j"), ["bada"])
        sp_load("c5", npreT, npre_d, ["npre"])
        sp_load("c6", npostT, npost_d, ["npost"])
        fw.dma("pool", "c7", lambda e: e.dma_start(out=csB, in_=cs_d), writes=["csB"])
        dve(lambda e: e.tensor_copy(out=identB, in_=identF), ["identF"], ["identB"])
        dve(lambda e: e.memset(onesB, 1.0), [], ["onesB"])
        dve(lambda e: e.memset(epsT, 1e-6), [], ["eps"])
        act(sinkE, sinkE, AF.Exp, ["sinkE"], ["sinkE"])
        act(sB, cvecT, AF.Silu, ["cvec"], ["sB"])

        for b in range(8):
            s = b % 2
            sp_load(("xin", s), xtok[s], x_d[b * 128:(b + 1) * 128, :], [("xtok", s)])
            for g in range(2):
                pb, pk = bank()
                for j in range(4):
                    c = g * 4 + j
                    tr(pb[:, j * 128:(j + 1) * 128], xtok[s][:, c * 128:(c + 1) * 128], [("xtok", s)], [pk])
                dve(lambda e, pb=pb, g=g, b=b: e.tensor_copy(out=xT[:, g * 4:(g + 1) * 4, b * 128:(b + 1) * 128],
                                                             in_=pb[:].rearrange("p (j t) -> p j t", j=4)),
                    [pk], [("xT", c) for c in range(g * 4, g * 4 + 4)])

        DVSLOT = {0: 1, 1: 0, 2: 2, 3: 4, 4: 3, 5: 5}

        def ada_seg(l, i):
            wva = wada_d[l].rearrange("(kc p) n -> p kc n", p=128)
            for cb in (2 * i, 2 * i + 1):
                ws, wk = wload(wva[:, :, cb * 512:(cb + 1) * 512], [128, 8, 512])
                pb, pk = bank()
                for jj in range(4):
                    for kc in range(8):
                        mm(pb[:, jj:jj + 1], ws[:, kc, jj * 128:(jj + 1) * 128], sB[:, kc:kc + 1], kc == 0, kc == 7,
                           [wk, "sB"], [pk])
                dve(lambda e, pb=pb, l=l, cb=cb: e.tensor_tensor(out=modc[:, l, cb * 4:(cb + 1) * 4], in0=pb[:, 0:4],
                                                                in1=badaT[:, l, cb * 4:(cb + 1) * 4], op=ALU.add),
                    [pk, "bada"], [("modc", l, cb)])
            mv = modc[:, l, i * 8:(i + 1) * 8]
            slot = DVSLOT[i]
            rk = [("modc", l, 2 * i), ("modc", l, 2 * i + 1), "npre", "npost"]
            wk2 = [("dv", l, slot)]
            if i in (1, 4):
                npv = npreT[:, l * 16 + (0 if i == 1 else 8): l * 16 + (8 if i == 1 else 16)]
                dve(lambda e: e.tensor_scalar_add(out=dv[:, l, slot, :], in0=mv, scalar1=1.0), rk, wk2)
                dve(lambda e: e.tensor_tensor(out=dv[:, l, slot, :], in0=dv[:, l, slot, :], in1=npv, op=ALU.mult), rk + wk2, wk2)
            elif i in (0, 3):
                dve(lambda e: e.tensor_copy(out=dv[:, l, slot, :], in_=mv), rk, wk2)
            else:
                nqv = npostT[:, l * 16 + (0 if i == 2 else 8): l * 16 + (8 if i == 2 else 16)]
                dve(lambda e: e.tensor_tensor(out=dv[:, l, slot, :], in0=mv, in1=nqv, op=ALU.mult), rk, wk2)

        ada_seg(0, 0)
        ada_seg(0, 1)

        def rms_stats(src, srcname, sqv, sqname, presquared=False):
            for c in range(8):
                if not presquared:
                    act(sqv[:, c, :], src[:, c, :], AF.Square, [(srcname, c)], [(sqname, c)])
            for hf in range(2):
                pb, pk = bank()
                for c in range(8):
                    mm(pb[:], onesB[:], sqv[:, c, hf * 512:(hf + 1) * 512], c == 0, c == 7, [(sqname, c), "onesB"], [pk])
                act(rstd[:, hf * 512:(hf + 1) * 512], pb[:], AF.Sqrt, [pk, "eps"], [("rstd", hf)],
                    bias=epsT[:, 0:1], scale=1.0 / 1024.0)
                dve(lambda e, hf=hf: e.reciprocal(out=rstd[:, hf * 512:(hf + 1) * 512], in_=rstd[:, hf * 512:(hf + 1) * 512]),
                    [("rstd", hf)], [("rstd", hf)])

        def mod_norm(l, si, bi_, dst, dstname):
            for c in range(8):
                t = tmp[c % 2]
                dve(lambda e, t=t, c=c: e.scalar_tensor_tensor(out=t, in0=xT[:, c, :], scalar=dv[:, l, si, c:c + 1], in1=rstd,
                                                              op0=ALU.mult, op1=ALU.mult),
                    [("xT", c), ("rstd", 0), ("rstd", 1), ("dv", l, si)], [("tmp", c % 2)])
                act(dst[:, c, :], t, AF.Identity, [("tmp", c % 2), ("dv", l, bi_)], [(dstname, c)], bias=dv[:, l, bi_, c:c + 1], scale=1.0)

        def post_resid(l, gi, ysrc, yname):
            for c in range(8):
                t = tmp[c % 2]
                dve(lambda e, t=t, c=c: e.scalar_tensor_tensor(out=t, in0=ysrc[:, c, :], scalar=dv[:, l, gi, c:c + 1], in1=rstd,
                                                              op0=ALU.mult, op1=ALU.mult),
                    [(yname, c), ("rstd", 0), ("rstd", 1), ("dv", l, gi)], [("tmp", c % 2)])
                dve(lambda e, t=t, c=c: e.tensor_tensor(out=xT[:, c, :], in0=xT[:, c, :], in1=t, op=ALU.add),
                    [("tmp", c % 2), ("xT", c)], [("xT", c)])

        def fm_group(ws, wk, col0, kcs, rhs_of, hf):
            pb, pk = bank()
            for kc in range(kcs):
                rap, rkey = rhs_of(kc, hf)
                mm(pb[:], ws[:, kc, col0:col0 + 128], rap, kc == 0, kc == kcs - 1, [wk, rkey], [pk])
            return pb, pk

        def h_rhs(kc, hf):
            return hT[:, kc, hf * 512:(hf + 1) * 512], ("hT", kc)

        evq = {"n": 0}

        def copy_evac(out, in_, reads, writes, scale=None):
            evq["n"] += 1
            if scale is not None or evq["n"] % 2 == 0:
                act(out, in_, AF.Copy, reads, writes, scale=scale)
            else:
                dve(lambda e: e.tensor_copy(out=out, in_=in_), reads, writes)

        def attn_scores(kblocks, q_rhs, n_q_mm, PT, PTname):
            pi = cnt.setdefault(PTname, 0) % 2
            cnt[PTname] = pi + 1
            pt = PT[pi]
            for kbi, kb in enumerate(kblocks):
                hasb = kb["bias"] is not None
                vb = validT[:, kb["valid"]:kb["valid"] + 1]
                if n_q_mm == 1:
                    pb, pk = bank()
                    kap, kkey = kb["kT"](0)
                    qap, qkey = q_rhs(0)
                    out = pb[:].rearrange("p (c q) -> p c q", c=4)
                    mm(out, kap, qap, True, not hasb, [kkey, qkey], [pk])
                    if hasb:
                        bap, bkey = kb["bias"](0)
                        mm(out, identB[:], bap, False, True, [bkey, "identB"], [pk])
                    act(pt[:, kbi, :], pb[:], AF.Exp, [pk, "valid"], [(PTname, pi, kbi)], bias=vb, scale=1.0)
                else:
                    b0 = bank_pair()
                    pbs = [(psum[b0], ("ps", b0)), (psum[b0 + 1], ("ps", b0 + 1))]
                    for c in range(4):
                        pb, pk = pbs[c % 2]
                        kap, kkey = kb["kT"](c)
                        qap, qkey = q_rhs(c)
                        out = pb[:, (c // 2) * 128:(c // 2 + 1) * 128]
                        mm(out, kap, qap, True, True, [kkey, qkey], [pk])
                    act(pt[:, kbi, :].rearrange("p (b x) -> p b x", b=2), psum_all[:, b0:b0 + 2, 0:256], AF.Exp,
                        [pbs[0][1], pbs[1][1], "valid"], [(PTname, pi, kbi, 0), (PTname, pi, kbi, 1)], bias=vb, scale=1.0)
                    if hasb:
                        eap, ekey = kb["bias"](0)
                        dve(lambda e, eap=eap, kbi=kbi: e.tensor_tensor(out=pt[:, kbi, :].rearrange("p (c q) -> p c q", c=4),
                                                                       in0=pt[:, kbi, :].rearrange("p (c q) -> p c q", c=4), in1=eap, op=ALU.mult),
                            [ekey], [(PTname, pi, kbi, 0), (PTname, pi, kbi, 1)])
                for (ks, qs) in kb.get("zero", ()):
                    dve(lambda e, ks=ks, qs=qs, kbi=kbi: e.memset(
                        pt[ks * 64:(ks + 1) * 64, kbi, :].rearrange("p (c q) -> p c q", c=4)[:, :, qs * 64:(qs + 1) * 64], 0.0),
                        [], [(PTname, pi, kbi, 0), (PTname, pi, kbi, 1)])
            return (pt, pi)

        def attn_pv(kblocks, n_q_mm, h, PTname, otk, otkkey, ocols0, sink_col0):
            pt, pi = h
            nkb = len(kblocks)
            ob, ok = obank()
            for c in range(4):
                pos = c if n_q_mm == 1 else (c % 2) * 2 + c // 2
                for kbi, kb in enumerate(kblocks):
                    vap, vkey = kb["V"](c)
                    ptk = [(PTname, pi, kbi)] if n_q_mm == 1 else [(PTname, pi, kbi, c % 2)]
                    mm(ob[:, c * 80:c * 80 + 65], pt[:, kbi, pos * 128:(pos + 1) * 128], vap, kbi == 0, kbi == nkb - 1,
                       ptk + [vkey], [ok])
            ov = ob[:, 0:320].rearrange("p (c e) -> p c e", c=4)
            ns = cnt.setdefault("nrm", 0) % 2
            cnt["nrm"] = ns + 1
            den = nrm[:, ns, 0:4]
            rec = nrm[:, ns, 8:12]
            nk = ("nrm", ns)
            if sink_col0 is not None:
                dve(lambda e: e.tensor_tensor(out=den, in0=ov[:, :, 64], in1=sinkE[:, sink_col0:sink_col0 + 4], op=ALU.add),
                    [ok, "sinkE"], [nk])
                dve(lambda e: e.reciprocal(out=rec, in_=den), [nk], [nk])
            else:
                dve(lambda e: e.reciprocal(out=rec, in_=ov[:, :, 64]), [ok], [nk])
            for c in range(4):
                dve(lambda e, c=c: e.tensor_scalar(out=otk[:, ocols0 + c * 64: ocols0 + (c + 1) * 64], in0=ov[:, c, 0:64],
                                                   scalar1=rec[:, c:c + 1], scalar2=None, op0=ALU.mult),
                    [ok, nk], [otkkey])

        def otok_to_fm(otk, otkkey, ncols, OT, OTname, ch0, i):
            nch = ncols // 128
            pb, pk = bank()
            pbb = pb[:].bitcast(BF16)
            for ch in range(nch):
                fw.op("pe", lambda e, ch=ch: e.transpose(pbb[:, ch * 128:(ch + 1) * 128], otk[:, ch * 128:(ch + 1) * 128], identB[:]),
                      [otkkey, "identB"], [pk])
            copy_evac(OT[:, ch0:ch0 + nch, i * 128:(i + 1) * 128],
                      pbb[:, 0:nch * 128].rearrange("p (c q) -> p c q", c=nch), [pk],
                      [(OTname, ch0 + ch) for ch in range(nch)])

        def run_pipeline(blocks, hooks=None):
            prev = None

            def finish(p):
                b, h = p
                attn_pv(b["kbl"], b["nq"], h, b["PTname"], b["otk"], b["otkkey"], b["ocols0"], b["sink"])
                if b["post"] is not None:
                    b["post"]()
            for bi_, b in enumerate(blocks):
                h = attn_scores(b["kbl"], b["q_rhs"], b["nq"], b["PT"], b["PTname"])
                if prev is not None:
                    finish(prev)
                prev = (b, h)
                if hooks and bi_ in hooks:
                    hooks[bi_]()
            finish(prev)

        import os as _os
        KSTOP = int(_os.environ.get("KSTOP", "99"))
        for l in range(2):
          for _stage in range(1):
            wv = winx_d[l].rearrange("(kc p) n -> p kc n", p=128)
            if l * 6 + 1 > KSTOP:
                break
            PH = fw.__dict__.setdefault("phases", [])
            PH.append(("L%d norm1" % l, len(fw.prog["pe"])))

            fw.fence(FFN + ["hT"], SQ + ["hT"])
            rms_stats(xT, "xT", sq, "mT")
            mod_norm(l, 0, 1, hT, "hT")

            if l * 6 + 2 > KSTOP:
                break
            PH.append(("L%d C" % l, len(fw.prog["pe"])))
            fw.fence(SQ + WOUT + FFN, MIX_C)
            ws, wk = wload(wv[:, :, 0:512], [128, 8, 512])
            for j in range(4):
                for hf in range(2):
                    pb, pk = fm_group(ws, wk, j * 128, 8, h_rhs, hf)
                    copy_evac(UCT[:, j, hf * 512:(hf + 1) * 512], pb[:], [pk], [("UCT", j)])
            for tb in range(8):
                for gp in range(2):
                    pb, pk = bank()
                    for gg in range(2):
                        g = gp * 2 + gg
                        mm(pb[:, gg * 256:(gg + 1) * 256], UCT[:, g, tb * 128:(tb + 1) * 128], csB[:], True, True,
                           [("UCT", g), "csB"], [pk])
                    copy_evac(ABtok[:, tb, gp * 2:gp * 2 + 2, :], pb[:].rearrange("p (g e) -> p g e", g=2), [pk], [("ABtok", tb)])
            fw.fence(FFN, OTS)
            for hf in range(2):
                cv = dftc_d.rearrange("(tb p) n -> p tb n", p=128)[:, :, hf * 512:(hf + 1) * 512]
                sv = dfts_d.rearrange("(tb p) n -> p tb n", p=128)[:, :, hf * 512:(hf + 1) * 512]
                wc, wck = wload(cv, [128, 8, 512])
                wsn, wsk = wload(sv, [128, 8, 512])
                for g in range(4):
                    pb, pk = bank()
                    for tb in range(8):
                        mm(pb[:], ABtok[:, tb, g, 0:128], wc[:, tb, :], tb == 0, False, [("ABtok", tb), wck], [pk])
                        mm(pb[:], ABtok[:, tb, g, 128:256], wsn[:, tb, :], False, tb == 7, [("ABtok", tb), wsk], [pk])
                    copy_evac(OCT[:, g, hf * 512:(hf + 1) * 512], pb[:], [pk], [("OCT", g)])

            if l * 6 + 3 > KSTOP:
                break
            PH.append(("L%d A" % l, len(fw.prog["pe"])))
            fw.fence(MIX_C, MIX_A)
            sp_load("ropeld", ropeT, rope_d.rearrange("k p t -> p k t"), ["rope"])
            fw.dma("pool", "trild", lambda e: e.dma_start(out=triB, in_=tri_d.rearrange("p (a b) -> p a b", a=2)), writes=["tri"])
            for tb in range(2):
                s = tb % 2
                sp_load(("xin", s), xtok[s][:, 0:128], cak_d[l, tb * 128:(tb + 1) * 128, :], [("xtok", s)])
                pb, pk = bank()
                tr(pb[:, 0:128], xtok[s][:, 0:128], [("xtok", s)], [pk])
                for g in range(2):
                    copy_evac(KcATp[g][g * 64:(g + 1) * 64, tb * 128:(tb + 1) * 128], pb[g * 64:(g + 1) * 64, 0:128], [pk], [("KcAT", g, tb)])
            for tb in range(2):
                fw.dma("pool", "vcald", lambda e, l=l, tb=tb: e.dma_start(out=VcA[:, tb, :, 0:64],
                                                                        in_=cav_d[l, tb * 128:(tb + 1) * 128, :].rearrange("p (h d) -> p h d", h=2)),
                       writes=["VcA"])
            dve(lambda e: e.memset(VcA[:, :, :, 64:65], 1.0), [], ["VcA1"])
            for g in range(2):
                og = 1 - g
                dve(lambda e, g=g, og=og: e.memset(KATp[g][og * 64:(og + 1) * 64, :], 0.0), [], [("KATz", g)])
                dve(lambda e, g=g, og=og: e.memset(KcATp[g][og * 64:(og + 1) * 64, :], 0.0), [], [("KcATz", g)])
            dve(lambda e: e.memset(VA[:, :, :, 64:65], 1.0), [], ["VA1"])
            KSUB = int(_os.environ.get("KSUB", "99"))
            if KSUB < 1:
                break
            for u in range(2):
                ws, wk = wload(wv[:, :, 512 + u * 512: 1024 + u * 512], [128, 8, 512])
                for cc in range(2):
                    c = u * 2 + cc
                    for hf in range(2):
                        p1, k1 = fm_group(ws, wk, cc * 256, 8, h_rhs, hf)
                        p2, k2 = fm_group(ws, wk, cc * 256 + 128, 8, h_rhs, hf)
                        sl = slice(hf * 512, (hf + 1) * 512)
                        dve(lambda e, p1=p1, sl=sl: e.tensor_tensor(out=tmp[0][:, 0:512], in0=p1[:], in1=ropeT[:, 0, sl], op=ALU.mult),
                            [k1, "rope"], [("tmp", 0)])
                        dve(lambda e, p2=p2, sl=sl: e.tensor_tensor(out=tmp[1][:, 0:512], in0=p2[:], in1=ropeT[:, 1, sl], op=ALU.mult),
                            [k2, "rope"], [("tmp", 1)])
                        dve(lambda e, c=c, sl=sl: e.tensor_tensor(out=QAT[:, c, sl], in0=tmp[0][:, 0:512], in1=tmp[1][:, 0:512], op=ALU.add),
                            [("tmp", 0), ("tmp", 1)], [("QAT", c)])
            if KSUB < 2:
                break
            ws, wk = wload(wv[:, :, 1536:2048], [128, 8, 512])
            for hf in range(2):
                p1, k1 = fm_group(ws, wk, 0, 8, h_rhs, hf)
                p2, k2 = fm_group(ws, wk, 128, 8, h_rhs, hf)
                sl = slice(hf * 512, (hf + 1) * 512)
                dve(lambda e, p1=p1, sl=sl: e.tensor_tensor(out=tmp[0][:, 0:512], in0=p1[:], in1=ropeT[:, 2, sl], op=ALU.mult),
                    [k1, "rope"], [("tmp", 0)])
                dve(lambda e, p2=p2, sl=sl: e.tensor_tensor(out=tmp[1][:, 0:512], in0=p2[:], in1=ropeT[:, 3, sl], op=ALU.mult),
                    [k2, "rope"], [("tmp", 1)])
                for g in range(2):
                    dve(lambda e, sl=sl, g=g: e.tensor_tensor(out=KATp[g][g * 64:(g + 1) * 64, sl], in0=tmp[0][g * 64:(g + 1) * 64, 0:512],
                                                             in1=tmp[1][g * 64:(g + 1) * 64, 0:512], op=ALU.add),
                        [("tmp", 0), ("tmp", 1)], [("KAT", g, hf)])
            KV = int(_os.environ.get("KVAR", "7"))
            for tb in range(8 if KV & 8 == 0 else 0):
                pb, pk = bank()
                for kc in range(8):
                    mm(pb[:, 0:256], hT[:, kc, tb * 128:(tb + 1) * 128], ws[:, kc, 256:512], kc == 0, kc == 7, [wk, ("hT", kc)], [pk])
                s = tb % 2
                if KV & 1:
                    act(stage[s][:, 0:256], pb[:, 0:256], AF.Copy, [pk], [("stageA", s)])
                if KV & 2:
                    dve(lambda e, pb=pb, tb=tb: e.tensor_copy(out=VA[:, tb, :, 0:64], in_=pb[:, 128:256].rearrange("p (h d) -> p h d", h=2)),
                        [pk], [("VA", tb)])
                if KV & 4:
                    fw.dma("sp", ("kvoA", s), lambda e, s=s, tb=tb, l=l: e.dma_start(out=kv_d[l, tb * 128:(tb + 1) * 128, 0:256], in_=stage[s][:, 0:256]),
                           reads=[("stageA", s)], writes=[("kvo", l, tb, 0)])
            if KSUB < 3:
                break
            PH.append(("L%d A-attn" % l, len(fw.prog["pe"])))
            dve(lambda e: e.memset(ctmpA[:, 0, 0:1], 0.0),
                [("QAT", c) for c in range(4)] + ["VA1", "VcA1"] + [("KATz", g) for g in range(2)] + [("KcATz", g) for g in range(2)]
                + [("KAT", g, hf) for g in range(2) for hf in range(2)] + [("KcAT", g, tb) for g in range(2) for tb in range(2)],
                [("QATall",)])
            blocksA = []
            for i in range(8 if KSUB > 3 else 1):
                for g in range(2):
                    def mk_local(kbk, tri_idx, vcol, g=g):
                        return {"kT": (lambda c, kbk=kbk, g=g: (KATp[g][:, kbk * 128:(kbk + 1) * 128], ("QATall",))),
                                "bias": None if tri_idx is None else (lambda c, tri_idx=tri_idx: (triB[:, tri_idx, :].rearrange("p (c q) -> p c q", c=4), "tri")),
                                "valid": vcol,
                                "V": (lambda c, kbk=kbk, g=g: (VA[:, kbk, g, 0:65], ("VA", kbk)))}

                    def mk_ctx(tb, vcol, g=g):
                        return {"kT": (lambda c, tb=tb, g=g: (KcATp[g][:, tb * 128:(tb + 1) * 128], ("QATall",))),
                                "bias": None, "valid": vcol,
                                "V": (lambda c, tb=tb, g=g: (VcA[:, tb, g, 0:65], "VcA"))}
                    kbl = [mk_local(max(i - 1, 0), 0, i * 5 + 0), mk_local(i, None, i * 5 + 1), mk_local(min(i + 1, 7), 1, i * 5 + 2),
                           mk_ctx(0, i * 5 + 3), mk_ctx(1, i * 5 + 4)]
                    oi = i % 2
                    post = None
                    if g == 1:
                        post = (lambda oi=oi, i=i: otok_to_fm(OtokA[oi], ("OtokA", oi), 512, OAT, "OAT", 0, i))
                    blocksA.append({"kbl": kbl, "q_rhs": (lambda c, i=i: (QAT[:, :, i * 128:(i + 1) * 128], ("QATall",))), "nq": 1,
                                    "PT": PTa, "PTname": "PTa", "otk": OtokA[oi], "otkkey": ("OtokA", oi), "ocols0": g * 256,
                                    "sink": l * 8 + g * 4, "post": post})
            run_pipeline(blocksA, hooks={3: (lambda l=l: ada_seg(l, 2)), 9: (lambda l=l: ada_seg(l, 3))} if KSUB > 3 else None)

            if l * 6 + 4 > KSTOP:
                break
            PH.append(("L%d B" % l, len(fw.prog["pe"])))
            fw.fence(MIX_A, MIX_B)
            for tb in range(2):
                for ch in range(4):
                    s = (tb * 4 + ch) % 2
                    sp_load(("xin", s), xtok[s][:, 0:128], cbk_d[l, tb * 128:(tb + 1) * 128, ch * 128:(ch + 1) * 128], [("xtok", s)])
                    pb, pk = bank()
                    tr(pb[:, 0:128], xtok[s][:, 0:128], [("xtok", s)], [pk])
                    copy_evac(KcBT[:, ch, tb * 128:(tb + 1) * 128], pb[:, 0:128], [pk], ["KcBT"])
            for tb in range(2):
                fw.dma("pool", "vcbld", lambda e, l=l, tb=tb: e.dma_start(out=VcB[:, tb, :, 0:64],
                                                                        in_=cbv_d[l, tb * 128:(tb + 1) * 128, :].rearrange("p (h d) -> p h d", h=8)),
                       writes=["VcB"])
            dve(lambda e: e.memset(VcB[:, :, :, 64:65], 1.0), [], ["VcB1"])
            dve(lambda e: e.memset(VB[:, :, :, 64:65], 1.0), [], ["VB1"])
            ws, wk = wload(wv[:, :, 2048:2560], [128, 8, 512])
            for j in range(4):
                for hf in range(2):
                    pb, pk = fm_group(ws, wk, j * 128, 8, h_rhs, hf)
                    copy_evac(QBT[:, j, hf * 512:(hf + 1) * 512], pb[:], [pk], [("QBT", j)], scale=0.125)
            ws, wk = wload(wv[:, :, 2560:3072], [128, 8, 512])
            for j in range(4):
                for hf in range(2):
                    pb, pk = fm_group(ws, wk, j * 128, 8, h_rhs, hf)
                    copy_evac(KBT[:, j, hf * 512:(hf + 1) * 512], pb[:], [pk], [("KBT", j)])
            wsk_, wkk = wload(wv[:, :, 3072:3584], [128, 8, 512])
            wsv_, wkv = wload(wv[:, :, 3584:4096], [128, 8, 512])
            for tb in range(8):
                s = tb % 2
                pb, pk = bank()
                for kc in range(8):
                    mm(pb[:], hT[:, kc, tb * 128:(tb + 1) * 128], wsk_[:, kc, :], kc == 0, kc == 7, [wkk, ("hT", kc)], [pk])
                act(stage[s][:, 256:768], pb[:], AF.Copy, [pk], [("stageB", s)])
                pb2, pk2 = bank()
                for kc in range(8):
                    mm(pb2[:], hT[:, kc, tb * 128:(tb + 1) * 128], wsv_[:, kc, :], kc == 0, kc == 7, [wkv, ("hT", kc)], [pk2])
                act(stage[s][:, 768:1280], pb2[:], AF.Copy, [pk2], [("stageB", s)])
                dve(lambda e, pb2=pb2, tb=tb: e.tensor_copy(out=VB[:, tb, :, 0:64], in_=pb2[:].rearrange("p (h d) -> p h d", h=8)),
                    [pk2], [("VB", tb)])
                fw.dma("sp", ("kvoB", s), lambda e, s=s, tb=tb, l=l: e.dma_start(out=kv_d[l, tb * 128:(tb + 1) * 128, 256:1280], in_=stage[s][:, 256:1280]),
                       reads=[("stageB", s)], writes=[("kvo", l, tb, 1)])
            dve(lambda e: e.memset(ctmpB[:, 0, 0:1], 0.0), [("QBT", c) for c in range(4)] + [("KBT", c) for c in range(4)] + ["VB1", "VcB1", "KcBT", "VcB"],
                [("QKBall",)])
            PH.append(("L%d B-attn" % l, len(fw.prog["pe"])))
            for hg in range(2):
                fw.dma("pool", "ttbld", lambda e, l=l, hg=hg: e.dma_start(out=TTB, in_=ttb_d[l, hg].rearrange("p (h c) -> p h c", h=4)),
                       writes=["TTB"])
                for c4 in range(4):
                    act(TTB[:, c4, :], TTB[:, c4, :], AF.Exp, ["TTB"], ["TTB"])
                blocksB = []
                for i in range(8):
                    kb0 = min(max(i - 2, 0), 3)
                    kbl = []
                    for j in range(5):
                        kbk = kb0 + j
                        delta = kbk - i
                        pos0 = 8 - 2 * delta
                        d = {"kT": (lambda c, kbk=kbk, hg=hg: (KBT[(c % 2) * 64:(c % 2) * 64 + 64, hg * 2 + c // 2, kbk * 128:(kbk + 1) * 128], ("QKBall",))),
                             "bias": (lambda c, pos0=pos0: (TTB[:, :, pos0 * 64: pos0 * 64 + 128], "TTB")),
                             "valid": 40 + i * 7 + j,
                             "V": (lambda c, kbk=kbk, hg=hg: (VB[:, kbk, hg * 4 + c, 0:65], ("VB", kbk)))}
                        if 2 <= i <= 5 and delta == -2:
                            d["zero"] = [(0, 1)]
                        if 2 <= i <= 5 and delta == 2:
                            d["zero"] = [(0, 0), (1, 0), (1, 1)]
                        kbl.append(d)
                    for tb in range(2):
                        kbl.append({"kT": (lambda c, tb=tb, hg=hg: (KcBT[(c % 2) * 64:(c % 2) * 64 + 64, hg * 2 + c // 2, tb * 128:(tb + 1) * 128], ("QKBall",))),
                                    "bias": None, "valid": 40 + i * 7 + 5 + tb,
                                    "V": (lambda c, tb=tb, hg=hg: (VcB[:, tb, hg * 4 + c, 0:65], ("QKBall",)))})
                    oi = i % 2
                    blocksB.append({"kbl": kbl,
                                    "q_rhs": (lambda c, i=i, hg=hg: (QBT[(c % 2) * 64:(c % 2) * 64 + 64, hg * 2 + c // 2, i * 128:(i + 1) * 128], ("QKBall",))),
                                    "nq": 4, "PT": PTb, "PTname": "PTb", "otk": OtokB[oi], "otkkey": ("OtokB", oi), "ocols0": 0, "sink": None,
                                    "post": (lambda oi=oi, i=i, hg=hg: otok_to_fm(OtokB[oi], ("OtokB", oi), 256, OBT, "OBT", hg * 2, i))})
                if hg == 0:
                    hk = {1: (lambda l=l: ada_seg(l, 4)), 4: (lambda l=l: ada_seg(l, 5))}
                else:
                    hk = {1: (lambda: ada_seg(1, 0)), 4: (lambda: ada_seg(1, 1))} if l == 0 else None
                run_pipeline(blocksB, hooks=hk)

            if l * 6 + 5 > KSTOP:
                break
            PH.append(("L%d merge" % l, len(fw.prog["pe"])))
            fw.fence(MIX_B + MIX_A + MIX_C + SQ, WOUT)
            wbv = wbr_d[l].rearrange("b (kc p) n -> p (b kc) n", p=128)
            OTv = [OAT, OBT, OCT]
            OTn = ["OAT", "OBT", "OCT"]
            for j in range(8):
                wg, wgk = wload(wv[:, :, 4096 + j * 384: 4096 + (j + 1) * 384], [128, 8, 384])
                wb, wbk = wload(wbv[:, :, j * 128:(j + 1) * 128], [128, 12, 128])
                for hf in range(2):
                    sl = slice(hf * 512, (hf + 1) * 512)
                    for br in range(3):
                        pg, kg = fm_group(wg, wgk, br * 128, 8, h_rhs, hf)
                        sg = tmp[0][:, 0:512] if br % 2 == 0 else tmp[1][:, 0:512]
                        sgk = ("tmp", br % 2)
                        act(sg, pg[:], AF.Sigmoid, [kg], [sgk])
                        py, ky = bank()
                        for kc in range(4):
                            mm(py[:], wb[:, br * 4 + kc, :], OTv[br][:, kc, sl], kc == 0, kc == 3, [wbk, (OTn[br], kc)], [ky])
                        if br == 0:
                            dve(lambda e, py=py, sg=sg: e.tensor_tensor(out=rstd[:, 0:512], in0=py[:], in1=sg, op=ALU.mult),
                                [ky, sgk], [("rstd", 0)])
                        else:
                            dve(lambda e, py=py, sg=sg: e.tensor_tensor(out=sg, in0=py[:], in1=sg, op=ALU.mult), [ky, sgk], [sgk])
                            if br == 1:
                                dve(lambda e, sg=sg: e.tensor_tensor(out=rstd[:, 0:512], in0=rstd[:, 0:512], in1=sg, op=ALU.add),
                                    [sgk, ("rstd", 0)], [("rstd", 0)])
                            else:
                                dve(lambda e, sg=sg, j=j, sl=sl: e.tensor_tensor(out=mT[:, j, sl], in0=rstd[:, 0:512], in1=sg, op=ALU.add),
                                    [sgk, ("rstd", 0)], [("mT", j)])
            PH.append(("L%d wout" % l, len(fw.prog["pe"])))
            wov = wout_d[l].rearrange("(kc p) n -> p kc n", p=128)
            m_rhs = lambda kc, hf: (mT[:, kc, hf * 512:(hf + 1) * 512], ("mT", kc))
            for u in range(2):
                ws, wk = wload(wov[:, :, u * 512:(u + 1) * 512], [128, 8, 512])
                for jj in range(4):
                    j = u * 4 + jj
                    for hf in range(2):
                        pb, pk = fm_group(ws, wk, jj * 128, 8, m_rhs, hf)
                        sl = slice(hf * 512, (hf + 1) * 512)
                        act(sq2[:, j, sl], pb[:], AF.Square, [pk], [("hT", j)])
                        dve(lambda e, pb=pb, j=j, sl=sl: e.tensor_copy(out=yT[:, j, sl], in_=pb[:]), [pk], [("yT", j)])
            rms_stats(yT, "yT", sq2, "hT", presquared=True)
            post_resid(l, 2, yT, "yT")

            if l * 6 + 6 > KSTOP:
                break
            PH.append(("L%d ffn" % l, len(fw.prog["pe"])))
            fw.fence(WOUT + OTS + SQ, FFN)
            rms_stats(xT, "xT", sq, "mT")
            mod_norm(l, 3, 4, hT, "hT")
            wfv = wfi_d[l].rearrange("(kc p) (s n) -> p kc s n", p=128, s=2)
            for jp in range(11):
                ws, wk = wload(wfv[:, :, :, jp * 256:(jp + 1) * 256], [128, 8, 2, 256])
                for jj in range(2):
                    j = jp * 2 + jj
                    for hf in range(2):
                        sl = slice(hf * 512, (hf + 1) * 512)
                        pg, kg = bank()
                        for kc in range(8):
                            mm(pg[:], ws[:, kc, 0, jj * 128:(jj + 1) * 128], hT[:, kc, sl], kc == 0, kc == 7, [wk, ("hT", kc)], [kg])
                        pu, ku = bank()
                        for kc in range(8):
                            mm(pu[:], ws[:, kc, 1, jj * 128:(jj + 1) * 128], hT[:, kc, sl], kc == 0, kc == 7, [wk, ("hT", kc)], [ku])
                        sg = tmp[(j * 2 + hf) % 2][:, 0:512]
                        sgk = ("tmp", (j * 2 + hf) % 2)
                        act(sg, pg[:], AF.Silu, [kg], [sgk])
                        dve(lambda e, pu=pu, sg=sg, j=j, sl=sl: e.tensor_tensor(out=aT[:, j, sl], in0=pu[:], in1=sg, op=ALU.mult),
                            [ku, sgk], [("aT", j)])
            wfov = wfo_d[l].rearrange("(kc p) n -> p kc n", p=128)
            a_rhs = lambda kc, hf: (aT[:, kc, hf * 512:(hf + 1) * 512], ("aT", kc))
            for u in range(4):
                wsA, wkA = wload(wfov[:, 0:11, u * 256:(u + 1) * 256], [128, 11, 256])
                wsB, wkB = wload(wfov[:, 11:22, u * 256:(u + 1) * 256], [128, 11, 256])
                for jj in range(2):
                    j = u * 2 + jj
                    for hf in range(2):
                        pb, pk = bank()
                        for kc in range(22):
                            ws_, wk_ = (wsA, wkA) if kc < 11 else (wsB, wkB)
                            rap, rkey = a_rhs(kc, hf)
                            mm(pb[:], ws_[:, kc % 11, jj * 128:(jj + 1) * 128], rap, kc == 0, kc == 21, [wk_, rkey], [pk])
                        sl = slice(hf * 512, (hf + 1) * 512)
                        act(sq2[:, j, sl], pb[:], AF.Square, [pk], [("hT", j)])
                        dve(lambda e, pb=pb, j=j, sl=sl: e.tensor_copy(out=y2T[:, j, sl], in_=pb[:]), [pk], [("y2T", j)])
            rms_stats(y2T, "y2T", sq2, "hT", presquared=True)
            post_resid(l, 5, y2T, "y2T")

        fw.__dict__.setdefault("phases", []).append(("out", len(fw.prog["pe"])))
        for b in range(8):
            s = b % 2
            for g in range(2):
                pb, pk = bank()
                for j in range(4):
                    c = g * 4 + j
                    tr(pb[:, j * 128:(j + 1) * 128], xT[:, c, b * 128:(b + 1) * 128], [("xT", c)], [pk])
                copy_evac(xtok[s][:, g * 512:(g + 1) * 512], pb[:], [pk], [("xtok", s)])
            fw.dma("sp", ("yout", s), lambda e, b=b, s=s: e.dma_start(out=y_d[b * 128:(b + 1) * 128, :], in_=xtok[s]),
                   reads=[("xtok", s)], writes=[("yo", b)])
        fw.final_wait("sp", [("yo", b) for b in range(8)] + [k for k in [("kvo", l, tb, k) for l in range(2) for tb in range(8) for k in range(2)] if k in fw.state])
        fw.emit(st)
        nc._phases = fw.__dict__.get("phases", [])
    return nc


def _rope_tables(sample):
    out = np.zeros((4, 128, T), np.float32)
    d = np.arange(128) % 64
    if sample:
        t = np.arange(T)
        row = (t // 64).astype(np.float32)
        col = (t % 64).astype(np.float32)
        inv = (np.float32(10000.0) ** (-np.arange(16, dtype=np.float32) / np.float32(16))).astype(np.float32)
        ang = np.concatenate([row[:, None] * inv, col[:, None] * inv], axis=-1).astype(np.float32)
        cos = np.cos(ang).astype(np.float32)
        sin = np.sin(ang).astype(np.float32)
        C = cos[:, d % 32].T
        S = sin[:, d % 32].T * np.where(d < 32, -1.0, 1.0)[:, None].astype(np.float32)
    else:
        C = np.ones((128, T), np.float32)
        S = np.zeros((128, T), np.float32)
    out[0] = C * np.float32(0.125)
    out[1] = S * np.float32(0.125)
    out[2] = C
    out[3] = S
    return out


def _dft_tables(sample):
    t = np.arange(T, dtype=np.int64)
    if sample:
        n = T
        ph = (t[:, None] * t[None, :]) % n
        m = np.ones((T, T))
    else:
        n = 256
        ph = ((t[:, None] % n) * (t[None, :] % n)) % n
        m = ((t[:, None] // n) == (t[None, :] // n)).astype(np.float64)
    ang = 2.0 * np.pi * ph / n
    import ml_dtypes
    ct = (np.cos(ang) * m / np.sqrt(n)).astype(np.float32).astype(ml_dtypes.bfloat16)
    nst = (-np.sin(ang) * m / np.sqrt(n)).astype(np.float32).astype(ml_dtypes.bfloat16)
    return ct, nst


def _cs_table():
    c = np.arange(128, dtype=np.int64)
    ang = 2.0 * np.pi * ((c[:, None] * c[None, :]) % 128) / 128.0
    return np.concatenate([np.cos(ang), np.sin(ang)], axis=1).astype(np.float32) / np.float32(np.sqrt(128.0))


def _ttb_tables(b_rpb, sample):
    out = np.zeros((2, 2, 128, 4, 18, 64), np.float32)
    if not sample:
        return out.reshape(2, 2, 128, 4 * 1152)
    kcol = np.arange(64)[:, None]
    c = np.arange(64)[None, :]
    ws = np.clip(c - 8, 0, 48)
    colok = (kcol >= ws) & (kcol < ws + 16)
    dc = np.clip(kcol - c + 15, 0, 30)
    for l in range(2):
        for h in range(8):
            for ks in range(2):
                for pos in range(18):
                    a = 15 - pos + ks
                    if 0 <= a <= 14:
                        tile = np.where(colok, b_rpb[l, h, a][dc], np.float32(NEG))
                    else:
                        tile = np.full((64, 64), NEG, np.float32)
                    out[l, h // 4, ks * 64:(ks + 1) * 64, ((h % 4) % 2) * 2 + (h % 4) // 2, pos, :] = tile
    return out.reshape(2, 2, 128, 4 * 1152)


def _tri_table(sample):
    out = np.zeros((128, 2, 4, 128), np.float32)
    if sample:
        j = np.arange(128)[:, None]
        q = np.arange(128)[None, :]
        out[:, 0] = np.where(j >= q, 0.0, NEG)[:, None, :]
        out[:, 1] = np.where(j <= q, 0.0, NEG)[:, None, :]
    return out.reshape(128, 1024)


def _valid_table(sample):
    v = np.zeros((96,), np.float32)
    for i in range(8):
        if sample:
            a = [i >= 1, True, i <= 6, True, True]
        else:
            a = [i % 2 == 1, True, i % 2 == 0, False, False]
        for k in range(5):
            v[i * 5 + k] = 0.0 if a[k] else NEG
        kb0 = min(max(i - 2, 0), 3)
        for j in range(5):
            kb = kb0 + j
            if sample:
                ok = (kb <= 3) if i <= 1 else ((kb >= 4) if i >= 6 else True)
            else:
                ok = (kb // 2) == (i // 2)
            v[40 + i * 7 + j] = 0.0 if ok else NEG
        for tb in range(2):
            v[40 + i * 7 + 5 + tb] = 0.0 if sample else NEG
    return np.broadcast_to(v[None, :], (128, 96)).copy()


def _winx(w_in):
    idx = []
    idx += list(range(2304, 2816))
    rot = lambda base: [base + (d + 32) % 64 for d in range(64)]
    nat = lambda base: [base + d for d in range(64)]
    for c in range(4):
        idx += nat(c * 64) + nat((c + 4) * 64)
        idx += rot(c * 64) + rot((c + 4) * 64)
    idx += nat(512) + nat(576)
    idx += rot(512) + rot(576)
    idx += list(range(512, 640)) + list(range(640, 768))
    idx += list(range(768, 1280))
    idx += list(range(1280, 1792))
    idx += list(range(1280, 1792))
    idx += list(range(1792, 2304))
    for j in range(8):
        for br in range(3):
            idx += list(range(2816 + br * 1024 + j * 128, 2816 + br * 1024 + (j + 1) * 128))
    idx = np.asarray(idx)
    assert idx.shape[0] == WIN_COLS
    return np.ascontiguousarray(w_in[:, :, idx])


_CACHE = {}


def make_in_maps(x_prompt, x_sample, cache_a_k, cache_a_v, cache_b_k, cache_b_v, c, c_ctx,
                 w_ada, b_ada, norm_pre, norm_post, w_in, a_sink, b_rpb, w_branch, w_out,
                 w_ffn_in, w_ffn_out):
    f = lambda a: np.ascontiguousarray(np.asarray(a, dtype=np.float32))
    x_prompt, x_sample = f(x_prompt), f(x_sample)
    cache_a_k, cache_a_v, cache_b_k, cache_b_v = f(cache_a_k), f(cache_a_v), f(cache_b_k), f(cache_b_v)
    c, c_ctx = f(c), f(c_ctx)
    w_ada, b_ada, norm_pre, norm_post, w_in = f(w_ada), f(b_ada), f(norm_pre), f(norm_post), f(w_in)
    a_sink, b_rpb, w_branch, w_out, w_ffn_in, w_ffn_out = f(a_sink), f(b_rpb), f(w_branch), f(w_out), f(w_ffn_in), f(w_ffn_out)

    shared = {
        "w_ada": w_ada,
        "b_adaT": np.ascontiguousarray(b_ada.reshape(2, 48, 128).transpose(0, 2, 1)),
        "npreT": np.ascontiguousarray(norm_pre.reshape(2, 2, 8, 128).transpose(3, 0, 1, 2).reshape(128, 32)),
        "npostT": np.ascontiguousarray(norm_post.reshape(2, 2, 8, 128).transpose(3, 0, 1, 2).reshape(128, 32)),
        "w_inx": _winx(w_in),
        "sinkb": np.ascontiguousarray(np.broadcast_to(a_sink.reshape(1, 16), (128, 16))),
        "dft_cs": _cs_table(),
        "w_branch": w_branch, "w_out": w_out, "w_ffn_in": w_ffn_in, "w_ffn_out": w_ffn_out,
        "ident": np.eye(128, dtype=np.float32),
    }
    per_type = {}
    for sample in (False, True):
        ct, nst = _dft_tables(sample)
        per_type[sample] = {
            "ttb": _ttb_tables(b_rpb, sample), "tri": _tri_table(sample), "valid": _valid_table(sample),
            "rope": _rope_tables(sample), "dft_ct": ct, "dft_nst": nst,
        }
    in_maps = []
    for core in range(8):
        m = dict(shared)
        if core < 4:
            m.update(per_type[False])
            m["x"] = np.ascontiguousarray(x_prompt[core * 4:(core + 1) * 4].reshape(T, 1024))
            m["cvec"] = np.ascontiguousarray(c_ctx.reshape(8, 128).T)
            m["cak"] = np.zeros((2, 256, 128), np.float32)
            m["cav"] = np.zeros((2, 256, 128), np.float32)
            m["cbk"] = np.zeros((2, 256, 512), np.float32)
            m["cbv"] = np.zeros((2, 256, 512), np.float32)
        else:
            b = core - 4
            m.update(per_type[True])
            m["x"] = np.ascontiguousarray(x_sample[b])
            m["cvec"] = np.ascontiguousarray(c[b].reshape(8, 128).T)
            m["cak"] = np.ascontiguousarray(cache_a_k[b].reshape(2, 256, 128))
            m["cav"] = np.ascontiguousarray(cache_a_v[b].reshape(2, 256, 128))
            m["cbk"] = np.ascontiguousarray(cache_b_k[b].reshape(2, 256, 512))
            m["cbv"] = np.ascontiguousarray(cache_b_v[b].reshape(2, 256, 512))
        in_maps.append(m)

    return in_maps


def kernel(**inputs):
    in_maps = make_in_maps(**inputs)
    if "nc" not in _CACHE:
        _CACHE["nc"] = build_program()
    nc = _CACHE["nc"]
    res = run_bass_kernel_spmd(nc, in_maps, core_ids=list(range(8)))
    return assemble(res.results)


def assemble(outs):
    y_prompt = np.concatenate([outs[k]["y"].reshape(4, 256, 1024) for k in range(4)], axis=0).astype(np.float32)
    y_sample = np.stack([outs[4 + k]["y"] for k in range(4)], axis=0).astype(np.float32)
    kv = np.concatenate([outs[k]["kvout"].reshape(2, 4, 256, 1280).transpose(1, 0, 2, 3) for k in range(4)], axis=0)
    new_a_k = np.ascontiguousarray(kv[..., 0:128]).reshape(16, 2, 256, 2, 64)
    new_a_v = np.ascontiguousarray(kv[..., 128:256]).reshape(16, 2, 256, 2, 64)
    new_b_k = np.ascontiguousarray(kv[..., 256:768]).reshape(16, 2, 256, 8, 64)
    new_b_v = np.ascontiguousarray(kv[..., 768:1280]).reshape(16, 2, 256, 8, 64)
    return (y_prompt, y_sample, new_a_k, new_a_v, new_b_k, new_b_v)
```

```python
import numpy as np
from contextlib import ExitStack
import concourse.bass as bass
import concourse.mybir as mybir
from concourse.bass_utils import run_bass_kernel_spmd

F32 = mybir.dt.float32
BF16 = mybir.dt.bfloat16
AF = mybir.ActivationFunctionType
ALU = mybir.AluOpType

NEG = -1e30
T = 1024
NS = 5
SLOT = 4096
WIN_COLS = 7168


class FW:
    def __init__(self, nc):
        self.nc = nc
        self.prog = {e: [] for e in ("pe", "act", "dve", "pool", "sp")}
        self.state = {}
        self.pending = {}
        self.dma_count = {}
        self.nseq = {e: 0 for e in self.prog}

    @staticmethod
    def _name(k):
        return k[0] if isinstance(k, tuple) else k

    def _get(self, k):
        st = self.state.get(k)
        if st is None:
            nm = self._name(k)
            if nm in self.pending:
                st = {"w": None, "r": dict(self.pending[nm])}
                self.state[k] = st
        return st

    def _collect(self, reads, writes):
        deps = {}

        def add(kk, v):
            if kk not in deps or deps[kk] < v:
                deps[kk] = v
        for k in reads:
            st = self._get(k)
            if st and st["w"] is not None:
                add((st["w"][0], st["w"][1]), st["w"][2])
        for k in writes:
            st = self._get(k)
            if st:
                if st["w"] is not None:
                    add((st["w"][0], st["w"][1]), st["w"][2])
                for kk, v in st["r"].items():
                    add(kk, v)
        return deps

    def _update(self, ticket, reads, writes):
        kk = (ticket[0], ticket[1])
        for k in reads:
            st = self.state.setdefault(k, {"w": None, "r": {}})
            if st["r"].get(kk, -1) < ticket[2]:
                st["r"][kk] = ticket[2]
        for k in writes:
            self.state[k] = {"w": ticket, "r": {}}

    def fence(self, old_names, new_names):
        old_names = set(old_names)
        new_names = set(new_names)
        F = {}
        for k, st in self.state.items():
            if self._name(k) in old_names:
                if st["w"] is not None:
                    kk = (st["w"][0], st["w"][1])
                    F[kk] = max(F.get(kk, -1), st["w"][2])
                for kk, v in st["r"].items():
                    F[kk] = max(F.get(kk, -1), v)
        for k, st in self.state.items():
            if self._name(k) in new_names:
                for kk, v in F.items():
                    st["r"][kk] = max(st["r"].get(kk, -1), v)
        for nm in new_names:
            p = self.pending.setdefault(nm, {})
            for kk, v in F.items():
                p[kk] = max(p.get(kk, -1), v)

    def op(self, eng, fn, reads=(), writes=()):
        ps_r = [k for k in reads if self._name(k) == "ps"]
        if ps_r:
            reads = [k for k in reads if self._name(k) != "ps"]
            writes = list(writes) + ps_r
        deps = self._collect(reads, writes)
        seq = self.nseq[eng]
        self.nseq[eng] += 1
        ticket = ("c", eng, seq)
        self.prog[eng].append({"fn": fn, "deps": deps, "ticket": ticket, "dma": None})
        self._update(ticket, reads, writes)

    def dma(self, queue, semkey, fn, reads=(), writes=()):
        deps = self._collect(reads, writes)
        cnt = self.dma_count.get(semkey, 0) + 1
        self.dma_count[semkey] = cnt
        ticket = ("d", semkey, cnt * 16)
        seq = self.nseq[queue]
        self.nseq[queue] += 1
        self.prog[queue].append({"fn": fn, "deps": deps, "ticket": ("c", queue, seq), "dma": semkey})
        self._update(ticket, reads, writes)

    def final_wait(self, queue, keys):
        deps = self._collect(keys, ())
        self.prog[queue].append({"fn": None, "deps": deps, "ticket": None, "dma": None})

    def emit(self, stack):
        nc = self.nc
        signal = {e: set() for e in self.prog}
        for e, lst in self.prog.items():
            for ins in lst:
                for (kind, key), val in ins["deps"].items():
                    if kind == "c":
                        if key == e and e in ("pe", "sp"):
                            continue
                        signal[key].add(val)
        rank = {}
        for e in self.prog:
            rank[e] = {seq: i + 1 for i, seq in enumerate(sorted(signal[e]))}
        sems = {}
        for e in self.prog:
            sems[("c", e)] = stack.enter_context(nc.semaphore("s_" + e))
        for n, k in enumerate(sorted(self.dma_count, key=str)):
            sems[("d", k)] = stack.enter_context(nc.semaphore("d%d" % n))
        block = stack.enter_context(nc.Block())
        engobj = {"pe": "tensor", "act": "scalar", "dve": "vector", "pool": "gpsimd", "sp": "sync"}

        def run(e, eng):
            waited = {}
            for ins in self.prog[e]:
                for (kind, key), val in sorted(ins["deps"].items(), key=str):
                    if kind == "c":
                        if key == e and e in ("pe", "sp"):
                            continue
                        v = rank[key][val]
                    else:
                        v = val
                    if waited.get((kind, key), 0) >= v:
                        continue
                    waited[(kind, key)] = v
                    eng.wait_ge(sems[(kind, key)], v)
                if ins["fn"] is None:
                    continue
                bi = ins["fn"](eng)
                if ins["dma"] is not None:
                    bi.then_inc(sems[("d", ins["dma"])], 16)
                elif ins["ticket"][2] in rank[e]:
                    bi.then_inc(sems[("c", e)], 1)

        for e in self.prog:
            if self.prog[e]:
                getattr(block, engobj[e])(lambda eng, e=e: run(e, eng))


def build_program():
    nc = bass.Bass("TRN2", target_bir_lowering=False)

    def D(name, shape, kind="ExternalInput"):
        return nc.dram_tensor(name, list(shape), F32, kind=kind).ap()

    x_d = D("x", [T, 1024])
    cvec_d = D("cvec", [128, 8])
    cak_d = D("cak", [2, 256, 128])
    cav_d = D("cav", [2, 256, 128])
    cbk_d = D("cbk", [2, 256, 512])
    cbv_d = D("cbv", [2, 256, 512])
    wada_d = D("w_ada", [2, 1024, 6144])
    bada_d = D("b_adaT", [2, 128, 48])
    npre_d = D("npreT", [128, 32])
    npost_d = D("npostT", [128, 32])
    winx_d = D("w_inx", [2, 1024, WIN_COLS])
    sink_d = D("sinkb", [128, 16])
    ttb_d = D("ttb", [2, 2, 128, 4 * 1152])
    tri_d = D("tri", [128, 1024])
    valid_d = D("valid", [128, 96])
    rope_d = D("rope", [4, 128, T])
    dftc_d = nc.dram_tensor("dft_ct", [T, T], BF16, kind="ExternalInput").ap()
    dfts_d = nc.dram_tensor("dft_nst", [T, T], BF16, kind="ExternalInput").ap()
    cs_d = D("dft_cs", [128, 256])
    wbr_d = D("w_branch", [2, 3, 512, 1024])
    wout_d = D("w_out", [2, 1024, 1024])
    wfi_d = D("w_ffn_in", [2, 1024, 5632])
    wfo_d = D("w_ffn_out", [2, 2816, 1024])
    ident_d = D("ident", [128, 128])
    y_d = D("y", [T, 1024], kind="ExternalOutput")
    kv_d = D("kvout", [2, T, 1280], kind="ExternalOutput")

    st = ExitStack()
    with st:
        NB = 206 * 1024
        arena = st.enter_context(nc.sbuf_tensor("arena", [128, NB // 2], BF16))
        psum_all = st.enter_context(nc.psum_tensor("psall", [128, 8, 512], F32))
        psum = [psum_all[:, i, :] for i in range(8)]
        fw = FW(nc)

        def view(off, shape, dt):
            n = 1
            for s in shape[1:]:
                n *= s
            assert off % 4 == 0
            if dt == BF16:
                ap = arena[:, off // 2: off // 2 + n]
                nbytes = n * 2
            else:
                ap = arena[:, off // 2: off // 2 + 2 * n].bitcast(F32)
                nbytes = n * 4
            assert off + nbytes <= NB, (off, nbytes)
            if len(shape) == 3:
                ap = ap.rearrange("p (a b) -> p a b", a=shape[1])
            elif len(shape) == 4:
                ap = ap.rearrange("p (a b c) -> p a b c", a=shape[1], b=shape[2])
            return ap

        class Alloc:
            def __init__(self, base, limit):
                self.off = base
                self.limit = limit

            def __call__(self, shape, dt):
                n = 1
                for s in shape[1:]:
                    n *= s
                nb = n * (2 if dt == BF16 else 4)
                nb = (nb + 31) // 32 * 32
                v = view(self.off, shape, dt)
                self.off += nb
                assert self.off <= self.limit, (self.off, self.limit)
                return v

        fx = Alloc(0, 106 * 1024)
        xT = fx([128, 8, T], F32)
        wring = [fx([128, SLOT], BF16) for _ in range(NS)]
        tmp = [fx([128, T], F32) for _ in range(2)]
        rstd = fx([128, T], F32)
        identF = fx([128, 128], F32)
        identB = fx([128, 128], BF16)
        onesB = fx([128, 128], BF16)
        validT = fx([128, 96], F32)
        sinkE = fx([128, 16], F32)
        epsT = fx([128, 1], F32)
        cvecT = fx([128, 8], F32)
        sB = fx([128, 8], BF16)
        badaT = fx([128, 2, 48], F32)
        modc = fx([128, 2, 48], F32)
        npreT = fx([128, 32], F32)
        npostT = fx([128, 32], F32)
        dv = fx([128, 2, 6, 8], F32)
        csB = fx([128, 256], BF16)
        stage = [fx([128, 1280], F32) for _ in range(2)]
        xtok = [fx([128, 1024], F32) for _ in range(2)]
        nrm = fx([128, 2, 16], F32)
        ABASE = fx.off
        assert ABASE <= 106 * 1024

        hT = view(ABASE, [128, 8, T], BF16)
        OT0 = ABASE + 16384
        OAT = view(OT0, [128, 4, T], BF16)
        OBT = view(OT0 + 8192, [128, 4, T], BF16)
        OCT = view(OT0 + 16384, [128, 4, T], BF16)
        S0 = OT0 + 24576
        SLIM = NB
        assert SLIM - S0 >= 58 * 1024, (SLIM - S0)
        sq = view(S0, [128, 8, T], BF16)
        mT = view(S0, [128, 8, T], BF16)
        yT = view(S0 + 16384, [128, 8, T], F32)
        c_al = Alloc(S0, SLIM)
        UCT = c_al([128, 4, T], BF16)
        ABtok = c_al([128, 8, 4, 256], BF16)
        a_al = Alloc(S0, SLIM)
        QAT = a_al([128, 4, T], BF16)
        KATp = [a_al([128, T], BF16) for _ in range(2)]
        VA = a_al([128, 8, 2, 80], BF16)
        ropeT = a_al([128, 4, T], F32)
        triB = a_al([128, 2, 512], BF16)
        KcATp = [a_al([128, 256], BF16) for _ in range(2)]
        VcA = a_al([128, 2, 2, 80], BF16)
        PTa = [a_al([128, 5, 512], BF16) for _ in range(2)]
        OtokA = [a_al([128, 512], BF16) for _ in range(2)]
        ctmpA = a_al([128, 2, 128], F32)
        b_al = Alloc(S0, SLIM)
        QBT = b_al([128, 4, T], BF16)
        KBT = b_al([128, 4, T], BF16)
        VB = b_al([128, 8, 8, 80], BF16)
        TTB = b_al([128, 4, 1152], BF16)
        KcBT = b_al([128, 4, 256], BF16)
        VcB = b_al([128, 2, 8, 80], BF16)
        PTb = [b_al([128, 7, 512], BF16) for _ in range(2)]
        OtokB = [b_al([128, 256], BF16) for _ in range(2)]
        ctmpB = b_al([128, 2, 512], F32)
        aT = view(OT0, [128, 22, T], BF16)
        y2T = view(OT0 + 45056, [128, 8, T], F32)
        assert OT0 + 45056 + 32768 <= NB
        sq2 = view(ABASE, [128, 8, T], BF16)

        MIX_C = ["UCT", "ABtok"]
        MIX_A = ["QAT", "KAT", "VA", "rope", "tri", "KcAT", "VcA", "PTa", "OtokA", "ctmpA", "QATall", "VA1", "VcA1", "KATz", "KcATz"]
        MIX_B = ["QBT", "KBT", "VB", "TTB", "KcBT", "VcB", "PTb", "OtokB", "ctmpB", "QKBall", "VB1", "VcB1"]
        SQ = ["mT"]
        WOUT = ["mT", "yT"]
        FFN = ["aT", "y2T"]
        OTS = ["OAT", "OBT", "OCT"]

        cnt = {"bank": 0, "w": 0, "obank": 0}

        def bank():
            b = cnt["bank"] % 6
            cnt["bank"] += 1
            return psum[b], ("ps", b)

        def bank_pair():
            if cnt["bank"] % 2:
                cnt["bank"] += 1
            b = cnt["bank"] % 6
            cnt["bank"] += 2
            return b

        def obank():
            b = 6 + cnt["obank"] % 2
            cnt["obank"] += 1
            return psum[b], ("ps", b)

        def wload(src, shape):
            s = cnt["w"] % NS
            cnt["w"] += 1
            n = 1
            for k in shape[1:]:
                n *= k
            assert n <= SLOT
            dst = wring[s][:, 0:n]
            if len(shape) == 3:
                dst = dst.rearrange("p (a b) -> p a b", a=shape[1])
            else:
                dst = dst.rearrange("p (a b c) -> p a b c", a=shape[1], b=shape[2])
            if len(shape) == 3:
                fw.dma("pool", ("w", s), lambda e, dst=dst, src=src: e.dma_start(out=dst, in_=src),
                       writes=[("wslot", s)])
            else:
                for q in range(shape[2]):
                    fw.dma("pool", ("w", s), lambda e, dst=dst, src=src, q=q: e.dma_start(out=dst[:, :, q, :], in_=src[:, :, q, :]),
                           writes=[("wslot", s)])
            return dst, ("wslot", s)

        def mm(out, lhsT, rhs, start, stop, reads, writes):
            fw.op("pe", lambda e: e.matmul(out, lhsT=lhsT, rhs=rhs, start=start, stop=stop), reads, writes)

        def tr(out, in_, reads, writes):
            fw.op("pe", lambda e: e.transpose(out, in_, identF[:]), list(reads) + ["identF"], writes)

        def act(out, in_, func, reads, writes, bias=None, scale=None):
            kw = {}
            if bias is not None:
                kw["bias"] = bias
            if scale is not None:
                kw["scale"] = scale
            fw.op("act", lambda e: e.activation(out=out, in_=in_, func=func, **kw), reads, writes)

        def dve(fn, reads, writes):
            fw.op("dve", fn, reads, writes)

        def sp_load(key, out, in_, writes):
            fw.dma("sp", key, lambda e: e.dma_start(out=out, in_=in_), writes=writes)

        sp_load("c0", identF, ident_d, ["identF"])
        sp_load("c1", validT, valid_d, ["valid"])
        sp_load("c2", sinkE, sink_d, ["sinkE"])
        sp_load("c3", cvecT, cvec_d, ["cvec"])
        sp_load("c4", badaT, bada_d.rearrange("l p j -> p l j"), ["bada"])
        sp_load("c5", npreT, npre_d, ["npre"])
        sp_load("c6", npostT, npost_d, ["npost"])
        fw.dma("pool", "c7", lambda e: e.dma_start(out=csB, in_=cs_d), writes=["csB"])
        dve(lambda e: e.tensor_copy(out=identB, in_=identF), ["identF"], ["identB"])
        dve(lambda e: e.memset(onesB, 1.0), [], ["onesB"])
        dve(lambda e: e.memset(epsT, 1e-6), [], ["eps"])
        act(sinkE, sinkE, AF.Exp, ["sinkE"], ["sinkE"])
        act(sB, cvecT, AF.Silu, ["cvec"], ["sB"])

        for b in range(8):
            s = b % 2
            sp_load(("xin", s), xtok[s], x_d[b * 128:(b + 1) * 128, :], [("xtok", s)])
            for g in range(2):
                pb, pk = bank()
                for j in range(4):
                    c = g * 4 + j
                    tr(pb[:, j * 128:(j + 1) * 128], xtok[s][:, c * 128:(c + 1) * 128], [("xtok", s)], [pk])
                dve(lambda e, pb=pb, g=g, b=b: e.tensor_copy(out=xT[:, g * 4:(g + 1) * 4, b * 128:(b + 1) * 128],
                                                             in_=pb[:].rearrange("p (j t) -> p j t", j=4)),
                    [pk], [("xT", c) for c in range(g * 4, g * 4 + 4)])

        DVSLOT = {0: 1, 1: 0, 2: 2, 3: 4, 4: 3, 5: 5}

        def ada_seg(l, i):
            wva = wada_d[l].rearrange("(kc p) n -> p kc n", p=128)
            for cb in (2 * i, 2 * i + 1):
                ws, wk = wload(wva[:, :, cb * 512:(cb + 1) * 512], [128, 8, 512])
                pb, pk = bank()
                for jj in range(4):
                    for kc in range(8):
                        mm(pb[:, jj:jj + 1], ws[:, kc, jj * 128:(jj + 1) * 128], sB[:, kc:kc + 1], kc == 0, kc == 7,
                           [wk, "sB"], [pk])
                dve(lambda e, pb=pb, l=l, cb=cb: e.tensor_tensor(out=modc[:, l, cb * 4:(cb + 1) * 4], in0=pb[:, 0:4],
                                                                in1=badaT[:, l, cb * 4:(cb + 1) * 4], op=ALU.add),
                    [pk, "bada"], [("modc", l, cb)])
            mv = modc[:, l, i * 8:(i + 1) * 8]
            slot = DVSLOT[i]
            rk = [("modc", l, 2 * i), ("modc", l, 2 * i + 1), "npre", "npost"]
            wk2 = [("dv", l, slot)]
            if i in (1, 4):
                npv = npreT[:, l * 16 + (0 if i == 1 else 8): l * 16 + (8 if i == 1 else 16)]
                dve(lambda e: e.tensor_scalar_add(out=dv[:, l, slot, :], in0=mv, scalar1=1.0), rk, wk2)
                dve(lambda e: e.tensor_tensor(out=dv[:, l, slot, :], in0=dv[:, l, slot, :], in1=npv, op=ALU.mult), rk + wk2, wk2)
            elif i in (0, 3):
                dve(lambda e: e.tensor_copy(out=dv[:, l, slot, :], in_=mv), rk, wk2)
            else:
                nqv = npostT[:, l * 16 + (0 if i == 2 else 8): l * 16 + (8 if i == 2 else 16)]
                dve(lambda e: e.tensor_tensor(out=dv[:, l, slot, :], in0=mv, in1=nqv, op=ALU.mult), rk, wk2)

        ada_seg(0, 0)
        ada_seg(0, 1)

        def rms_stats(src, srcname, sqv, sqname, presquared=False):
            for c in range(8):
                if not presquared:
                    act(sqv[:, c, :], src[:, c, :], AF.Square, [(srcname, c)], [(sqname, c)])
            for hf in range(2):
                pb, pk = bank()
                for c in range(8):
                    mm(pb[:], onesB[:], sqv[:, c, hf * 512:(hf + 1) * 512], c == 0, c == 7, [(sqname, c), "onesB"], [pk])
                act(rstd[:, hf * 512:(hf + 1) * 512], pb[:], AF.Sqrt, [pk, "eps"], [("rstd", hf)],
                    bias=epsT[:, 0:1], scale=1.0 / 1024.0)
                dve(lambda e, hf=hf: e.reciprocal(out=rstd[:, hf * 512:(hf + 1) * 512], in_=rstd[:, hf * 512:(hf + 1) * 512]),
                    [("rstd", hf)], [("rstd", hf)])

        def mod_norm(l, si, bi_, dst, dstname):
            for c in range(8):
                t = tmp[c % 2]
                dve(lambda e, t=t, c=c: e.scalar_tensor_tensor(out=t, in0=xT[:, c, :], scalar=dv[:, l, si, c:c + 1], in1=rstd,
                                                              op0=ALU.mult, op1=ALU.mult),
                    [("xT", c), ("rstd", 0), ("rstd", 1), ("dv", l, si)], [("tmp", c % 2)])
                act(dst[:, c, :], t, AF.Identity, [("tmp", c % 2), ("dv", l, bi_)], [(dstname, c)], bias=dv[:, l, bi_, c:c + 1], scale=1.0)

        def post_resid(l, gi, ysrc, yname):
            for c in range(8):
                t = tmp[c % 2]
                dve(lambda e, t=t, c=c: e.scalar_tensor_tensor(out=t, in0=ysrc[:, c, :], scalar=dv[:, l, gi, c:c + 1], in1=rstd,
                                                              op0=ALU.mult, op1=ALU.mult),
                    [(yname, c), ("rstd", 0), ("rstd", 1), ("dv", l, gi)], [("tmp", c % 2)])
                dve(lambda e, t=t, c=c: e.tensor_tensor(out=xT[:, c, :], in0=xT[:, c, :], in1=t, op=ALU.add),
                    [("tmp", c % 2), ("xT", c)], [("xT", c)])

        def fm_group(ws, wk, col0, kcs, rhs_of, hf):
            pb, pk = bank()
            for kc in range(kcs):
                rap, rkey = rhs_of(kc, hf)
                mm(pb[:], ws[:, kc, col0:col0 + 128], rap, kc == 0, kc == kcs - 1, [wk, rkey], [pk])
            return pb, pk

        def h_rhs(kc, hf):
            return hT[:, kc, hf * 512:(hf + 1) * 512], ("hT", kc)

        evq = {"n": 0}

        def copy_evac(out, in_, reads, writes, scale=None):
            evq["n"] += 1
            if scale is not None or evq["n"] % 2 == 0:
                act(out, in_, AF.Copy, reads, writes, scale=scale)
            else:
                dve(lambda e: e.tensor_copy(out=out, in_=in_), reads, writes)

        def attn_scores(kblocks, q_rhs, n_q_mm, PT, PTname):
            pi = cnt.setdefault(PTname, 0) % 2
            cnt[PTname] = pi + 1
            pt = PT[pi]
            for kbi, kb in enumerate(kblocks):
                hasb = kb["bias"] is not None
                vb = validT[:, kb["valid"]:kb["valid"] + 1]
                if n_q_mm == 1:
                    pb, pk = bank()
                    kap, kkey = kb["kT"](0)
                    qap, qkey = q_rhs(0)
                    out = pb[:].rearrange("p (c q) -> p c q", c=4)
                    mm(out, kap, qap, True, not hasb, [kkey, qkey], [pk])
                    if hasb:
                        bap, bkey = kb["bias"](0)
                        mm(out, identB[:], bap, False, True, [bkey, "identB"], [pk])
                    act(pt[:, kbi, :], pb[:], AF.Exp, [pk, "valid"], [(PTname, pi, kbi)], bias=vb, scale=1.0)
                else:
                    b0 = bank_pair()
                    pbs = [(psum[b0], ("ps", b0)), (psum[b0 + 1], ("ps", b0 + 1))]
                    for c in range(4):
                        pb, pk = pbs[c % 2]
                        kap, kkey = kb["kT"](c)
                        qap, qkey = q_rhs(c)
                        out = pb[:, (c // 2) * 128:(c // 2 + 1) * 128]
                        mm(out, kap, qap, True, True, [kkey, qkey], [pk])
                    act(pt[:, kbi, :].rearrange("p (b x) -> p b x", b=2), psum_all[:, b0:b0 + 2, 0:256], AF.Exp,
                        [pbs[0][1], pbs[1][1], "valid"], [(PTname, pi, kbi, 0), (PTname, pi, kbi, 1)], bias=vb, scale=1.0)
                    if hasb:
                        eap, ekey = kb["bias"](0)
                        dve(lambda e, eap=eap, kbi=kbi: e.tensor_tensor(out=pt[:, kbi, :].rearrange("p (c q) -> p c q", c=4),
                                                                       in0=pt[:, kbi, :].rearrange("p (c q) -> p c q", c=4), in1=eap, op=ALU.mult),
                            [ekey], [(PTname, pi, kbi, 0), (PTname, pi, kbi, 1)])
                for (ks, qs) in kb.get("zero", ()):
                    dve(lambda e, ks=ks, qs=qs, kbi=kbi: e.memset(
                        pt[ks * 64:(ks + 1) * 64, kbi, :].rearrange("p (c q) -> p c q", c=4)[:, :, qs * 64:(qs + 1) * 64], 0.0),
                        [], [(PTname, pi, kbi, 0), (PTname, pi, kbi, 1)])
            return (pt, pi)

        def attn_pv(kblocks, n_q_mm, h, PTname, otk, otkkey, ocols0, sink_col0):
            pt, pi = h
            nkb = len(kblocks)
            ob, ok = obank()
            for c in range(4):
                pos = c if n_q_mm == 1 else (c % 2) * 2 + c // 2
                for kbi, kb in enumerate(kblocks):
                    vap, vkey = kb["V"](c)
                    ptk = [(PTname, pi, kbi)] if n_q_mm == 1 else [(PTname, pi, kbi, c % 2)]
                    mm(ob[:, c * 80:c * 80 + 65], pt[:, kbi, pos * 128:(pos + 1) * 128], vap, kbi == 0, kbi == nkb - 1,
                       ptk + [vkey], [ok])
            ov = ob[:, 0:320].rearrange("p (c e) -> p c e", c=4)
            ns = cnt.setdefault("nrm", 0) % 2
            cnt["nrm"] = ns + 1
            den = nrm[:, ns, 0:4]
            rec = nrm[:, ns, 8:12]
            nk = ("nrm", ns)
            if sink_col0 is not None:
                dve(lambda e: e.tensor_tensor(out=den, in0=ov[:, :, 64], in1=sinkE[:, sink_col0:sink_col0 + 4], op=ALU.add),
                    [ok, "sinkE"], [nk])
                dve(lambda e: e.reciprocal(out=rec, in_=den), [nk], [nk])
            else:
                dve(lambda e: e.reciprocal(out=rec, in_=ov[:, :, 64]), [ok], [nk])
            for c in range(4):
                dve(lambda e, c=c: e.tensor_scalar(out=otk[:, ocols0 + c * 64: ocols0 + (c + 1) * 64], in0=ov[:, c, 0:64],
                                                   scalar1=rec[:, c:c + 1], scalar2=None, op0=ALU.mult),
                    [ok, nk], [otkkey])

        def otok_to_fm(otk, otkkey, ncols, OT, OTname, ch0, i):
            nch = ncols // 128
            pb, pk = bank()
            pbb = pb[:].bitcast(BF16)
            for ch in range(nch):
                fw.op("pe", lambda e, ch=ch: e.transpose(pbb[:, ch * 128:(ch + 1) * 128], otk[:, ch * 128:(ch + 1) * 128], identB[:]),
                      [otkkey, "identB"], [pk])
            copy_evac(OT[:, ch0:ch0 + nch, i * 128:(i + 1) * 128],
                      pbb[:, 0:nch * 128].rearrange("p (c q) -> p c q", c=nch), [pk],
                      [(OTname, ch0 + ch) for ch in range(nch)])

        def run_pipeline(blocks, hooks=None):
            prev = None
            pend = [None]

            def finish(p):
                b, h = p
                attn_pv(b["kbl"], b["nq"], h, b["PTname"], b["otk"], b["otkkey"], b["ocols0"], b["sink"])
                return b["post"]
            for bi_, b in enumerate(blocks):
                h = attn_scores(b["kbl"], b["q_rhs"], b["nq"], b["PT"], b["PTname"])
                if pend[0] is not None:
                    pend[0]()
                    pend[0] = None
                if prev is not None:
                    pend[0] = finish(prev)
                prev = (b, h)
                if hooks and bi_ in hooks:
                    hooks[bi_]()
            if pend[0] is not None:
                pend[0]()
            last = finish(prev)
            if last is not None:
                last()

        KSTOP = 99
        for l in range(2):
          for _stage in range(1):
            wv = winx_d[l].rearrange("(kc p) n -> p kc n", p=128)
            if l * 6 + 1 > KSTOP:
                break
            PH = fw.__dict__.setdefault("phases", [])
            PH.append(("L%d norm1" % l, len(fw.prog["pe"])))

            fw.fence(FFN + ["hT"], SQ + ["hT"])
            rms_stats(xT, "xT", sq, "mT")
            mod_norm(l, 0, 1, hT, "hT")

            if l * 6 + 2 > KSTOP:
                break
            PH.append(("L%d C" % l, len(fw.prog["pe"])))
            fw.fence(SQ + WOUT + FFN, MIX_C)
            ws, wk = wload(wv[:, :, 0:512], [128, 8, 512])
            for j in range(4):
                for hf in range(2):
                    pb, pk = fm_group(ws, wk, j * 128, 8, h_rhs, hf)
                    copy_evac(UCT[:, j, hf * 512:(hf + 1) * 512], pb[:], [pk], [("UCT", j)])
            for tb in range(8):
                for gp in range(2):
                    pb, pk = bank()
                    for gg in range(2):
                        g = gp * 2 + gg
                        mm(pb[:, gg * 256:(gg + 1) * 256], UCT[:, g, tb * 128:(tb + 1) * 128], csB[:], True, True,
                           [("UCT", g), "csB"], [pk])
                    copy_evac(ABtok[:, tb, gp * 2:gp * 2 + 2, :], pb[:].rearrange("p (g e) -> p g e", g=2), [pk], [("ABtok", tb)])
            fw.fence(FFN, OTS)
            for hf in range(2):
                cv = dftc_d.rearrange("(tb p) n -> p tb n", p=128)[:, :, hf * 512:(hf + 1) * 512]
                sv = dfts_d.rearrange("(tb p) n -> p tb n", p=128)[:, :, hf * 512:(hf + 1) * 512]
                wc, wck = wload(cv, [128, 8, 512])
                wsn, wsk = wload(sv, [128, 8, 512])
                for g in range(4):
                    pb, pk = bank()
                    for tb in range(8):
                        mm(pb[:], ABtok[:, tb, g, 0:128], wc[:, tb, :], tb == 0, False, [("ABtok", tb), wck], [pk])
                        mm(pb[:], ABtok[:, tb, g, 128:256], wsn[:, tb, :], False, tb == 7, [("ABtok", tb), wsk], [pk])
                    copy_evac(OCT[:, g, hf * 512:(hf + 1) * 512], pb[:], [pk], [("OCT", g)])

            if l * 6 + 3 > KSTOP:
                break
            PH.append(("L%d A" % l, len(fw.prog["pe"])))
            fw.fence(MIX_C, MIX_A)
            sp_load("ropeld", ropeT, rope_d.rearrange("k p t -> p k t"), ["rope"])
            fw.dma("pool", "trild", lambda e: e.dma_start(out=triB, in_=tri_d.rearrange("p (a b) -> p a b", a=2)), writes=["tri"])
            for tb in range(2):
                s = tb % 2
                sp_load(("xin", s), xtok[s][:, 0:128], cak_d[l, tb * 128:(tb + 1) * 128, :], [("xtok", s)])
                pb, pk = bank()
                tr(pb[:, 0:128], xtok[s][:, 0:128], [("xtok", s)], [pk])
                for g in range(2):
                    copy_evac(KcATp[g][g * 64:(g + 1) * 64, tb * 128:(tb + 1) * 128], pb[g * 64:(g + 1) * 64, 0:128], [pk], [("KcAT", g, tb)])
            for tb in range(2):
                fw.dma("pool", "vcald", lambda e, l=l, tb=tb: e.dma_start(out=VcA[:, tb, :, 0:64],
                                                                        in_=cav_d[l, tb * 128:(tb + 1) * 128, :].rearrange("p (h d) -> p h d", h=2)),
                       writes=["VcA"])
            dve(lambda e: e.memset(VcA[:, :, :, 64:65], 1.0), [], ["VcA1"])
            for g in range(2):
                og = 1 - g
                dve(lambda e, g=g, og=og: e.memset(KATp[g][og * 64:(og + 1) * 64, :], 0.0), [], [("KATz", g)])
                dve(lambda e, g=g, og=og: e.memset(KcATp[g][og * 64:(og + 1) * 64, :], 0.0), [], [("KcATz", g)])
            dve(lambda e: e.memset(VA[:, :, :, 64:65], 1.0), [], ["VA1"])
            KSUB = 99
            if KSUB < 1:
                break
            for u in range(2):
                ws, wk = wload(wv[:, :, 512 + u * 512: 1024 + u * 512], [128, 8, 512])
                for cc in range(2):
                    c = u * 2 + cc
                    for hf in range(2):
                        p1, k1 = fm_group(ws, wk, cc * 256, 8, h_rhs, hf)
                        p2, k2 = fm_group(ws, wk, cc * 256 + 128, 8, h_rhs, hf)
                        sl = slice(hf * 512, (hf + 1) * 512)
                        dve(lambda e, p1=p1, sl=sl: e.tensor_tensor(out=tmp[0][:, 0:512], in0=p1[:], in1=ropeT[:, 0, sl], op=ALU.mult),
                            [k1, "rope"], [("tmp", 0)])
                        dve(lambda e, p2=p2, sl=sl: e.tensor_tensor(out=tmp[1][:, 0:512], in0=p2[:], in1=ropeT[:, 1, sl], op=ALU.mult),
                            [k2, "rope"], [("tmp", 1)])
                        dve(lambda e, c=c, sl=sl: e.tensor_tensor(out=QAT[:, c, sl], in0=tmp[0][:, 0:512], in1=tmp[1][:, 0:512], op=ALU.add),
                            [("tmp", 0), ("tmp", 1)], [("QAT", c)])
            if KSUB < 2:
                break
            ws, wk = wload(wv[:, :, 1536:2048], [128, 8, 512])
            for hf in range(2):
                p1, k1 = fm_group(ws, wk, 0, 8, h_rhs, hf)
                p2, k2 = fm_group(ws, wk, 128, 8, h_rhs, hf)
                sl = slice(hf * 512, (hf + 1) * 512)
                dve(lambda e, p1=p1, sl=sl: e.tensor_tensor(out=tmp[0][:, 0:512], in0=p1[:], in1=ropeT[:, 2, sl], op=ALU.mult),
                    [k1, "rope"], [("tmp", 0)])
                dve(lambda e, p2=p2, sl=sl: e.tensor_tensor(out=tmp[1][:, 0:512], in0=p2[:], in1=ropeT[:, 3, sl], op=ALU.mult),
                    [k2, "rope"], [("tmp", 1)])
                for g in range(2):
                    dve(lambda e, sl=sl, g=g: e.tensor_tensor(out=KATp[g][g * 64:(g + 1) * 64, sl], in0=tmp[0][g * 64:(g + 1) * 64, 0:512],
                                                             in1=tmp[1][g * 64:(g + 1) * 64, 0:512], op=ALU.add),
                        [("tmp", 0), ("tmp", 1)], [("KAT", g, hf)])
            KV = 7
            for tb in range(8 if KV & 8 == 0 else 0):
                pb, pk = bank()
                for kc in range(8):
                    mm(pb[:, 0:256], hT[:, kc, tb * 128:(tb + 1) * 128], ws[:, kc, 256:512], kc == 0, kc == 7, [wk, ("hT", kc)], [pk])
                s = tb % 2
                if KV & 1:
                    act(stage[s][:, 0:256], pb[:, 0:256], AF.Copy, [pk], [("stageA", s)])
                if KV & 2:
                    dve(lambda e, pb=pb, tb=tb: e.tensor_copy(out=VA[:, tb, :, 0:64], in_=pb[:, 128:256].rearrange("p (h d) -> p h d", h=2)),
                        [pk], [("VA", tb)])
                if KV & 4:
                    fw.dma("sp", ("kvoA", s), lambda e, s=s, tb=tb, l=l: e.dma_start(out=kv_d[l, tb * 128:(tb + 1) * 128, 0:256], in_=stage[s][:, 0:256]),
                           reads=[("stageA", s)], writes=[("kvo", l, tb, 0)])
            if KSUB < 3:
                break
            PH.append(("L%d A-attn" % l, len(fw.prog["pe"])))
            dve(lambda e: e.memset(ctmpA[:, 0, 0:1], 0.0),
                [("QAT", c) for c in range(4)] + ["VA1", "VcA1"] + [("KATz", g) for g in range(2)] + [("KcATz", g) for g in range(2)]
                + [("KAT", g, hf) for g in range(2) for hf in range(2)] + [("KcAT", g, tb) for g in range(2) for tb in range(2)],
                [("QATall",)])
            blocksA = []
            for i in range(8 if KSUB > 3 else 1):
                for g in range(2):
                    def mk_local(kbk, tri_idx, vcol, g=g):
                        return {"kT": (lambda c, kbk=kbk, g=g: (KATp[g][:, kbk * 128:(kbk + 1) * 128], ("QATall",))),
                                "bias": None if tri_idx is None else (lambda c, tri_idx=tri_idx: (triB[:, tri_idx, :].rearrange("p (c q) -> p c q", c=4), "tri")),
                                "valid": vcol,
                                "V": (lambda c, kbk=kbk, g=g: (VA[:, kbk, g, 0:65], ("VA", kbk)))}

                    def mk_ctx(tb, vcol, g=g):
                        return {"kT": (lambda c, tb=tb, g=g: (KcATp[g][:, tb * 128:(tb + 1) * 128], ("QATall",))),
                                "bias": None, "valid": vcol,
                                "V": (lambda c, tb=tb, g=g: (VcA[:, tb, g, 0:65], "VcA"))}
                    kbl = [mk_local(max(i - 1, 0), 0, i * 5 + 0), mk_local(i, None, i * 5 + 1), mk_local(min(i + 1, 7), 1, i * 5 + 2),
                           mk_ctx(0, i * 5 + 3), mk_ctx(1, i * 5 + 4)]
                    oi = i % 2
                    post = None
                    if g == 1:
                        post = (lambda oi=oi, i=i: otok_to_fm(OtokA[oi], ("OtokA", oi), 512, OAT, "OAT", 0, i))
                    blocksA.append({"kbl": kbl, "q_rhs": (lambda c, i=i: (QAT[:, :, i * 128:(i + 1) * 128], ("QATall",))), "nq": 1,
                                    "PT": PTa, "PTname": "PTa", "otk": OtokA[oi], "otkkey": ("OtokA", oi), "ocols0": g * 256,
                                    "sink": l * 8 + g * 4, "post": post})
            run_pipeline(blocksA, hooks={3: (lambda l=l: ada_seg(l, 2)), 9: (lambda l=l: ada_seg(l, 3))} if KSUB > 3 else None)

            if l * 6 + 4 > KSTOP:
                break
            PH.append(("L%d B" % l, len(fw.prog["pe"])))
            fw.fence(MIX_A, MIX_B)
            for tb in range(2):
                for ch in range(4):
                    s = (tb * 4 + ch) % 2
                    sp_load(("xin", s), xtok[s][:, 0:128], cbk_d[l, tb * 128:(tb + 1) * 128, ch * 128:(ch + 1) * 128], [("xtok", s)])
                    pb, pk = bank()
                    tr(pb[:, 0:128], xtok[s][:, 0:128], [("xtok", s)], [pk])
                    copy_evac(KcBT[:, ch, tb * 128:(tb + 1) * 128], pb[:, 0:128], [pk], ["KcBT"])
            for tb in range(2):
                fw.dma("pool", "vcbld", lambda e, l=l, tb=tb: e.dma_start(out=VcB[:, tb, :, 0:64],
                                                                        in_=cbv_d[l, tb * 128:(tb + 1) * 128, :].rearrange("p (h d) -> p h d", h=8)),
                       writes=["VcB"])
            dve(lambda e: e.memset(VcB[:, :, :, 64:65], 1.0), [], ["VcB1"])
            dve(lambda e: e.memset(VB[:, :, :, 64:65], 1.0), [], ["VB1"])
            ws, wk = wload(wv[:, :, 2048:2560], [128, 8, 512])
            for j in range(4):
                for hf in range(2):
                    pb, pk = fm_group(ws, wk, j * 128, 8, h_rhs, hf)
                    copy_evac(QBT[:, j, hf * 512:(hf + 1) * 512], pb[:], [pk], [("QBT", j)], scale=0.125)
            ws, wk = wload(wv[:, :, 2560:3072], [128, 8, 512])
            for j in range(4):
                for hf in range(2):
                    pb, pk = fm_group(ws, wk, j * 128, 8, h_rhs, hf)
                    copy_evac(KBT[:, j, hf * 512:(hf + 1) * 512], pb[:], [pk], [("KBT", j)])
            wsk_, wkk = wload(wv[:, :, 3072:3584], [128, 8, 512])
            wsv_, wkv = wload(wv[:, :, 3584:4096], [128, 8, 512])
            for tb in range(8):
                s = tb % 2
                pb, pk = bank()
                for kc in range(8):
                    mm(pb[:], hT[:, kc, tb * 128:(tb + 1) * 128], wsk_[:, kc, :], kc == 0, kc == 7, [wkk, ("hT", kc)], [pk])
                act(stage[s][:, 256:768], pb[:], AF.Copy, [pk], [("stageB", s)])
                pb2, pk2 = bank()
                for kc in range(8):
                    mm(pb2[:], hT[:, kc, tb * 128:(tb + 1) * 128], wsv_[:, kc, :], kc == 0, kc == 7, [wkv, ("hT", kc)], [pk2])
                act(stage[s][:, 768:1280], pb2[:], AF.Copy, [pk2], [("stageB", s)])
                dve(lambda e, pb2=pb2, tb=tb: e.tensor_copy(out=VB[:, tb, :, 0:64], in_=pb2[:].rearrange("p (h d) -> p h d", h=8)),
                    [pk2], [("VB", tb)])
                fw.dma("sp", ("kvoB", s), lambda e, s=s, tb=tb, l=l: e.dma_start(out=kv_d[l, tb * 128:(tb + 1) * 128, 256:1280], in_=stage[s][:, 256:1280]),
                       reads=[("stageB", s)], writes=[("kvo", l, tb, 1)])
            dve(lambda e: e.memset(ctmpB[:, 0, 0:1], 0.0), [("QBT", c) for c in range(4)] + [("KBT", c) for c in range(4)] + ["VB1", "VcB1", "KcBT", "VcB"],
                [("QKBall",)])
            PH.append(("L%d B-attn" % l, len(fw.prog["pe"])))
            for hg in range(2):
                fw.dma("pool", "ttbld", lambda e, l=l, hg=hg: e.dma_start(out=TTB, in_=ttb_d[l, hg].rearrange("p (h c) -> p h c", h=4)),
                       writes=["TTB"])
                for c4 in range(4):
                    act(TTB[:, c4, :], TTB[:, c4, :], AF.Exp, ["TTB"], ["TTB"])
                blocksB = []
                for i in range(8):
                    kb0 = min(max(i - 2, 0), 3)
                    kbl = []
                    for j in range(5):
                        kbk = kb0 + j
                        delta = kbk - i
                        pos0 = 8 - 2 * delta
                        d = {"kT": (lambda c, kbk=kbk, hg=hg: (KBT[(c % 2) * 64:(c % 2) * 64 + 64, hg * 2 + c // 2, kbk * 128:(kbk + 1) * 128], ("QKBall",))),
                             "bias": (lambda c, pos0=pos0: (TTB[:, :, pos0 * 64: pos0 * 64 + 128], "TTB")),
                             "valid": 40 + i * 7 + j,
                             "V": (lambda c, kbk=kbk, hg=hg: (VB[:, kbk, hg * 4 + c, 0:65], ("VB", kbk)))}
                        if 2 <= i <= 5 and delta == -2:
                            d["zero"] = [(0, 1)]
                        if 2 <= i <= 5 and delta == 2:
                            d["zero"] = [(0, 0), (1, 0), (1, 1)]
                        kbl.append(d)
                    for tb in range(2):
                        kbl.append({"kT": (lambda c, tb=tb, hg=hg: (KcBT[(c % 2) * 64:(c % 2) * 64 + 64, hg * 2 + c // 2, tb * 128:(tb + 1) * 128], ("QKBall",))),
                                    "bias": None, "valid": 40 + i * 7 + 5 + tb,
                                    "V": (lambda c, tb=tb, hg=hg: (VcB[:, tb, hg * 4 + c, 0:65], ("QKBall",)))})
                    oi = i % 2
                    blocksB.append({"kbl": kbl,
                                    "q_rhs": (lambda c, i=i, hg=hg: (QBT[(c % 2) * 64:(c % 2) * 64 + 64, hg * 2 + c // 2, i * 128:(i + 1) * 128], ("QKBall",))),
                                    "nq": 4, "PT": PTb, "PTname": "PTb", "otk": OtokB[oi], "otkkey": ("OtokB", oi), "ocols0": 0, "sink": None,
                                    "post": (lambda oi=oi, i=i, hg=hg: otok_to_fm(OtokB[oi], ("OtokB", oi), 256, OBT, "OBT", hg * 2, i))})
                if hg == 0:
                    hk = {1: (lambda l=l: ada_seg(l, 4)), 4: (lambda l=l: ada_seg(l, 5))}
                else:
                    hk = {1: (lambda: ada_seg(1, 0)), 4: (lambda: ada_seg(1, 1))} if l == 0 else None
                run_pipeline(blocksB, hooks=hk)

            if l * 6 + 5 > KSTOP:
                break
            PH.append(("L%d merge" % l, len(fw.prog["pe"])))
            fw.fence(MIX_B + MIX_A + MIX_C + SQ, WOUT)
            wbv = wbr_d[l].rearrange("b (kc p) n -> p (b kc) n", p=128)
            OTv = [OAT, OBT, OCT]
            OTn = ["OAT", "OBT", "OCT"]
            for j in range(8):
                wg, wgk = wload(wv[:, :, 4096 + j * 384: 4096 + (j + 1) * 384], [128, 8, 384])
                wb, wbk = wload(wbv[:, :, j * 128:(j + 1) * 128], [128, 12, 128])
                for hf in range(2):
                    sl = slice(hf * 512, (hf + 1) * 512)
                    for br in range(3):
                        pg, kg = fm_group(wg, wgk, br * 128, 8, h_rhs, hf)
                        sg = tmp[0][:, 0:512] if br % 2 == 0 else tmp[1][:, 0:512]
                        sgk = ("tmp", br % 2)
                        act(sg, pg[:], AF.Sigmoid, [kg], [sgk])
                        py, ky = bank()
                        for kc in range(4):
                            mm(py[:], wb[:, br * 4 + kc, :], OTv[br][:, kc, sl], kc == 0, kc == 3, [wbk, (OTn[br], kc)], [ky])
                        if br == 0:
                            dve(lambda e, py=py, sg=sg: e.tensor_tensor(out=rstd[:, 0:512], in0=py[:], in1=sg, op=ALU.mult),
                                [ky, sgk], [("rstd", 0)])
                        else:
                            dve(lambda e, py=py, sg=sg: e.tensor_tensor(out=sg, in0=py[:], in1=sg, op=ALU.mult), [ky, sgk], [sgk])
                            if br == 1:
                                dve(lambda e, sg=sg: e.tensor_tensor(out=rstd[:, 0:512], in0=rstd[:, 0:512], in1=sg, op=ALU.add),
                                    [sgk, ("rstd", 0)], [("rstd", 0)])
                            else:
                                dve(lambda e, sg=sg, j=j, sl=sl: e.tensor_tensor(out=mT[:, j, sl], in0=rstd[:, 0:512], in1=sg, op=ALU.add),
                                    [sgk, ("rstd", 0)], [("mT", j)])
            PH.append(("L%d wout" % l, len(fw.prog["pe"])))
            wov = wout_d[l].rearrange("(kc p) n -> p kc n", p=128)
            m_rhs = lambda kc, hf: (mT[:, kc, hf * 512:(hf + 1) * 512], ("mT", kc))
            for u in range(2):
                ws, wk = wload(wov[:, :, u * 512:(u + 1) * 512], [128, 8, 512])
                for jj in range(4):
                    j = u * 4 + jj
                    for hf in range(2):
                        pb, pk = fm_group(ws, wk, jj * 128, 8, m_rhs, hf)
                        sl = slice(hf * 512, (hf + 1) * 512)
                        act(sq2[:, j, sl], pb[:], AF.Square, [pk], [("hT", j)])
                        dve(lambda e, pb=pb, j=j, sl=sl: e.tensor_copy(out=yT[:, j, sl], in_=pb[:]), [pk], [("yT", j)])
            rms_stats(yT, "yT", sq2, "hT", presquared=True)
            post_resid(l, 2, yT, "yT")

            if l * 6 + 6 > KSTOP:
                break
            PH.append(("L%d ffn" % l, len(fw.prog["pe"])))
            fw.fence(WOUT + OTS + SQ, FFN)
            rms_stats(xT, "xT", sq, "mT")
            mod_norm(l, 3, 4, hT, "hT")
            wfv = wfi_d[l].rearrange("(kc p) (s n) -> p kc s n", p=128, s=2)
            for jp in range(11):
                ws, wk = wload(wfv[:, :, :, jp * 256:(jp + 1) * 256], [128, 8, 2, 256])
                for jj in range(2):
                    j = jp * 2 + jj
                    for hf in range(2):
                        sl = slice(hf * 512, (hf + 1) * 512)
                        pg, kg = bank()
                        for kc in range(8):
                            mm(pg[:], ws[:, kc, 0, jj * 128:(jj + 1) * 128], hT[:, kc, sl], kc == 0, kc == 7, [wk, ("hT", kc)], [kg])
                        pu, ku = bank()
                        for kc in range(8):
                            mm(pu[:], ws[:, kc, 1, jj * 128:(jj + 1) * 128], hT[:, kc, sl], kc == 0, kc == 7, [wk, ("hT", kc)], [ku])
                        sg = tmp[(j * 2 + hf) % 2][:, 0:512]
                        sgk = ("tmp", (j * 2 + hf) % 2)
                        act(sg, pg[:], AF.Silu, [kg], [sgk])
                        dve(lambda e, pu=pu, sg=sg, j=j, sl=sl: e.tensor_tensor(out=aT[:, j, sl], in0=pu[:], in1=sg, op=ALU.mult),
                            [ku, sgk], [("aT", j)])
            wfov = wfo_d[l].rearrange("(kc p) n -> p kc n", p=128)
            a_rhs = lambda kc, hf: (aT[:, kc, hf * 512:(hf + 1) * 512], ("aT", kc))
            for u in range(4):
                wsA, wkA = wload(wfov[:, 0:11, u * 256:(u + 1) * 256], [128, 11, 256])
                wsB, wkB = wload(wfov[:, 11:22, u * 256:(u + 1) * 256], [128, 11, 256])
                for jj in range(2):
                    j = u * 2 + jj
                    for hf in range(2):
                        pb, pk = bank()
                        for kc in range(22):
                            ws_, wk_ = (wsA, wkA) if kc < 11 else (wsB, wkB)
                            rap, rkey = a_rhs(kc, hf)
                            mm(pb[:], ws_[:, kc % 11, jj * 128:(jj + 1) * 128], rap, kc == 0, kc == 21, [wk_, rkey], [pk])
                        sl = slice(hf * 512, (hf + 1) * 512)
                        act(sq2[:, j, sl], pb[:], AF.Square, [pk], [("hT", j)])
                        dve(lambda e, pb=pb, j=j, sl=sl: e.tensor_copy(out=y2T[:, j, sl], in_=pb[:]), [pk], [("y2T", j)])
            rms_stats(y2T, "y2T", sq2, "hT", presquared=True)
            post_resid(l, 5, y2T, "y2T")

        fw.__dict__.setdefault("phases", []).append(("out", len(fw.prog["pe"])))
        for b in range(8):
            s = b % 2
            for g in range(2):
                pb, pk = bank()
                for j in range(4):
                    c = g * 4 + j
                    tr(pb[:, j * 128:(j + 1) * 128], xT[:, c, b * 128:(b + 1) * 128], [("xT", c)], [pk])
                copy_evac(xtok[s][:, g * 512:(g + 1) * 512], pb[:], [pk], [("xtok", s)])
            fw.dma("sp", ("yout", s), lambda e, b=b, s=s: e.dma_start(out=y_d[b * 128:(b + 1) * 128, :], in_=xtok[s]),
                   reads=[("xtok", s)], writes=[("yo", b)])
        fw.final_wait("sp", [("yo", b) for b in range(8)] + [k for k in [("kvo", l, tb, k) for l in range(2) for tb in range(8) for k in range(2)] if k in fw.state])
        fw.emit(st)
        nc._phases = fw.__dict__.get("phases", [])
    return nc


def _rope_tables(sample):
    out = np.zeros((4, 128, T), np.float32)
    d = np.arange(128) % 64
    if sample:
        t = np.arange(T)
        row = (t // 64).astype(np.float32)
        col = (t % 64).astype(np.float32)
        inv = (np.float32(10000.0) ** (-np.arange(16, dtype=np.float32) / np.float32(16))).astype(np.float32)
        ang = np.concatenate([row[:, None] * inv, col[:, None] * inv], axis=-1).astype(np.float32)
        cos = np.cos(ang).astype(np.float32)
        sin = np.sin(ang).astype(np.float32)
        C = cos[:, d % 32].T
        S = sin[:, d % 32].T * np.where(d < 32, -1.0, 1.0)[:, None].astype(np.float32)
    else:
        C = np.ones((128, T), np.float32)
        S = np.zeros((128, T), np.float32)
    out[0] = C * np.float32(0.125)
    out[1] = S * np.float32(0.125)
    out[2] = C
    out[3] = S
    return out


def _dft_tables(sample):
    t = np.arange(T, dtype=np.int64)
    if sample:
        n = T
        ph = (t[:, None] * t[None, :]) % n
        m = np.ones((T, T))
    else:
        n = 256
        ph = ((t[:, None] % n) * (t[None, :] % n)) % n
        m = ((t[:, None] // n) == (t[None, :] // n)).astype(np.float64)
    ang = 2.0 * np.pi * ph / n
    import ml_dtypes
    ct = (np.cos(ang) * m / np.sqrt(n)).astype(np.float32).astype(ml_dtypes.bfloat16)
    nst = (-np.sin(ang) * m / np.sqrt(n)).astype(np.float32).astype(ml_dtypes.bfloat16)
    return ct, nst


def _cs_table():
    c = np.arange(128, dtype=np.int64)
    ang = 2.0 * np.pi * ((c[:, None] * c[None, :]) % 128) / 128.0
    return np.concatenate([np.cos(ang), np.sin(ang)], axis=1).astype(np.float32) / np.float32(np.sqrt(128.0))


def _ttb_tables(b_rpb, sample):
    out = np.zeros((2, 2, 128, 4, 18, 64), np.float32)
    if not sample:
        return out.reshape(2, 2, 128, 4 * 1152)
    kcol = np.arange(64)[:, None]
    c = np.arange(64)[None, :]
    ws = np.clip(c - 8, 0, 48)
    colok = (kcol >= ws) & (kcol < ws + 16)
    dc = np.clip(kcol - c + 15, 0, 30)
    for l in range(2):
        for h in range(8):
            for ks in range(2):
                for pos in range(18):
                    a = 15 - pos + ks
                    if 0 <= a <= 14:
                        tile = np.where(colok, b_rpb[l, h, a][dc], np.float32(NEG))
                    else:
                        tile = np.full((64, 64), NEG, np.float32)
                    out[l, h // 4, ks * 64:(ks + 1) * 64, ((h % 4) % 2) * 2 + (h % 4) // 2, pos, :] = tile
    return out.reshape(2, 2, 128, 4 * 1152)


def _tri_table(sample):
    out = np.zeros((128, 2, 4, 128), np.float32)
    if sample:
        j = np.arange(128)[:, None]
        q = np.arange(128)[None, :]
        out[:, 0] = np.where(j >= q, 0.0, NEG)[:, None, :]
        out[:, 1] = np.where(j <= q, 0.0, NEG)[:, None, :]
    return out.reshape(128, 1024)


def _valid_table(sample):
    v = np.zeros((96,), np.float32)
    for i in range(8):
        if sample:
            a = [i >= 1, True, i <= 6, True, True]
        else:
            a = [i % 2 == 1, True, i % 2 == 0, False, False]
        for k in range(5):
            v[i * 5 + k] = 0.0 if a[k] else NEG
        kb0 = min(max(i - 2, 0), 3)
        for j in range(5):
            kb = kb0 + j
            if sample:
                ok = (kb <= 3) if i <= 1 else ((kb >= 4) if i >= 6 else True)
            else:
                ok = (kb // 2) == (i // 2)
            v[40 + i * 7 + j] = 0.0 if ok else NEG
        for tb in range(2):
            v[40 + i * 7 + 5 + tb] = 0.0 if sample else NEG
    return np.broadcast_to(v[None, :], (128, 96)).copy()


def _winx(w_in):
    idx = []
    idx += list(range(2304, 2816))
    rot = lambda base: [base + (d + 32) % 64 for d in range(64)]
    nat = lambda base: [base + d for d in range(64)]
    for c in range(4):
        idx += nat(c * 64) + nat((c + 4) * 64)
        idx += rot(c * 64) + rot((c + 4) * 64)
    idx += nat(512) + nat(576)
    idx += rot(512) + rot(576)
    idx += list(range(512, 640)) + list(range(640, 768))
    idx += list(range(768, 1280))
    idx += list(range(1280, 1792))
    idx += list(range(1280, 1792))
    idx += list(range(1792, 2304))
    for j in range(8):
        for br in range(3):
            idx += list(range(2816 + br * 1024 + j * 128, 2816 + br * 1024 + (j + 1) * 128))
    idx = np.asarray(idx)
    assert idx.shape[0] == WIN_COLS
    return np.ascontiguousarray(w_in[:, :, idx])


_CACHE = {}


def make_in_maps(x_prompt, x_sample, cache_a_k, cache_a_v, cache_b_k, cache_b_v, c, c_ctx,
                 w_ada, b_ada, norm_pre, norm_post, w_in, a_sink, b_rpb, w_branch, w_out,
                 w_ffn_in, w_ffn_out):
    f = lambda a: np.ascontiguousarray(np.asarray(a, dtype=np.float32))
    x_prompt, x_sample = f(x_prompt), f(x_sample)
    cache_a_k, cache_a_v, cache_b_k, cache_b_v = f(cache_a_k), f(cache_a_v), f(cache_b_k), f(cache_b_v)
    c, c_ctx = f(c), f(c_ctx)
    w_ada, b_ada, norm_pre, norm_post, w_in = f(w_ada), f(b_ada), f(norm_pre), f(norm_post), f(w_in)
    a_sink, b_rpb, w_branch, w_out, w_ffn_in, w_ffn_out = f(a_sink), f(b_rpb), f(w_branch), f(w_out), f(w_ffn_in), f(w_ffn_out)

    shared = {
        "w_ada": w_ada,
        "b_adaT": np.ascontiguousarray(b_ada.reshape(2, 48, 128).transpose(0, 2, 1)),
        "npreT": np.ascontiguousarray(norm_pre.reshape(2, 2, 8, 128).transpose(3, 0, 1, 2).reshape(128, 32)),
        "npostT": np.ascontiguousarray(norm_post.reshape(2, 2, 8, 128).transpose(3, 0, 1, 2).reshape(128, 32)),
        "w_inx": _winx(w_in),
        "sinkb": np.ascontiguousarray(np.broadcast_to(a_sink.reshape(1, 16), (128, 16))),
        "dft_cs": _cs_table(),
        "w_branch": w_branch, "w_out": w_out, "w_ffn_in": w_ffn_in, "w_ffn_out": w_ffn_out,
        "ident": np.eye(128, dtype=np.float32),
    }
    per_type = {}
    for sample in (False, True):
        ct, nst = _dft_tables(sample)
        per_type[sample] = {
            "ttb": _ttb_tables(b_rpb, sample), "tri": _tri_table(sample), "valid": _valid_table(sample),
            "rope": _rope_tables(sample), "dft_ct": ct, "dft_nst": nst,
        }
    in_maps = []
    for core in range(8):
        m = dict(shared)
        if core < 4:
            m.update(per_type[False])
            m["x"] = np.ascontiguousarray(x_prompt[core * 4:(core + 1) * 4].reshape(T, 1024))
            m["cvec"] = np.ascontiguousarray(c_ctx.reshape(8, 128).T)
            m["cak"] = np.zeros((2, 256, 128), np.float32)
            m["cav"] = np.zeros((2, 256, 128), np.float32)
            m["cbk"] = np.zeros((2, 256, 512), np.float32)
            m["cbv"] = np.zeros((2, 256, 512), np.float32)
        else:
            b = core - 4
            m.update(per_type[True])
            m["x"] = np.ascontiguousarray(x_sample[b])
            m["cvec"] = np.ascontiguousarray(c[b].reshape(8, 128).T)
            m["cak"] = np.ascontiguousarray(cache_a_k[b].reshape(2, 256, 128))
            m["cav"] = np.ascontiguousarray(cache_a_v[b].reshape(2, 256, 128))
            m["cbk"] = np.ascontiguousarray(cache_b_k[b].reshape(2, 256, 512))
            m["cbv"] = np.ascontiguousarray(cache_b_v[b].reshape(2, 256, 512))
        in_maps.append(m)

    return in_maps


def kernel(**inputs):
    in_maps = make_in_maps(**inputs)
    if "nc" not in _CACHE:
        _CACHE["nc"] = build_program()
    nc = _CACHE["nc"]
    res = run_bass_kernel_spmd(nc, in_maps, core_ids=list(range(8)))
    return assemble(res.results)


def assemble(outs):
    y_prompt = np.concatenate([outs[k]["y"].reshape(4, 256, 1024) for k in range(4)], axis=0).astype(np.float32)
    y_sample = np.stack([outs[4 + k]["y"] for k in range(4)], axis=0).astype(np.float32)
    kv = np.concatenate([outs[k]["kvout"].reshape(2, 4, 256, 1280).transpose(1, 0, 2, 3) for k in range(4)], axis=0)
    new_a_k = np.ascontiguousarray(kv[..., 0:128]).reshape(16, 2, 256, 2, 64)
    new_a_v = np.ascontiguousarray(kv[..., 128:256]).reshape(16, 2, 256, 2, 64)
    new_b_k = np.ascontiguousarray(kv[..., 256:768]).reshape(16, 2, 256, 8, 64)
    new_b_v = np.ascontiguousarray(kv[..., 768:1280]).reshape(16, 2, 256, 8, 64)
    return (y_prompt, y_sample, new_a_k, new_a_v, new_b_k, new_b_v)
```

```python
import numpy as np
from contextlib import ExitStack
import concourse.bass as bass
import concourse.mybir as mybir
from concourse.bass_utils import run_bass_kernel_spmd

F32 = mybir.dt.float32
BF16 = mybir.dt.bfloat16
AF = mybir.ActivationFunctionType
ALU = mybir.AluOpType

NEG = -1e30
T = 1024
NS = 5
SLOT = 4096
WIN_COLS = 7168


class FW:
    def __init__(self, nc):
        self.nc = nc
        self.prog = {e: [] for e in ("pe", "act", "dve", "pool", "sp")}
        self.state = {}
        self.pending = {}
        self.dma_count = {}
        self.nseq = {e: 0 for e in self.prog}

    @staticmethod
    def _name(k):
        return k[0] if isinstance(k, tuple) else k

    def _get(self, k):
        st = self.state.get(k)
        if st is None:
            nm = self._name(k)
            if nm in self.pending:
                st = {"w": None, "r": dict(self.pending[nm])}
                self.state[k] = st
        return st

    def _collect(self, reads, writes):
        deps = {}

        def add(kk, v):
            if kk not in deps or deps[kk] < v:
                deps[kk] = v
        for k in reads:
            st = self._get(k)
            if st and st["w"] is not None:
                add((st["w"][0], st["w"][1]), st["w"][2])
        for k in writes:
            st = self._get(k)
            if st:
                if st["w"] is not None:
                    add((st["w"][0], st["w"][1]), st["w"][2])
                for kk, v in st["r"].items():
                    add(kk, v)
        return deps

    def _update(self, ticket, reads, writes):
        kk = (ticket[0], ticket[1])
        for k in reads:
            st = self.state.setdefault(k, {"w": None, "r": {}})
            if st["r"].get(kk, -1) < ticket[2]:
                st["r"][kk] = ticket[2]
        for k in writes:
            self.state[k] = {"w": ticket, "r": {}}

    def fence(self, old_names, new_names):
        old_names = set(old_names)
        new_names = set(new_names)
        F = {}
        for k, st in self.state.items():
            if self._name(k) in old_names:
                if st["w"] is not None:
                    kk = (st["w"][0], st["w"][1])
                    F[kk] = max(F.get(kk, -1), st["w"][2])
                for kk, v in st["r"].items():
                    F[kk] = max(F.get(kk, -1), v)
        for k, st in self.state.items():
            if self._name(k) in new_names:
                for kk, v in F.items():
                    st["r"][kk] = max(st["r"].get(kk, -1), v)
        for nm in new_names:
            p = self.pending.setdefault(nm, {})
            for kk, v in F.items():
                p[kk] = max(p.get(kk, -1), v)

    def op(self, eng, fn, reads=(), writes=()):
        ps_r = [k for k in reads if self._name(k) == "ps"]
        if ps_r:
            reads = [k for k in reads if self._name(k) != "ps"]
            writes = list(writes) + ps_r
        deps = self._collect(reads, writes)
        seq = self.nseq[eng]
        self.nseq[eng] += 1
        ticket = ("c", eng, seq)
        self.prog[eng].append({"fn": fn, "deps": deps, "ticket": ticket, "dma": None})
        self._update(ticket, reads, writes)

    def dma(self, queue, semkey, fn, reads=(), writes=()):
        deps = self._collect(reads, writes)
        cnt = self.dma_count.get(semkey, 0) + 1
        self.dma_count[semkey] = cnt
        ticket = ("d", semkey, cnt * 16)
        seq = self.nseq[queue]
        self.nseq[queue] += 1
        self.prog[queue].append({"fn": fn, "deps": deps, "ticket": ("c", queue, seq), "dma": semkey})
        self._update(ticket, reads, writes)

    def final_wait(self, queue, keys):
        deps = self._collect(keys, ())
        self.prog[queue].append({"fn": None, "deps": deps, "ticket": None, "dma": None})

    def emit(self, stack):
        nc = self.nc
        signal = {e: set() for e in self.prog}
        for e, lst in self.prog.items():
            for ins in lst:
                for (kind, key), val in ins["deps"].items():
                    if kind == "c":
                        if key == e and e in ("pe", "sp"):
                            continue
                        signal[key].add(val)
        rank = {}
        for e in self.prog:
            rank[e] = {seq: i + 1 for i, seq in enumerate(sorted(signal[e]))}
        sems = {}
        for e in self.prog:
            sems[("c", e)] = stack.enter_context(nc.semaphore("s_" + e))
        for n, k in enumerate(sorted(self.dma_count, key=str)):
            sems[("d", k)] = stack.enter_context(nc.semaphore("d%d" % n))
        block = stack.enter_context(nc.Block())
        engobj = {"pe": "tensor", "act": "scalar", "dve": "vector", "pool": "gpsimd", "sp": "sync"}

        def run(e, eng):
            waited = {}
            for ins in self.prog[e]:
                for (kind, key), val in sorted(ins["deps"].items(), key=str):
                    if kind == "c":
                        if key == e and e in ("pe", "sp"):
                            continue
                        v = rank[key][val]
                    else:
                        v = val
                    if waited.get((kind, key), 0) >= v:
                        continue
                    waited[(kind, key)] = v
                    eng.wait_ge(sems[(kind, key)], v)
                if ins["fn"] is None:
                    continue
                bi = ins["fn"](eng)
                if ins["dma"] is not None:
                    bi.then_inc(sems[("d", ins["dma"])], 16)
                elif ins["ticket"][2] in rank[e]:
                    bi.then_inc(sems[("c", e)], 1)

        for e in self.prog:
            if self.prog[e]:
                getattr(block, engobj[e])(lambda eng, e=e: run(e, eng))


def build_program():
    nc = bass.Bass("TRN2", target_bir_lowering=False)

    def D(name, shape, kind="ExternalInput"):
        return nc.dram_tensor(name, list(shape), F32, kind=kind).ap()

    x_d = D("x", [T, 1024])
    cvec_d = D("cvec", [128, 8])
    cak_d = D("cak", [2, 256, 128])
    cav_d = D("cav", [2, 256, 128])
    cbk_d = D("cbk", [2, 256, 512])
    cbv_d = D("cbv", [2, 256, 512])
    wada_d = D("w_ada", [2, 1024, 6144])
    bada_d = D("b_adaT", [2, 128, 48])
    npre_d = D("npreT", [128, 32])
    npost_d = D("npostT", [128, 32])
    winx_d = D("w_inx", [2, 1024, WIN_COLS])
    sink_d = D("sinkb", [128, 16])
    ttb_d = D("ttb", [2, 2, 128, 4 * 1152])
    tri_d = D("tri", [128, 1024])
    valid_d = D("valid", [128, 96])
    rope_d = D("rope", [4, 128, T])
    dftc_d = nc.dram_tensor("dft_ct", [T, T], BF16, kind="ExternalInput").ap()
    dfts_d = nc.dram_tensor("dft_nst", [T, T], BF16, kind="ExternalInput").ap()
    cs_d = D("dft_cs", [128, 256])
    wbr_d = D("w_branch", [2, 3, 512, 1024])
    wout_d = D("w_out", [2, 1024, 1024])
    wfi_d = D("w_ffn_in", [2, 1024, 5632])
    wfo_d = D("w_ffn_out", [2, 2816, 1024])
    ident_d = D("ident", [128, 128])
    y_d = D("y", [T, 1024], kind="ExternalOutput")
    kv_d = D("kvout", [2, T, 1280], kind="ExternalOutput")

    st = ExitStack()
    with st:
        NB = 206 * 1024
        arena = st.enter_context(nc.sbuf_tensor("arena", [128, NB // 2], BF16))
        psum_all = st.enter_context(nc.psum_tensor("psall", [128, 8, 512], F32))
        psum = [psum_all[:, i, :] for i in range(8)]
        fw = FW(nc)

        def view(off, shape, dt):
            n = 1
            for s in shape[1:]:
                n *= s
            assert off % 4 == 0
            if dt == BF16:
                ap = arena[:, off // 2: off // 2 + n]
                nbytes = n * 2
            else:
                ap = arena[:, off // 2: off // 2 + 2 * n].bitcast(F32)
                nbytes = n * 4
            assert off + nbytes <= NB, (off, nbytes)
            if len(shape) == 3:
                ap = ap.rearrange("p (a b) -> p a b", a=shape[1])
            elif len(shape) == 4:
                ap = ap.rearrange("p (a b c) -> p a b c", a=shape[1], b=shape[2])
            return ap

        class Alloc:
            def __init__(self, base, limit):
                self.off = base
                self.limit = limit

            def __call__(self, shape, dt):
                n = 1
                for s in shape[1:]:
                    n *= s
                nb = n * (2 if dt == BF16 else 4)
                nb = (nb + 31) // 32 * 32
                v = view(self.off, shape, dt)
                self.off += nb
                assert self.off <= self.limit, (self.off, self.limit)
                return v

        fx = Alloc(0, 106 * 1024)
        xT = fx([128, 8, T], F32)
        wring = [fx([128, SLOT], BF16) for _ in range(NS)]
        tmp = [fx([128, T], F32) for _ in range(2)]
        rstd = fx([128, T], F32)
        identF = fx([128, 128], F32)
        identB = fx([128, 128], BF16)
        onesB = fx([128, 128], BF16)
        validT = fx([128, 96], F32)
        sinkE = fx([128, 16], F32)
        epsT = fx([128, 1], F32)
        cvecT = fx([128, 8], F32)
        sB = fx([128, 8], BF16)
        badaT = fx([128, 2, 48], F32)
        modc = fx([128, 2, 48], F32)
        npreT = fx([128, 32], F32)
        npostT = fx([128, 32], F32)
        dv = fx([128, 2, 6, 8], F32)
        csB = fx([128, 256], BF16)
        stage = [fx([128, 1280], F32) for _ in range(2)]
        xtok = [fx([128, 1024], F32) for _ in range(2)]
        nrm = fx([128, 2, 16], F32)
        ABASE = fx.off
        assert ABASE <= 106 * 1024

        hT = view(ABASE, [128, 8, T], BF16)
        OT0 = ABASE + 16384
        OAT = view(OT0, [128, 4, T], BF16)
        OBT = view(OT0 + 8192, [128, 4, T], BF16)
        OCT = view(OT0 + 16384, [128, 4, T], BF16)
        S0 = OT0 + 24576
        SLIM = NB
        assert SLIM - S0 >= 58 * 1024, (SLIM - S0)
        sq = view(S0, [128, 8, T], BF16)
        mT = view(S0, [128, 8, T], BF16)
        yT = view(S0 + 16384, [128, 8, T], F32)
        c_al = Alloc(S0, SLIM)
        UCT = c_al([128, 4, T], BF16)
        ABtok = c_al([128, 8, 4, 256], BF16)
        a_al = Alloc(S0, SLIM)
        QAT = a_al([128, 4, T], BF16)
        KATp = [a_al([128, T], BF16) for _ in range(2)]
        VA = a_al([128, 8, 2, 80], BF16)
        ropeT = a_al([128, 4, T], F32)
        triB = a_al([128, 2, 512], BF16)
        KcATp = [a_al([128, 256], BF16) for _ in range(2)]
        VcA = a_al([128, 2, 2, 80], BF16)
        PTa = [a_al([128, 5, 512], BF16) for _ in range(2)]
        OtokA = [a_al([128, 512], BF16) for _ in range(2)]
        ctmpA = a_al([128, 2, 128], F32)
        b_al = Alloc(S0, SLIM)
        QBT = b_al([128, 4, T], BF16)
        KBTp = b_al([128, 4, T], BF16)
        VB = b_al([128, 8, 8, 80], BF16)
        TTB = b_al([128, 4, 1152], BF16)
        KcBT = b_al([128, 4, 256], BF16)
        KcBTp = b_al([128, 4, 256], BF16)
        VcB = b_al([128, 2, 8, 80], BF16)
        PTb = [b_al([128, 7, 512], BF16) for _ in range(2)]
        OtokB = [b_al([128, 256], BF16) for _ in range(2)]
        ctmpB = b_al([128, 2, 16], F32)
        aT = view(OT0, [128, 22, T], BF16)
        y2T = view(OT0 + 45056, [128, 8, T], F32)
        assert OT0 + 45056 + 32768 <= NB
        sq2 = view(ABASE, [128, 8, T], BF16)

        MIX_C = ["UCT", "ABtok"]
        MIX_A = ["QAT", "KAT", "VA", "rope", "tri", "KcAT", "VcA", "PTa", "OtokA", "ctmpA", "QATall", "VA1", "VcA1", "KATz", "KcATz"]
        MIX_B = ["QBT", "KBT", "VB", "TTB", "KcBT", "VcB", "PTb", "OtokB", "ctmpB", "QKBall", "VB1", "VcB1", "KcBTp", "KBTz"]
        SQ = ["mT"]
        WOUT = ["mT", "yT"]
        FFN = ["aT", "y2T"]
        OTS = ["OAT", "OBT", "OCT"]

        cnt = {"bank": 0, "w": 0, "obank": 0}

        def bank():
            b = cnt["bank"] % 6
            cnt["bank"] += 1
            return psum[b], ("ps", b)

        def bank_pair():
            if cnt["bank"] % 2:
                cnt["bank"] += 1
            b = cnt["bank"] % 6
            cnt["bank"] += 2
            return b

        def obank():
            b = 6 + cnt["obank"] % 2
            cnt["obank"] += 1
            return psum[b], ("ps", b)

        def wload(src, shape):
            s = cnt["w"] % NS
            cnt["w"] += 1
            n = 1
            for k in shape[1:]:
                n *= k
            assert n <= SLOT
            dst = wring[s][:, 0:n]
            if len(shape) == 3:
                dst = dst.rearrange("p (a b) -> p a b", a=shape[1])
            else:
                dst = dst.rearrange("p (a b c) -> p a b c", a=shape[1], b=shape[2])
            if len(shape) == 3:
                fw.dma("pool", ("w", s), lambda e, dst=dst, src=src: e.dma_start(out=dst, in_=src),
                       writes=[("wslot", s)])
            else:
                for q in range(shape[2]):
                    fw.dma("pool", ("w", s), lambda e, dst=dst, src=src, q=q: e.dma_start(out=dst[:, :, q, :], in_=src[:, :, q, :]),
                           writes=[("wslot", s)])
            return dst, ("wslot", s)

        def mm(out, lhsT, rhs, start, stop, reads, writes):
            fw.op("pe", lambda e: e.matmul(out, lhsT=lhsT, rhs=rhs, start=start, stop=stop), reads, writes)

        def tr(out, in_, reads, writes):
            fw.op("pe", lambda e: e.transpose(out, in_, identF[:]), list(reads) + ["identF"], writes)

        def act(out, in_, func, reads, writes, bias=None, scale=None):
            kw = {}
            if bias is not None:
                kw["bias"] = bias
            if scale is not None:
                kw["scale"] = scale
            fw.op("act", lambda e: e.activation(out=out, in_=in_, func=func, **kw), reads, writes)

        def dve(fn, reads, writes):
            fw.op("dve", fn, reads, writes)

        def sp_load(key, out, in_, writes):
            fw.dma("sp", key, lambda e: e.dma_start(out=out, in_=in_), writes=writes)

        sp_load("c0", identF, ident_d, ["identF"])
        sp_load("c1", validT, valid_d, ["valid"])
        sp_load("c2", sinkE, sink_d, ["sinkE"])
        sp_load("c3", cvecT, cvec_d, ["cvec"])
        sp_load("c4", badaT, bada_d.rearrange("l p j -> p l j"), ["bada"])
        sp_load("c5", npreT, npre_d, ["npre"])
        sp_load("c6", npostT, npost_d, ["npost"])
        fw.dma("pool", "c7", lambda e: e.dma_start(out=csB, in_=cs_d), writes=["csB"])
        dve(lambda e: e.tensor_copy(out=identB, in_=identF), ["identF"], ["identB"])
        dve(lambda e: e.memset(onesB, 1.0), [], ["onesB"])
        dve(lambda e: e.memset(epsT, 1e-6), [], ["eps"])
        act(sinkE, sinkE, AF.Exp, ["sinkE"], ["sinkE"])
        act(sB, cvecT, AF.Silu, ["cvec"], ["sB"])

        for b in range(8):
            s = b % 2
            sp_load(("xin", s), xtok[s], x_d[b * 128:(b + 1) * 128, :], [("xtok", s)])
            for g in range(2):
                pb, pk = bank()
                for j in range(4):
                    c = g * 4 + j
                    tr(pb[:, j * 128:(j + 1) * 128], xtok[s][:, c * 128:(c + 1) * 128], [("xtok", s)], [pk])
                dve(lambda e, pb=pb, g=g, b=b: e.tensor_copy(out=xT[:, g * 4:(g + 1) * 4, b * 128:(b + 1) * 128],
                                                             in_=pb[:].rearrange("p (j t) -> p j t", j=4)),
                    [pk], [("xT", c) for c in range(g * 4, g * 4 + 4)])

        DVSLOT = {0: 1, 1: 0, 2: 2, 3: 4, 4: 3, 5: 5}

        def ada_seg(l, i):
            wva = wada_d[l].rearrange("(kc p) n -> p kc n", p=128)
            for cb in (2 * i, 2 * i + 1):
                ws, wk = wload(wva[:, :, cb * 512:(cb + 1) * 512], [128, 8, 512])
                pb, pk = bank()
                for jj in range(4):
                    for kc in range(8):
                        mm(pb[:, jj:jj + 1], ws[:, kc, jj * 128:(jj + 1) * 128], sB[:, kc:kc + 1], kc == 0, kc == 7,
                           [wk, "sB"], [pk])
                dve(lambda e, pb=pb, l=l, cb=cb: e.tensor_tensor(out=modc[:, l, cb * 4:(cb + 1) * 4], in0=pb[:, 0:4],
                                                                in1=badaT[:, l, cb * 4:(cb + 1) * 4], op=ALU.add),
                    [pk, "bada"], [("modc", l, cb)])
            mv = modc[:, l, i * 8:(i + 1) * 8]
            slot = DVSLOT[i]
            rk = [("modc", l, 2 * i), ("modc", l, 2 * i + 1), "npre", "npost"]
            wk2 = [("dv", l, slot)]
            if i in (1, 4):
                npv = npreT[:, l * 16 + (0 if i == 1 else 8): l * 16 + (8 if i == 1 else 16)]
                dve(lambda e: e.tensor_scalar_add(out=dv[:, l, slot, :], in0=mv, scalar1=1.0), rk, wk2)
                dve(lambda e: e.tensor_tensor(out=dv[:, l, slot, :], in0=dv[:, l, slot, :], in1=npv, op=ALU.mult), rk + wk2, wk2)
            elif i in (0, 3):
                dve(lambda e: e.tensor_copy(out=dv[:, l, slot, :], in_=mv), rk, wk2)
            else:
                nqv = npostT[:, l * 16 + (0 if i == 2 else 8): l * 16 + (8 if i == 2 else 16)]
                dve(lambda e: e.tensor_tensor(out=dv[:, l, slot, :], in0=mv, in1=nqv, op=ALU.mult), rk, wk2)

        ada_seg(0, 0)
        ada_seg(0, 1)

        def rms_stats(src, srcname, sqv, sqname, presquared=False):
            for c in range(8):
                if not presquared:
                    act(sqv[:, c, :], src[:, c, :], AF.Square, [(srcname, c)], [(sqname, c)])
            for hf in range(2):
                pb, pk = bank()
                for c in range(8):
                    mm(pb[:], onesB[:], sqv[:, c, hf * 512:(hf + 1) * 512], c == 0, c == 7, [(sqname, c), "onesB"], [pk])
                act(rstd[:, hf * 512:(hf + 1) * 512], pb[:], AF.Sqrt, [pk, "eps"], [("rstd", hf)],
                    bias=epsT[:, 0:1], scale=1.0 / 1024.0)
                dve(lambda e, hf=hf: e.reciprocal(out=rstd[:, hf * 512:(hf + 1) * 512], in_=rstd[:, hf * 512:(hf + 1) * 512]),
                    [("rstd", hf)], [("rstd", hf)])

        def mod_norm(l, si, bi_, dst, dstname):
            for c in range(8):
                t = tmp[c % 2]
                dve(lambda e, t=t, c=c: e.scalar_tensor_tensor(out=t, in0=xT[:, c, :], scalar=dv[:, l, si, c:c + 1], in1=rstd,
                                                              op0=ALU.mult, op1=ALU.mult),
                    [("xT", c), ("rstd", 0), ("rstd", 1), ("dv", l, si)], [("tmp", c % 2)])
                act(dst[:, c, :], t, AF.Identity, [("tmp", c % 2), ("dv", l, bi_)], [(dstname, c)], bias=dv[:, l, bi_, c:c + 1], scale=1.0)

        def post_resid(l, gi, ysrc, yname):
            for c in range(8):
                t = tmp[c % 2]
                dve(lambda e, t=t, c=c: e.scalar_tensor_tensor(out=t, in0=ysrc[:, c, :], scalar=dv[:, l, gi, c:c + 1], in1=rstd,
                                                              op0=ALU.mult, op1=ALU.mult),
                    [(yname, c), ("rstd", 0), ("rstd", 1), ("dv", l, gi)], [("tmp", c % 2)])
                dve(lambda e, t=t, c=c: e.tensor_tensor(out=xT[:, c, :], in0=xT[:, c, :], in1=t, op=ALU.add),
                    [("tmp", c % 2), ("xT", c)], [("xT", c)])

        def fm_group(ws, wk, col0, kcs, rhs_of, hf):
            pb, pk = bank()
            for kc in range(kcs):
                rap, rkey = rhs_of(kc, hf)
                mm(pb[:], ws[:, kc, col0:col0 + 128], rap, kc == 0, kc == kcs - 1, [wk, rkey], [pk])
            return pb, pk

        def h_rhs(kc, hf):
            return hT[:, kc, hf * 512:(hf + 1) * 512], ("hT", kc)

        evq = {"n": 0}

        def copy_evac(out, in_, reads, writes, scale=None):
            evq["n"] += 1
            if scale is not None or evq["n"] % 2 == 0:
                act(out, in_, AF.Copy, reads, writes, scale=scale)
            else:
                dve(lambda e: e.tensor_copy(out=out, in_=in_), reads, writes)

        def attn_scores(kblocks, q_rhs, n_q_mm, PT, PTname):
            pi = cnt.setdefault(PTname, 0) % 2
            cnt[PTname] = pi + 1
            pt = PT[pi]
            for kbi, kb in enumerate(kblocks):
                hasb = kb["bias"] is not None
                vb = validT[:, kb["valid"]:kb["valid"] + 1]
                if n_q_mm == 1:
                    pb, pk = bank()
                    kap, kkey = kb["kT"](0)
                    qap, qkey = q_rhs(0)
                    out = pb[:].rearrange("p (c q) -> p c q", c=4)
                    mm(out, kap, qap, True, not hasb, [kkey, qkey], [pk])
                    if hasb:
                        bap, bkey = kb["bias"](0)
                        mm(out, identB[:], bap, False, True, [bkey, "identB"], [pk])
                    act(pt[:, kbi, :], pb[:], AF.Exp, [pk, "valid"], [(PTname, pi, kbi)], bias=vb, scale=1.0)
                else:
                    b0 = bank_pair()
                    pbs = [(psum[b0], ("ps", b0)), (psum[b0 + 1], ("ps", b0 + 1))]
                    for c in range(4):
                        pb, pk = pbs[c % 2]
                        kap, kkey = kb["kT"](c)
                        qap, qkey = q_rhs(c)
                        out = pb[:, (c // 2) * 128:(c // 2 + 1) * 128]
                        mm(out, kap, qap, True, True, [kkey, qkey], [pk])
                    act(pt[:, kbi, :].rearrange("p (b x) -> p b x", b=2), psum_all[:, b0:b0 + 2, 0:256], AF.Exp,
                        [pbs[0][1], pbs[1][1], "valid"], [(PTname, pi, kbi, 0), (PTname, pi, kbi, 1)], bias=vb, scale=1.0)
                    if hasb:
                        eap, ekey = kb["bias"](0)
                        dve(lambda e, eap=eap, kbi=kbi: e.tensor_tensor(out=pt[:, kbi, :].rearrange("p (c q) -> p c q", c=4),
                                                                       in0=pt[:, kbi, :].rearrange("p (c q) -> p c q", c=4), in1=eap, op=ALU.mult),
                            [ekey], [(PTname, pi, kbi, 0), (PTname, pi, kbi, 1)])
                for (ks, qs) in kb.get("zero", ()):
                    dve(lambda e, ks=ks, qs=qs, kbi=kbi: e.memset(
                        pt[ks * 64:(ks + 1) * 64, kbi, :].rearrange("p (c q) -> p c q", c=4)[:, :, qs * 64:(qs + 1) * 64], 0.0),
                        [], [(PTname, pi, kbi, 0), (PTname, pi, kbi, 1)])
            return (pt, pi)

        def attn_pv(kblocks, n_q_mm, h, PTname, otk, otkkey, ocols0, sink_col0):
            pt, pi = h
            nkb = len(kblocks)
            ob, ok = obank()
            for c in range(4):
                pos = c if n_q_mm == 1 else (c % 2) * 2 + c // 2
                for kbi, kb in enumerate(kblocks):
                    vap, vkey = kb["V"](c)
                    ptk = [(PTname, pi, kbi)] if n_q_mm == 1 else [(PTname, pi, kbi, c % 2)]
                    mm(ob[:, c * 80:c * 80 + 65], pt[:, kbi, pos * 128:(pos + 1) * 128], vap, kbi == 0, kbi == nkb - 1,
                       ptk + [vkey], [ok])
            ov = ob[:, 0:320].rearrange("p (c e) -> p c e", c=4)
            ns = cnt.setdefault("nrm", 0) % 2
            cnt["nrm"] = ns + 1
            den = nrm[:, ns, 0:4]
            rec = nrm[:, ns, 8:12]
            nk = ("nrm", ns)
            if sink_col0 is not None:
                dve(lambda e: e.tensor_tensor(out=den, in0=ov[:, :, 64], in1=sinkE[:, sink_col0:sink_col0 + 4], op=ALU.add),
                    [ok, "sinkE"], [nk])
                dve(lambda e: e.reciprocal(out=rec, in_=den), [nk], [nk])
            else:
                dve(lambda e: e.reciprocal(out=rec, in_=ov[:, :, 64]), [ok], [nk])
            for c in range(4):
                dve(lambda e, c=c: e.tensor_scalar(out=otk[:, ocols0 + c * 64: ocols0 + (c + 1) * 64], in0=ov[:, c, 0:64],
                                                   scalar1=rec[:, c:c + 1], scalar2=None, op0=ALU.mult),
                    [ok, nk], [otkkey])

        def otok_to_fm(otk, otkkey, ncols, OT, OTname, ch0, i):
            nch = ncols // 128
            pb, pk = bank()
            pbb = pb[:].bitcast(BF16)
            for ch in range(nch):
                fw.op("pe", lambda e, ch=ch: e.transpose(pbb[:, ch * 128:(ch + 1) * 128], otk[:, ch * 128:(ch + 1) * 128], identB[:]),
                      [otkkey, "identB"], [pk])
            copy_evac(OT[:, ch0:ch0 + nch, i * 128:(i + 1) * 128],
                      pbb[:, 0:nch * 128].rearrange("p (c q) -> p c q", c=nch), [pk],
                      [(OTname, ch0 + ch) for ch in range(nch)])

        def run_pipeline(blocks, hooks=None):
            prev = None
            pend = [None]

            def finish(p):
                b, h = p
                attn_pv(b["kbl"], b["nq"], h, b["PTname"], b["otk"], b["otkkey"], b["ocols0"], b["sink"])
                return b["post"]
            for bi_, b in enumerate(blocks):
                h = attn_scores(b["kbl"], b["q_rhs"], b["nq"], b["PT"], b["PTname"])
                if pend[0] is not None:
                    pend[0]()
                    pend[0] = None
                if prev is not None:
                    pend[0] = finish(prev)
                prev = (b, h)
                if hooks and bi_ in hooks:
                    hooks[bi_]()
            if pend[0] is not None:
                pend[0]()
            last = finish(prev)
            if last is not None:
                last()

        KSTOP = 99
        for l in range(2):
          for _stage in range(1):
            wv = winx_d[l].rearrange("(kc p) n -> p kc n", p=128)
            if l * 6 + 1 > KSTOP:
                break
            PH = fw.__dict__.setdefault("phases", [])
            PH.append(("L%d norm1" % l, len(fw.prog["pe"])))

            fw.fence(FFN + ["hT"], SQ + ["hT"])
            rms_stats(xT, "xT", sq, "mT")
            mod_norm(l, 0, 1, hT, "hT")

            if l * 6 + 2 > KSTOP:
                break
            PH.append(("L%d C" % l, len(fw.prog["pe"])))
            fw.fence(SQ + WOUT + FFN, MIX_C)
            ws, wk = wload(wv[:, :, 0:512], [128, 8, 512])
            for j in range(4):
                for hf in range(2):
                    pb, pk = fm_group(ws, wk, j * 128, 8, h_rhs, hf)
                    copy_evac(UCT[:, j, hf * 512:(hf + 1) * 512], pb[:], [pk], [("UCT", j)])
            for tb in range(8):
                for gp in range(2):
                    pb, pk = bank()
                    for gg in range(2):
                        g = gp * 2 + gg
                        mm(pb[:, gg * 256:(gg + 1) * 256], UCT[:, g, tb * 128:(tb + 1) * 128], csB[:], True, True,
                           [("UCT", g), "csB"], [pk])
                    copy_evac(ABtok[:, tb, gp * 2:gp * 2 + 2, :], pb[:].rearrange("p (g e) -> p g e", g=2), [pk], [("ABtok", tb)])
            fw.fence(FFN, OTS)
            for hf in range(2):
                cv = dftc_d.rearrange("(tb p) n -> p tb n", p=128)[:, :, hf * 512:(hf + 1) * 512]
                sv = dfts_d.rearrange("(tb p) n -> p tb n", p=128)[:, :, hf * 512:(hf + 1) * 512]
                wc, wck = wload(cv, [128, 8, 512])
                wsn, wsk = wload(sv, [128, 8, 512])
                for g in range(4):
                    pb, pk = bank()
                    for tb in range(8):
                        mm(pb[:], ABtok[:, tb, g, 0:128], wc[:, tb, :], tb == 0, False, [("ABtok", tb), wck], [pk])
                        mm(pb[:], ABtok[:, tb, g, 128:256], wsn[:, tb, :], False, tb == 7, [("ABtok", tb), wsk], [pk])
                    copy_evac(OCT[:, g, hf * 512:(hf + 1) * 512], pb[:], [pk], [("OCT", g)])

            if l * 6 + 3 > KSTOP:
                break
            PH.append(("L%d A" % l, len(fw.prog["pe"])))
            fw.fence(MIX_C, MIX_A)
            sp_load("ropeld", ropeT, rope_d.rearrange("k p t -> p k t"), ["rope"])
            fw.dma("pool", "trild", lambda e: e.dma_start(out=triB, in_=tri_d.rearrange("p (a b) -> p a b", a=2)), writes=["tri"])
            for tb in range(2):
                s = tb % 2
                sp_load(("xin", s), xtok[s][:, 0:128], cak_d[l, tb * 128:(tb + 1) * 128, :], [("xtok", s)])
                pb, pk = bank()
                tr(pb[:, 0:128], xtok[s][:, 0:128], [("xtok", s)], [pk])
                for g in range(2):
                    copy_evac(KcATp[g][g * 64:(g + 1) * 64, tb * 128:(tb + 1) * 128], pb[g * 64:(g + 1) * 64, 0:128], [pk], [("KcAT", g, tb)])
            for tb in range(2):
                fw.dma("pool", "vcald", lambda e, l=l, tb=tb: e.dma_start(out=VcA[:, tb, :, 0:64],
                                                                        in_=cav_d[l, tb * 128:(tb + 1) * 128, :].rearrange("p (h d) -> p h d", h=2)),
                       writes=["VcA"])
            dve(lambda e: e.memset(VcA[:, :, :, 64:65], 1.0), [], ["VcA1"])
            for g in range(2):
                og = 1 - g
                dve(lambda e, g=g, og=og: e.memset(KATp[g][og * 64:(og + 1) * 64, :], 0.0), [], [("KATz", g)])
                dve(lambda e, g=g, og=og: e.memset(KcATp[g][og * 64:(og + 1) * 64, :], 0.0), [], [("KcATz", g)])
            dve(lambda e: e.memset(VA[:, :, :, 64:65], 1.0), [], ["VA1"])
            KSUB = 99
            if KSUB < 1:
                break
            for u in range(2):
                ws, wk = wload(wv[:, :, 512 + u * 512: 1024 + u * 512], [128, 8, 512])
                for cc in range(2):
                    c = u * 2 + cc
                    for hf in range(2):
                        p1, k1 = fm_group(ws, wk, cc * 256, 8, h_rhs, hf)
                        p2, k2 = fm_group(ws, wk, cc * 256 + 128, 8, h_rhs, hf)
                        sl = slice(hf * 512, (hf + 1) * 512)
                        dve(lambda e, p1=p1, sl=sl: e.tensor_tensor(out=tmp[0][:, 0:512], in0=p1[:], in1=ropeT[:, 0, sl], op=ALU.mult),
                            [k1, "rope"], [("tmp", 0)])
                        dve(lambda e, p2=p2, sl=sl: e.tensor_tensor(out=tmp[1][:, 0:512], in0=p2[:], in1=ropeT[:, 1, sl], op=ALU.mult),
                            [k2, "rope"], [("tmp", 1)])
                        dve(lambda e, c=c, sl=sl: e.tensor_tensor(out=QAT[:, c, sl], in0=tmp[0][:, 0:512], in1=tmp[1][:, 0:512], op=ALU.add),
                            [("tmp", 0), ("tmp", 1)], [("QAT", c)])
            if KSUB < 2:
                break
            ws, wk = wload(wv[:, :, 1536:2048], [128, 8, 512])
            for hf in range(2):
                p1, k1 = fm_group(ws, wk, 0, 8, h_rhs, hf)
                p2, k2 = fm_group(ws, wk, 128, 8, h_rhs, hf)
                sl = slice(hf * 512, (hf + 1) * 512)
                dve(lambda e, p1=p1, sl=sl: e.tensor_tensor(out=tmp[0][:, 0:512], in0=p1[:], in1=ropeT[:, 2, sl], op=ALU.mult),
                    [k1, "rope"], [("tmp", 0)])
                dve(lambda e, p2=p2, sl=sl: e.tensor_tensor(out=tmp[1][:, 0:512], in0=p2[:], in1=ropeT[:, 3, sl], op=ALU.mult),
                    [k2, "rope"], [("tmp", 1)])
                for g in range(2):
                    dve(lambda e, sl=sl, g=g: e.tensor_tensor(out=KATp[g][g * 64:(g + 1) * 64, sl], in0=tmp[0][g * 64:(g + 1) * 64, 0:512],
                                                             in1=tmp[1][g * 64:(g + 1) * 64, 0:512], op=ALU.add),
                        [("tmp", 0), ("tmp", 1)], [("KAT", g, hf)])
            KV = 7
            for tb in range(8 if KV & 8 == 0 else 0):
                pb, pk = bank()
                for kc in range(8):
                    mm(pb[:, 0:256], hT[:, kc, tb * 128:(tb + 1) * 128], ws[:, kc, 256:512], kc == 0, kc == 7, [wk, ("hT", kc)], [pk])
                s = tb % 2
                if KV & 1:
                    act(stage[s][:, 0:256], pb[:, 0:256], AF.Copy, [pk], [("stageA", s)])
                if KV & 2:
                    dve(lambda e, pb=pb, tb=tb: e.tensor_copy(out=VA[:, tb, :, 0:64], in_=pb[:, 128:256].rearrange("p (h d) -> p h d", h=2)),
                        [pk], [("VA", tb)])
                if KV & 4:
                    fw.dma("sp", ("kvoA", s), lambda e, s=s, tb=tb, l=l: e.dma_start(out=kv_d[l, tb * 128:(tb + 1) * 128, 0:256], in_=stage[s][:, 0:256]),
                           reads=[("stageA", s)], writes=[("kvo", l, tb, 0)])
            if KSUB < 3:
                break
            PH.append(("L%d A-attn" % l, len(fw.prog["pe"])))
            dve(lambda e: e.memset(ctmpA[:, 0, 0:1], 0.0),
                [("QAT", c) for c in range(4)] + ["VA1", "VcA1"] + [("KATz", g) for g in range(2)] + [("KcATz", g) for g in range(2)]
                + [("KAT", g, hf) for g in range(2) for hf in range(2)] + [("KcAT", g, tb) for g in range(2) for tb in range(2)],
                [("QATall",)])
            blocksA = []
            for i in range(8 if KSUB > 3 else 1):
                for g in range(2):
                    def mk_local(kbk, tri_idx, vcol, g=g):
                        return {"kT": (lambda c, kbk=kbk, g=g: (KATp[g][:, kbk * 128:(kbk + 1) * 128], ("QATall",))),
                                "bias": None if tri_idx is None else (lambda c, tri_idx=tri_idx: (triB[:, tri_idx, :].rearrange("p (c q) -> p c q", c=4), "tri")),
                                "valid": vcol,
                                "V": (lambda c, kbk=kbk, g=g: (VA[:, kbk, g, 0:65], ("VA", kbk)))}

                    def mk_ctx(tb, vcol, g=g):
                        return {"kT": (lambda c, tb=tb, g=g: (KcATp[g][:, tb * 128:(tb + 1) * 128], ("QATall",))),
                                "bias": None, "valid": vcol,
                                "V": (lambda c, tb=tb, g=g: (VcA[:, tb, g, 0:65], "VcA"))}
                    kbl = [mk_local(max(i - 1, 0), 0, i * 5 + 0), mk_local(i, None, i * 5 + 1), mk_local(min(i + 1, 7), 1, i * 5 + 2),
                           mk_ctx(0, i * 5 + 3), mk_ctx(1, i * 5 + 4)]
                    oi = i % 2
                    post = None
                    if g == 1:
                        post = (lambda oi=oi, i=i: otok_to_fm(OtokA[oi], ("OtokA", oi), 512, OAT, "OAT", 0, i))
                    blocksA.append({"kbl": kbl, "q_rhs": (lambda c, i=i: (QAT[:, :, i * 128:(i + 1) * 128], ("QATall",))), "nq": 1,
                                    "PT": PTa, "PTname": "PTa", "otk": OtokA[oi], "otkkey": ("OtokA", oi), "ocols0": g * 256,
                                    "sink": l * 8 + g * 4, "post": post})
            run_pipeline(blocksA, hooks={3: (lambda l=l: ada_seg(l, 2)), 9: (lambda l=l: ada_seg(l, 3))} if KSUB > 3 else None)

            if l * 6 + 4 > KSTOP:
                break
            PH.append(("L%d B" % l, len(fw.prog["pe"])))
            fw.fence(MIX_A, MIX_B)
            for tb in range(2):
                for ch in range(4):
                    s = (tb * 4 + ch) % 2
                    sp_load(("xin", s), xtok[s][:, 0:128], cbk_d[l, tb * 128:(tb + 1) * 128, ch * 128:(ch + 1) * 128], [("xtok", s)])
                    pb, pk = bank()
                    tr(pb[:, 0:128], xtok[s][:, 0:128], [("xtok", s)], [pk])
                    copy_evac(KcBT[:, ch, tb * 128:(tb + 1) * 128], pb[:, 0:128], [pk], ["KcBT"])
            for tb in range(2):
                fw.dma("pool", "vcbld", lambda e, l=l, tb=tb: e.dma_start(out=VcB[:, tb, :, 0:64],
                                                                        in_=cbv_d[l, tb * 128:(tb + 1) * 128, :].rearrange("p (h d) -> p h d", h=8)),
                       writes=["VcB"])
            dve(lambda e: e.memset(VcB[:, :, :, 64:65], 1.0), [], ["VcB1"])
            dve(lambda e: e.memset(VB[:, :, :, 64:65], 1.0), [], ["VB1"])
            ws, wk = wload(wv[:, :, 2048:2560], [128, 8, 512])
            for j in range(4):
                for hf in range(2):
                    pb, pk = fm_group(ws, wk, j * 128, 8, h_rhs, hf)
                    copy_evac(QBT[:, j, hf * 512:(hf + 1) * 512], pb[:], [pk], [("QBT", j)], scale=0.125)
            for c in range(4):
                oh = 1 - (c % 2)
                dve(lambda e, c=c, oh=oh: e.memset(KBTp[oh * 64:(oh + 1) * 64, c, :], 0.0), [], [("KBTz", c)])
                dve(lambda e, c=c, oh=oh: e.memset(KcBTp[oh * 64:(oh + 1) * 64, c, :], 0.0), [], [("KBTz", c)])

            def proj_KB(hg):
                wsk2, wkk2 = wload(wv[:, :, 2560 + hg * 256: 2560 + (hg + 1) * 256], [128, 8, 256])
                for jj in range(2):
                    for hf in range(2):
                        pb, pk = fm_group(wsk2, wkk2, jj * 128, 8, h_rhs, hf)
                        sl = slice(hf * 512, (hf + 1) * 512)
                        for par in range(2):
                            c = 2 * jj + par
                            copy_evac(KBTp[par * 64:(par + 1) * 64, c, sl], pb[par * 64:(par + 1) * 64, :], [pk], [("KBT", c, hf)])
                for c in range(4):
                    par = c % 2
                    copy_evac(KcBTp[par * 64:(par + 1) * 64, c, :], KcBT[par * 64:(par + 1) * 64, hg * 2 + c // 2, :], ["KcBT"], [("KcBTp", c)])
            proj_KB(0)
            wsk_, wkk = wload(wv[:, :, 3072:3584], [128, 8, 512])
            wsv_, wkv = wload(wv[:, :, 3584:4096], [128, 8, 512])
            for tb in range(8):
                s = tb % 2
                pb, pk = bank()
                for kc in range(8):
                    mm(pb[:], hT[:, kc, tb * 128:(tb + 1) * 128], wsk_[:, kc, :], kc == 0, kc == 7, [wkk, ("hT", kc)], [pk])
                act(stage[s][:, 256:768], pb[:], AF.Copy, [pk], [("stageB", s)])
                pb2, pk2 = bank()
                for kc in range(8):
                    mm(pb2[:], hT[:, kc, tb * 128:(tb + 1) * 128], wsv_[:, kc, :], kc == 0, kc == 7, [wkv, ("hT", kc)], [pk2])
                act(stage[s][:, 768:1280], pb2[:], AF.Copy, [pk2], [("stageB", s)])
                dve(lambda e, pb2=pb2, tb=tb: e.tensor_copy(out=VB[:, tb, :, 0:64], in_=pb2[:].rearrange("p (h d) -> p h d", h=8)),
                    [pk2], [("VB", tb)])
                fw.dma("sp", ("kvoB", s), lambda e, s=s, tb=tb, l=l: e.dma_start(out=kv_d[l, tb * 128:(tb + 1) * 128, 256:1280], in_=stage[s][:, 256:1280]),
                       reads=[("stageB", s)], writes=[("kvo", l, tb, 1)])
            dve(lambda e: e.memset(ctmpB[:, 0, 0:1], 0.0), [("QBT", c) for c in range(4)] + [("KBTz", c) for c in range(4)] + ["VB1", "VcB1", "VcB"],
                [("QKBall",)])
            PH.append(("L%d B-attn" % l, len(fw.prog["pe"])))
            for hg in range(2):
                if hg == 1:
                    proj_KB(1)
                fw.dma("pool", "ttbld", lambda e, l=l, hg=hg: e.dma_start(out=TTB, in_=ttb_d[l, hg].rearrange("p (h c) -> p h c", h=4)),
                       writes=["TTB"])
                for c4 in range(4):
                    act(TTB[:, c4, :], TTB[:, c4, :], AF.Exp, ["TTB"], ["TTB"])
                blocksB = []
                for i in range(8):
                    kb0 = min(max(i - 2, 0), 3)
                    kbl = []
                    for j in range(5):
                        kbk = kb0 + j
                        delta = kbk - i
                        pos0 = 8 - 2 * delta
                        d = {"kT": (lambda c, kbk=kbk: (KBTp[:, c, kbk * 128:(kbk + 1) * 128], ("KBT", c, kbk // 4))),
                             "bias": (lambda c, pos0=pos0: (TTB[:, :, pos0 * 64: pos0 * 64 + 128], "TTB")),
                             "valid": 40 + i * 7 + j,
                             "V": (lambda c, kbk=kbk, hg=hg: (VB[:, kbk, hg * 4 + c, 0:65], ("VB", kbk)))}
                        if 2 <= i <= 5 and delta == -2:
                            d["zero"] = [(0, 1)]
                        if 2 <= i <= 5 and delta == 2:
                            d["zero"] = [(0, 0), (1, 0), (1, 1)]
                        kbl.append(d)
                    for tb in range(2):
                        kbl.append({"kT": (lambda c, tb=tb: (KcBTp[:, c, tb * 128:(tb + 1) * 128], ("KcBTp", c))),
                                    "bias": None, "valid": 40 + i * 7 + 5 + tb,
                                    "V": (lambda c, tb=tb, hg=hg: (VcB[:, tb, hg * 4 + c, 0:65], ("QKBall",)))})
                    oi = i % 2
                    blocksB.append({"kbl": kbl,
                                    "q_rhs": (lambda c, i=i, hg=hg: (QBT[:, hg * 2 + c // 2, i * 128:(i + 1) * 128], ("QKBall",))),
                                    "nq": 4, "PT": PTb, "PTname": "PTb", "otk": OtokB[oi], "otkkey": ("OtokB", oi), "ocols0": 0, "sink": None,
                                    "post": (lambda oi=oi, i=i, hg=hg: otok_to_fm(OtokB[oi], ("OtokB", oi), 256, OBT, "OBT", hg * 2, i))})
                if hg == 0:
                    hk = {1: (lambda l=l: ada_seg(l, 4)), 4: (lambda l=l: ada_seg(l, 5))}
                else:
                    hk = {1: (lambda: ada_seg(1, 0)), 4: (lambda: ada_seg(1, 1))} if l == 0 else None
                run_pipeline(blocksB, hooks=hk)

            if l * 6 + 5 > KSTOP:
                break
            PH.append(("L%d merge" % l, len(fw.prog["pe"])))
            fw.fence(MIX_B + MIX_A + MIX_C + SQ, WOUT)
            wbv = wbr_d[l].rearrange("b (kc p) n -> p (b kc) n", p=128)
            OTv = [OAT, OBT, OCT]
            OTn = ["OAT", "OBT", "OCT"]
            for j in range(8):
                wg, wgk = wload(wv[:, :, 4096 + j * 384: 4096 + (j + 1) * 384], [128, 8, 384])
                wb, wbk = wload(wbv[:, :, j * 128:(j + 1) * 128], [128, 12, 128])
                for hf in range(2):
                    sl = slice(hf * 512, (hf + 1) * 512)
                    for br in range(3):
                        pg, kg = fm_group(wg, wgk, br * 128, 8, h_rhs, hf)
                        sg = tmp[0][:, 0:512] if br % 2 == 0 else tmp[1][:, 0:512]
                        sgk = ("tmp", br % 2)
                        act(sg, pg[:], AF.Sigmoid, [kg], [sgk])
                        py, ky = bank()
                        for kc in range(4):
                            mm(py[:], wb[:, br * 4 + kc, :], OTv[br][:, kc, sl], kc == 0, kc == 3, [wbk, (OTn[br], kc)], [ky])
                        if br == 0:
                            dve(lambda e, py=py, sg=sg: e.tensor_tensor(out=rstd[:, 0:512], in0=py[:], in1=sg, op=ALU.mult),
                                [ky, sgk], [("rstd", 0)])
                        else:
                            dve(lambda e, py=py, sg=sg: e.tensor_tensor(out=sg, in0=py[:], in1=sg, op=ALU.mult), [ky, sgk], [sgk])
                            if br == 1:
                                dve(lambda e, sg=sg: e.tensor_tensor(out=rstd[:, 0:512], in0=rstd[:, 0:512], in1=sg, op=ALU.add),
                                    [sgk, ("rstd", 0)], [("rstd", 0)])
                            else:
                                dve(lambda e, sg=sg, j=j, sl=sl: e.tensor_tensor(out=mT[:, j, sl], in0=rstd[:, 0:512], in1=sg, op=ALU.add),
                                    [sgk, ("rstd", 0)], [("mT", j)])
            PH.append(("L%d wout" % l, len(fw.prog["pe"])))
            wov = wout_d[l].rearrange("(kc p) n -> p kc n", p=128)
            m_rhs = lambda kc, hf: (mT[:, kc, hf * 512:(hf + 1) * 512], ("mT", kc))
            for u in range(2):
                ws, wk = wload(wov[:, :, u * 512:(u + 1) * 512], [128, 8, 512])
                for jj in range(4):
                    j = u * 4 + jj
                    for hf in range(2):
                        pb, pk = fm_group(ws, wk, jj * 128, 8, m_rhs, hf)
                        sl = slice(hf * 512, (hf + 1) * 512)
                        act(sq2[:, j, sl], pb[:], AF.Square, [pk], [("hT", j)])
                        dve(lambda e, pb=pb, j=j, sl=sl: e.tensor_copy(out=yT[:, j, sl], in_=pb[:]), [pk], [("yT", j)])
            rms_stats(yT, "yT", sq2, "hT", presquared=True)
            post_resid(l, 2, yT, "yT")

            if l * 6 + 6 > KSTOP:
                break
            PH.append(("L%d ffn" % l, len(fw.prog["pe"])))
            fw.fence(WOUT + OTS + SQ, FFN)
            rms_stats(xT, "xT", sq, "mT")
            mod_norm(l, 3, 4, hT, "hT")
            wfv = wfi_d[l].rearrange("(kc p) (s n) -> p kc s n", p=128, s=2)
            for jp in range(11):
                ws, wk = wload(wfv[:, :, :, jp * 256:(jp + 1) * 256], [128, 8, 2, 256])
                for jj in range(2):
                    j = jp * 2 + jj
                    for hf in range(2):
                        sl = slice(hf * 512, (hf + 1) * 512)
                        pg, kg = bank()
                        for kc in range(8):
                            mm(pg[:], ws[:, kc, 0, jj * 128:(jj + 1) * 128], hT[:, kc, sl], kc == 0, kc == 7, [wk, ("hT", kc)], [kg])
                        pu, ku = bank()
                        for kc in range(8):
                            mm(pu[:], ws[:, kc, 1, jj * 128:(jj + 1) * 128], hT[:, kc, sl], kc == 0, kc == 7, [wk, ("hT", kc)], [ku])
                        sg = tmp[(j * 2 + hf) % 2][:, 0:512]
                        sgk = ("tmp", (j * 2 + hf) % 2)
                        act(sg, pg[:], AF.Silu, [kg], [sgk])
                        dve(lambda e, pu=pu, sg=sg, j=j, sl=sl: e.tensor_tensor(out=aT[:, j, sl], in0=pu[:], in1=sg, op=ALU.mult),
                            [ku, sgk], [("aT", j)])
            wfov = wfo_d[l].rearrange("(kc p) n -> p kc n", p=128)
            a_rhs = lambda kc, hf: (aT[:, kc, hf * 512:(hf + 1) * 512], ("aT", kc))
            for u in range(4):
                wsA, wkA = wload(wfov[:, 0:11, u * 256:(u + 1) * 256], [128, 11, 256])
                wsB, wkB = wload(wfov[:, 11:22, u * 256:(u + 1) * 256], [128, 11, 256])
                for jj in range(2):
                    j = u * 2 + jj
                    for hf in range(2):
                        pb, pk = bank()
                        for kc in range(22):
                            ws_, wk_ = (wsA, wkA) if kc < 11 else (wsB, wkB)
                            rap, rkey = a_rhs(kc, hf)
                            mm(pb[:], ws_[:, kc % 11, jj * 128:(jj + 1) * 128], rap, kc == 0, kc == 21, [wk_, rkey], [pk])
                        sl = slice(hf * 512, (hf + 1) * 512)
                        act(sq2[:, j, sl], pb[:], AF.Square, [pk], [("hT", j)])
                        dve(lambda e, pb=pb, j=j, sl=sl: e.tensor_copy(out=y2T[:, j, sl], in_=pb[:]), [pk], [("y2T", j)])
            rms_stats(y2T, "y2T", sq2, "hT", presquared=True)
            post_resid(l, 5, y2T, "y2T")

        fw.__dict__.setdefault("phases", []).append(("out", len(fw.prog["pe"])))
        for b in range(8):
            s = b % 2
            for g in range(2):
                pb, pk = bank()
                for j in range(4):
                    c = g * 4 + j
                    tr(pb[:, j * 128:(j + 1) * 128], xT[:, c, b * 128:(b + 1) * 128], [("xT", c)], [pk])
                copy_evac(xtok[s][:, g * 512:(g + 1) * 512], pb[:], [pk], [("xtok", s)])
            fw.dma("sp", ("yout", s), lambda e, b=b, s=s: e.dma_start(out=y_d[b * 128:(b + 1) * 128, :], in_=xtok[s]),
                   reads=[("xtok", s)], writes=[("yo", b)])
        fw.final_wait("sp", [("yo", b) for b in range(8)] + [k for k in [("kvo", l, tb, k) for l in range(2) for tb in range(8) for k in range(2)] if k in fw.state])
        fw.emit(st)
        nc._phases = fw.__dict__.get("phases", [])
    return nc


def _rope_tables(sample):
    out = np.zeros((4, 128, T), np.float32)
    d = np.arange(128) % 64
    if sample:
        t = np.arange(T)
        row = (t // 64).astype(np.float32)
        col = (t % 64).astype(np.float32)
        inv = (np.float32(10000.0) ** (-np.arange(16, dtype=np.float32) / np.float32(16))).astype(np.float32)
        ang = np.concatenate([row[:, None] * inv, col[:, None] * inv], axis=-1).astype(np.float32)
        cos = np.cos(ang).astype(np.float32)
        sin = np.sin(ang).astype(np.float32)
        C = cos[:, d % 32].T
        S = sin[:, d % 32].T * np.where(d < 32, -1.0, 1.0)[:, None].astype(np.float32)
    else:
        C = np.ones((128, T), np.float32)
        S = np.zeros((128, T), np.float32)
    out[0] = C * np.float32(0.125)
    out[1] = S * np.float32(0.125)
    out[2] = C
    out[3] = S
    return out


def _dft_tables(sample):
    t = np.arange(T, dtype=np.int64)
    if sample:
        n = T
        ph = (t[:, None] * t[None, :]) % n
        m = np.ones((T, T))
    else:
        n = 256
        ph = ((t[:, None] % n) * (t[None, :] % n)) % n
        m = ((t[:, None] // n) == (t[None, :] // n)).astype(np.float64)
    ang = 2.0 * np.pi * ph / n
    import ml_dtypes
    ct = (np.cos(ang) * m / np.sqrt(n)).astype(np.float32).astype(ml_dtypes.bfloat16)
    nst = (-np.sin(ang) * m / np.sqrt(n)).astype(np.float32).astype(ml_dtypes.bfloat16)
    return ct, nst


def _cs_table():
    c = np.arange(128, dtype=np.int64)
    ang = 2.0 * np.pi * ((c[:, None] * c[None, :]) % 128) / 128.0
    return np.concatenate([np.cos(ang), np.sin(ang)], axis=1).astype(np.float32) / np.float32(np.sqrt(128.0))


def _ttb_tables(b_rpb, sample):
    out = np.zeros((2, 2, 128, 4, 18, 64), np.float32)
    if not sample:
        return out.reshape(2, 2, 128, 4 * 1152)
    kcol = np.arange(64)[:, None]
    c = np.arange(64)[None, :]
    ws = np.clip(c - 8, 0, 48)
    colok = (kcol >= ws) & (kcol < ws + 16)
    dc = np.clip(kcol - c + 15, 0, 30)
    for l in range(2):
        for h in range(8):
            for ks in range(2):
                for pos in range(18):
                    a = 15 - pos + ks
                    if 0 <= a <= 14:
                        tile = np.where(colok, b_rpb[l, h, a][dc], np.float32(NEG))
                    else:
                        tile = np.full((64, 64), NEG, np.float32)
                    out[l, h // 4, ks * 64:(ks + 1) * 64, ((h % 4) % 2) * 2 + (h % 4) // 2, pos, :] = tile
    return out.reshape(2, 2, 128, 4 * 1152)


def _tri_table(sample):
    out = np.zeros((128, 2, 4, 128), np.float32)
    if sample:
        j = np.arange(128)[:, None]
        q = np.arange(128)[None, :]
        out[:, 0] = np.where(j >= q, 0.0, NEG)[:, None, :]
        out[:, 1] = np.where(j <= q, 0.0, NEG)[:, None, :]
    return out.reshape(128, 1024)


def _valid_table(sample):
    v = np.zeros((96,), np.float32)
    for i in range(8):
        if sample:
            a = [i >= 1, True, i <= 6, True, True]
        else:
            a = [i % 2 == 1, True, i % 2 == 0, False, False]
        for k in range(5):
            v[i * 5 + k] = 0.0 if a[k] else NEG
        kb0 = min(max(i - 2, 0), 3)
        for j in range(5):
            kb = kb0 + j
            if sample:
                ok = (kb <= 3) if i <= 1 else ((kb >= 4) if i >= 6 else True)
            else:
                ok = (kb // 2) == (i // 2)
            v[40 + i * 7 + j] = 0.0 if ok else NEG
        for tb in range(2):
            v[40 + i * 7 + 5 + tb] = 0.0 if sample else NEG
    return np.broadcast_to(v[None, :], (128, 96)).copy()


def _winx(w_in):
    idx = []
    idx += list(range(2304, 2816))
    rot = lambda base: [base + (d + 32) % 64 for d in range(64)]
    nat = lambda base: [base + d for d in range(64)]
    for c in range(4):
        idx += nat(c * 64) + nat((c + 4) * 64)
        idx += rot(c * 64) + rot((c + 4) * 64)
    idx += nat(512) + nat(576)
    idx += rot(512) + rot(576)
    idx += list(range(512, 640)) + list(range(640, 768))
    idx += list(range(768, 1280))
    idx += list(range(1280, 1792))
    idx += list(range(1280, 1792))
    idx += list(range(1792, 2304))
    for j in range(8):
        for br in range(3):
            idx += list(range(2816 + br * 1024 + j * 128, 2816 + br * 1024 + (j + 1) * 128))
    idx = np.asarray(idx)
    assert idx.shape[0] == WIN_COLS
    return np.ascontiguousarray(w_in[:, :, idx])


_CACHE = {}


def make_in_maps(x_prompt, x_sample, cache_a_k, cache_a_v, cache_b_k, cache_b_v, c, c_ctx,
                 w_ada, b_ada, norm_pre, norm_post, w_in, a_sink, b_rpb, w_branch, w_out,
                 w_ffn_in, w_ffn_out):
    f = lambda a: np.ascontiguousarray(np.asarray(a, dtype=np.float32))
    x_prompt, x_sample = f(x_prompt), f(x_sample)
    cache_a_k, cache_a_v, cache_b_k, cache_b_v = f(cache_a_k), f(cache_a_v), f(cache_b_k), f(cache_b_v)
    c, c_ctx = f(c), f(c_ctx)
    w_ada, b_ada, norm_pre, norm_post, w_in = f(w_ada), f(b_ada), f(norm_pre), f(norm_post), f(w_in)
    a_sink, b_rpb, w_branch, w_out, w_ffn_in, w_ffn_out = f(a_sink), f(b_rpb), f(w_branch), f(w_out), f(w_ffn_in), f(w_ffn_out)

    shared = {
        "w_ada": w_ada,
        "b_adaT": np.ascontiguousarray(b_ada.reshape(2, 48, 128).transpose(0, 2, 1)),
        "npreT": np.ascontiguousarray(norm_pre.reshape(2, 2, 8, 128).transpose(3, 0, 1, 2).reshape(128, 32)),
        "npostT": np.ascontiguousarray(norm_post.reshape(2, 2, 8, 128).transpose(3, 0, 1, 2).reshape(128, 32)),
        "w_inx": _winx(w_in),
        "sinkb": np.ascontiguousarray(np.broadcast_to(a_sink.reshape(1, 16), (128, 16))),
        "dft_cs": _cs_table(),
        "w_branch": w_branch, "w_out": w_out, "w_ffn_in": w_ffn_in, "w_ffn_out": w_ffn_out,
        "ident": np.eye(128, dtype=np.float32),
    }
    per_type = {}
    for sample in (False, True):
        ct, nst = _dft_tables(sample)
        per_type[sample] = {
            "ttb": _ttb_tables(b_rpb, sample), "tri": _tri_table(sample), "valid": _valid_table(sample),
            "rope": _rope_tables(sample), "dft_ct": ct, "dft_nst": nst,
        }
    in_maps = []
    for core in range(8):
        m = dict(shared)
        if core < 4:
            m.update(per_type[False])
            m["x"] = np.ascontiguousarray(x_prompt[core * 4:(core + 1) * 4].reshape(T, 1024))
            m["cvec"] = np.ascontiguousarray(c_ctx.reshape(8, 128).T)
            m["cak"] = np.zeros((2, 256, 128), np.float32)
            m["cav"] = np.zeros((2, 256, 128), np.float32)
            m["cbk"] = np.zeros((2, 256, 512), np.float32)
            m["cbv"] = np.zeros((2, 256, 512), np.float32)
        else:
            b = core - 4
            m.update(per_type[True])
            m["x"] = np.ascontiguousarray(x_sample[b])
            m["cvec"] = np.ascontiguousarray(c[b].reshape(8, 128).T)
            m["cak"] = np.ascontiguousarray(cache_a_k[b].reshape(2, 256, 128))
            m["cav"] = np.ascontiguousarray(cache_a_v[b].reshape(2, 256, 128))
            m["cbk"] = np.ascontiguousarray(cache_b_k[b].reshape(2, 256, 512))
            m["cbv"] = np.ascontiguousarray(cache_b_v[b].reshape(2, 256, 512))
        in_maps.append(m)

    return in_maps


def kernel(**inputs):
    in_maps = make_in_maps(**inputs)
    if "nc" not in _CACHE:
        _CACHE["nc"] = build_program()
    nc = _CACHE["nc"]
    res = run_bass_kernel_spmd(nc, in_maps, core_ids=list(range(8)))
    return assemble(res.results)


def assemble(outs):
    y_prompt = np.concatenate([outs[k]["y"].reshape(4, 256, 1024) for k in range(4)], axis=0).astype(np.float32)
    y_sample = np.stack([outs[4 + k]["y"] for k in range(4)], axis=0).astype(np.float32)
    kv = np.concatenate([outs[k]["kvout"].reshape(2, 4, 256, 1280).transpose(1, 0, 2, 3) for k in range(4)], axis=0)
    new_a_k = np.ascontiguousarray(kv[..., 0:128]).reshape(16, 2, 256, 2, 64)
    new_a_v = np.ascontiguousarray(kv[..., 128:256]).reshape(16, 2, 256, 2, 64)
    new_b_k = np.ascontiguousarray(kv[..., 256:768]).reshape(16, 2, 256, 8, 64)
    new_b_v = np.ascontiguousarray(kv[..., 768:1280]).reshape(16, 2, 256, 8, 64)
    return (y_prompt, y_sample, new_a_k, new_a_v, new_b_k, new_b_v)
```
